# Optimizing a Trainium2 kernel written in Bass

```python
import math
import jax, jax.numpy as jnp
from jax import lax
import numpy as np

D_MODEL = 2048
BATCH = 4
SEQ = 8192
DEPTH = 2
DEC_BATCH = 32
DEC_SEQ = 64
PAST_LEN = 1024

CHUNK = 64
N_MIXERS = 2
POOL_WINDOWS = (2, 4, 8, 16)
N_POOL_GROUPS = len(POOL_WINDOWS)
POOL_GW = D_MODEL // N_POOL_GROUPS
POOL_STATE = max(POOL_WINDOWS) - 1
N_HEADS = 8
HEAD_HALF = D_MODEL // (2 * N_HEADS)
HEAD_DIM = 2 * HEAD_HALF
D_FF = -(-8 * D_MODEL // (3 * 256)) * 256
Q_BLOCK = 128
EPS = 1e-6
NEG_INF = -1e30

kernel_name = "hybrid_pool_diffattn_stream_step"


def rms_norm(x, g):
    xf = x.astype(jnp.float32)
    y = xf * lax.rsqrt(jnp.mean(xf * xf, axis=-1, keepdims=True) + EPS)
    return (y * g.astype(jnp.float32)).astype(x.dtype)


def alibi_slopes():
    h = jnp.arange(1, N_HEADS + 1, dtype=jnp.float32)
    return 2.0 ** (-8.0 * h / N_HEADS)


def swiglu(x, g, w_gate, w_up, w_down):
    h = rms_norm(x, g)
    return (jax.nn.silu(h @ w_gate) * (h @ w_up)) @ w_down


def pool_mixer(u, prefix, n_valid_prefix, pool_w, pool_b, pool_scale):
    B, T, _ = u.shape
    uf = u.astype(jnp.float32)
    ext = jnp.concatenate([prefix.astype(jnp.float32), uf], axis=1)
    cs = jnp.concatenate([jnp.zeros((B, 1, D_MODEL), jnp.float32),
                          jnp.cumsum(ext, axis=1)], axis=1)
    t = jnp.arange(T, dtype=jnp.int32)
    groups = []
    for g, w in enumerate(POOL_WINDOWS):
        c0, c1 = g * POOL_GW, (g + 1) * POOL_GW
        hi = cs[:, POOL_STATE + 1:POOL_STATE + 1 + T, c0:c1]
        lo = cs[:, POOL_STATE + 1 - w:POOL_STATE + 1 - w + T, c0:c1]
        cnt = jnp.minimum(w, t + 1 + n_valid_prefix).astype(jnp.float32)
        groups.append((hi - lo) / cnt[None, :, None])
    pooled = jnp.concatenate(groups, axis=-1) - uf
    mixed = jnp.einsum('btgc,gcd->btgd',
                       pooled.reshape(B, T, N_POOL_GROUPS, POOL_GW),
                       pool_w.astype(jnp.float32)).reshape(B, T, D_MODEL)
    out = (mixed + pool_b.astype(jnp.float32)) * pool_scale.astype(jnp.float32)
    new_state = ext[:, -POOL_STATE:]
    return out.astype(u.dtype), new_state.astype(u.dtype)


def diff_attend(q, k, v, q_pos, k_pos, lam):
    s = jnp.einsum('bqhcd,bkhcd->bhcqk', q.astype(jnp.float32),
                   k.astype(jnp.float32)) * (HEAD_HALF ** -0.5)
    dist = jnp.abs(q_pos[:, None] - k_pos[None, :]).astype(jnp.float32)
    bias = -alibi_slopes()[:, None, None] * dist
    visible = (k_pos[None, :] // CHUNK) <= (q_pos[:, None] // CHUNK)
    s = jnp.where(visible, s + bias[None, :, None], NEG_INF)
    p = jax.nn.softmax(s, axis=-1)
    a = p[:, :, 0] - lam * p[:, :, 1]
    return jnp.einsum('bhqk,bkhe->bqhe', a, v.astype(jnp.float32))


def diff_attn_mixer(u, past_k, past_v, lam_init, w_qkv, q_norm, k_norm,
                    lambda_q1, lambda_k1, lambda_q2, lambda_k2, subln, w_o):
    B, T, _ = u.shape
    q, k, v = jnp.split(u @ w_qkv, 3, axis=-1)
    q = rms_norm(q.reshape(B, T, N_HEADS, 2, HEAD_HALF), q_norm)
    k = rms_norm(k.reshape(B, T, N_HEADS, 2, HEAD_HALF), k_norm)
    v = v.reshape(B, T, N_HEADS, HEAD_DIM)
    f32 = jnp.float32
    lam = (jnp.exp(jnp.sum(lambda_q1.astype(f32) * lambda_k1.astype(f32)))
           - jnp.exp(jnp.sum(lambda_q2.astype(f32) * lambda_k2.astype(f32))) + lam_init)
    if past_k is None:
        n_blk = T // Q_BLOCK
        qb = q.reshape(B, n_blk, Q_BLOCK, N_HEADS, 2, HEAD_HALF).swapaxes(0, 1)
        k_pos = jnp.arange(T, dtype=jnp.int32)

        def one_block(args):
            qi, bi = args
            q_pos = bi * Q_BLOCK + jnp.arange(Q_BLOCK, dtype=jnp.int32)
            return diff_attend(qi, k, v, q_pos, k_pos, lam)

        o = lax.map(one_block, (qb, jnp.arange(n_blk, dtype=jnp.int32)))
        o = o.swapaxes(0, 1).reshape(B, T, N_HEADS, HEAD_DIM)
    else:
        P = past_k.shape[1]
        kk = jnp.concatenate([past_k.reshape(B, P, N_HEADS, 2, HEAD_HALF).astype(k.dtype), k], axis=1)
        vv = jnp.concatenate([past_v.astype(v.dtype), v], axis=1)
        q_pos = P + jnp.arange(T, dtype=jnp.int32)
        k_pos = jnp.arange(P + T, dtype=jnp.int32)
        o = diff_attend(q, kk, vv, q_pos, k_pos, lam)
    o = rms_norm(o, subln) * (1.0 - lam_init)
    y = o.reshape(B, T, D_MODEL).astype(u.dtype) @ w_o
    return y, k.reshape(B, T, N_HEADS, HEAD_DIM), v


def run_trunk(x, pool_prefix, n_valid_prefix, past_k, past_v, norm_mix, norm_ffn,
              pool_w, pool_b, pool_scale, w_qkv, q_norm, k_norm,
              lambda_q1, lambda_k1, lambda_q2, lambda_k2, subln, w_o,
              w_gate, w_up, w_down):
    pool_state = k_new = v_new = None
    for i in range(DEPTH):
        u = rms_norm(x, norm_mix[i])
        if i % N_MIXERS == 0:
            y, pool_state = pool_mixer(u, pool_prefix, n_valid_prefix, pool_w, pool_b, pool_scale)
        else:
            lam_init = 0.8 - 0.6 * math.exp(-0.3 * i)
            y, k_new, v_new = diff_attn_mixer(u, past_k, past_v, lam_init, w_qkv, q_norm, k_norm,
                                              lambda_q1, lambda_k1, lambda_q2, lambda_k2, subln, w_o)
        x = x + y.astype(x.dtype)
        x = x + swiglu(x, norm_ffn[i], w_gate[i], w_up[i], w_down[i]).astype(x.dtype)
    return x, pool_state, k_new, v_new


def setup_inputs(seed: int = 0) -> dict:
    key = jax.random.key(seed)
    ks = jax.random.split(key, 24)
    f32 = jnp.float32
    nrm = lambda k, shape, s: jax.random.normal(k, shape, f32) * s
    return {
        "x_prompt": nrm(ks[0], (BATCH, SEQ, D_MODEL), 1.0),
        "x_sample": nrm(ks[1], (DEC_BATCH, DEC_SEQ, D_MODEL), 1.0),
        "state_pool": nrm(ks[2], (DEC_BATCH, POOL_STATE, D_MODEL), 1.0),
        "cache_k": nrm(ks[3], (DEC_BATCH, PAST_LEN, N_HEADS, HEAD_DIM), 1.0),
        "cache_v": nrm(ks[4], (DEC_BATCH, PAST_LEN, N_HEADS, HEAD_DIM), 1.0),
        "norm_mix": 1.0 + nrm(ks[5], (DEPTH, D_MODEL), 0.05),
        "norm_ffn": 1.0 + nrm(ks[6], (DEPTH, D_MODEL), 0.05),
        "pool_w": nrm(ks[7], (N_POOL_GROUPS, POOL_GW, POOL_GW), POOL_GW ** -0.5),
        "pool_b": nrm(ks[8], (D_MODEL,), 0.02),
        "pool_scale": 1.0 + nrm(ks[9], (D_MODEL,), 0.1),
        "w_qkv": nrm(ks[10], (D_MODEL, 3 * D_MODEL), D_MODEL ** -0.5),
        "q_norm": 1.0 + nrm(ks[11], (HEAD_HALF,), 0.05),
        "k_norm": 1.0 + nrm(ks[12], (HEAD_HALF,), 0.05),
        "lambda_q1": nrm(ks[13], (HEAD_HALF,), 0.1),
        "lambda_k1": nrm(ks[14], (HEAD_HALF,), 0.1),
        "lambda_q2": nrm(ks[15], (HEAD_HALF,), 0.1),
        "lambda_k2": nrm(ks[16], (HEAD_HALF,), 0.1),
        "subln": 1.0 + nrm(ks[17], (HEAD_DIM,), 0.05),
        "w_o": nrm(ks[18], (D_MODEL, D_MODEL), D_MODEL ** -0.5),
        "w_gate": nrm(ks[19], (DEPTH, D_MODEL, D_FF), D_MODEL ** -0.5),
        "w_up": nrm(ks[20], (DEPTH, D_MODEL, D_FF), D_MODEL ** -0.5),
        "w_down": nrm(ks[21], (DEPTH, D_FF, D_MODEL), D_FF ** -0.5),
    }


def reference(x_prompt, x_sample, state_pool, cache_k, cache_v, norm_mix, norm_ffn,
              pool_w, pool_b, pool_scale, w_qkv, q_norm, k_norm,
              lambda_q1, lambda_k1, lambda_q2, lambda_k2, subln, w_o,
              w_gate, w_up, w_down):
    weights = (norm_mix, norm_ffn, pool_w, pool_b, pool_scale, w_qkv, q_norm, k_norm,
               lambda_q1, lambda_k1, lambda_q2, lambda_k2, subln, w_o, w_gate, w_up, w_down)
    zero_prefix = jnp.zeros((x_prompt.shape[0], POOL_STATE, D_MODEL), x_prompt.dtype)
    y_prompt, pool_state_prompt, k_prompt, v_prompt = run_trunk(
        x_prompt, zero_prefix, 0, None, None, *weights)
    y_sample, pool_state_sample, k_sample, v_sample = run_trunk(
        x_sample, state_pool, POOL_STATE, cache_k, cache_v, *weights)
    return (y_prompt, y_sample, pool_state_prompt, pool_state_sample,
            k_prompt, v_prompt, k_sample, v_sample)
```

```python
import contextlib
import numpy as np
import concourse.bass as bass
import concourse.mybir as mybir
from concourse.bass_utils import run_bass_kernel_spmd

F32 = mybir.dt.float32
BF16 = mybir.dt.bfloat16
ALU = mybir.AluOpType
AF = mybir.ActivationFunctionType
AX = mybir.AxisListType

D = 2048
DFF = 5632
NKC = 16
NFC = 44
NH = 8
EPS = 1e-6
SQ128 = float(np.sqrt(128.0))
SCALE = float(128.0 ** -0.5)
LAM_INIT = float(0.8 - 0.6 * np.exp(-0.3 * 1))
BIG = 1.0e30
SAME_ENGINE_SYNC = True


class Cfg:
    def __init__(self, nblk=16, ns=4, past=1024, ncores=8):
        self.nblk = nblk
        self.nstep = nblk // 2
        self.ns = ns
        self.past = past
        self.ncores = ncores
        self.ncb = past // 512
        self.nscr = nblk + ns * self.ncb + 1
        self.stop = None
        self.substop = None


class Buf:
    __slots__ = ("name", "lw", "rd", "sem", "dcnt")

    def __init__(self, name):
        self.name = name
        self.lw = None
        self.rd = {}
        self.sem = None
        self.dcnt = 0


class Op:
    __slots__ = ("eng", "fn", "deps", "isdma", "sembuf", "dval", "inc", "cnt", "waits")


ENGS = ("pe", "act", "dve", "pool", "sp")


class Sched:
    def __init__(self):
        self.q = {e: [] for e in ENGS}
        self.sembufs = []
        self.nops = 0

    def op(self, eng, fn, reads=(), writes=(), dma_sem=None):
        o = Op()
        o.eng = eng
        o.fn = fn
        o.isdma = dma_sem is not None
        o.inc = False
        o.cnt = 0
        o.sembuf = None
        o.dval = 0
        deps = []
        for b in reads:
            if b.lw is not None:
                deps.append(b.lw)
        for b in writes:
            if b.lw is not None:
                deps.append(b.lw)
            deps.extend(b.rd.values())
        key = ("d", self.nops) if o.isdma else eng
        for b in reads:
            b.rd[key] = o
        for b in writes:
            b.lw = o
            b.rd = {}
        o.deps = [d for d in set(deps) if d is not o]
        if o.isdma:
            if dma_sem.dcnt == 0:
                self.sembufs.append(dma_sem)
            dma_sem.dcnt += 1
            o.sembuf = dma_sem
            o.dval = 16 * dma_sem.dcnt
        self.q[eng].append(o)
        self.nops += 1
        return o

    def _skip(self, o, d):
        if d.isdma:
            return False
        if d.eng == o.eng and not o.isdma:
            if d.eng == "pe":
                return True
            return not SAME_ENGINE_SYNC
        return False

    def finalize(self):
        for e in ENGS:
            for o in self.q[e]:
                for d in o.deps:
                    if not d.isdma and not self._skip(o, d):
                        d.inc = True
        for e in ENGS:
            c = 0
            for o in self.q[e]:
                if not o.isdma and o.inc:
                    c += 1
                    o.cnt = c
        for e in ENGS:
            seen = {}
            for o in self.q[e]:
                need = {}
                for d in o.deps:
                    if self._skip(o, d):
                        continue
                    if d.isdma:
                        k = ("d", id(d.sembuf))
                        v = d.dval
                        s = d.sembuf
                    else:
                        k = ("e", d.eng)
                        v = d.cnt
                        s = d.eng
                    if seen.get(k, 0) >= v:
                        continue
                    if k not in need or need[k][1] < v:
                        need[k] = (s, v)
                for k, (s, v) in need.items():
                    seen[k] = v
                o.waits = list(need.values())


def build(cfg):
    nc = bass.Bass("TRN2", target_bir_lowering=False)
    S = Sched()
    NBLK, NSTEP, NS, NCB = cfg.nblk, cfg.nstep, cfg.ns, cfg.ncb
    NSTOK = NS * 64
    NTS = NSTOK // 128
    assert NSTOK % 128 == 0 or NS == 1
    if NS == 1:
        NTS = 1
    SROWS = min(128, NSTOK)

    def din(name, shape, dt=F32):
        return nc.dram_tensor(name, list(shape), dt, kind="ExternalInput").ap()

    def dout(name, shape, dt=F32):
        return nc.dram_tensor(name, list(shape), dt, kind="ExternalOutput").ap()

    def dint(name, shape, dt):
        return nc.dram_tensor(name, list(shape), dt, kind="Internal").ap()

    xloc = din("xloc", [NBLK, 512, D])
    xhalo = din("xhalo", [NBLK, 16, D])
    pcorr = din("pcorr", [NBLK, 64])
    xs = din("xs", [NS * 64, D])
    shalo = din("shalo", [NS, 16, D])
    ck = din("ck", [NS, cfg.past, D])
    cv = din("cv", [NS, cfg.past, D])
    kpos_d = din("kpos", [128, NBLK * 4])
    qref_d = din("qref", [128, NSTEP])
    omask_d = din("omask", [128, NSTEP])
    ident_d = din("ident", [128, 128])
    dm_d = din("dm", [128, 4 * 512])
    qrow_d = din("qrow", [1, 8 * 512])
    slopes_d = din("slopes", [1, 8])
    kcols_d = din("kcols", [128, NCB * 4 * 8])
    qrows_d = din("qrows", [1, 8 * 64])
    dms_d = din("dms", [128, 64])
    gT_d = din("gT", [128, 64])
    g0row_d = din("g0row", [1, D])
    pb_d = din("pool_b", [1, D])
    psc_d = din("pool_scale", [1, D])
    qn_d = din("q_norm", [1, 128])
    kn_d = din("k_norm", [1, 128])
    sub_d = din("subln", [1, 256])
    lam_d = din("lams", [4, 128])
    pool_w = din("pool_w", [4 * 512, 512])
    w_qkv = din("w_qkv", [D, 3 * D])
    w_o = din("w_o", [D, D])
    w_gate = din("w_gate", [2, D, DFF])
    w_up = din("w_up", [2, D, DFF])
    w_down = din("w_down", [2, DFF, D])

    y_own = dout("y_own", [NSTEP, 512, D])
    y_s = dout("y_s", [NS * 64, D])
    ps_p = dout("ps_p", [2, 16, D])
    ps_s = dout("ps_s", [NS, 16, D])
    k_own = dout("k_own", [NSTEP, 512, D])
    v_own = dout("v_own", [NSTEP, 512, D])
    k_s = dout("k_s", [NS * 64, D])
    v_s = dout("v_s", [NS * 64, D])

    pw_b = dint("pw_b", [4 * 512, 512], BF16)
    wqkv_b = dint("wqkv_b", [D, 3 * D], BF16)
    wo_b = dint("wo_b", [D, D], BF16)
    wg_b = dint("wg_b", [2, D, DFF], BF16)
    wu_b = dint("wu_b", [2, D, DFF], BF16)
    wd_b = dint("wd_b", [2, DFF, D], BF16)
    ktS = dint("ktS", [cfg.nscr, 16, 128, 512], BF16)
    vS = dint("vS", [cfg.nscr, 512, D], BF16)
    x1S = dint("x1S", [NSTEP + 1, 512, D], F32)

    B_pw, B_wqkv, B_wo = Buf("pw"), Buf("wqkv"), Buf("wo")
    B_wg = [Buf("wg0"), Buf("wg1")]
    B_wu = [Buf("wu0"), Buf("wu1")]
    B_wd = [Buf("wd0"), Buf("wd1")]
    B_kt = [Buf("kt%d" % i) for i in range(cfg.nscr)]
    B_vs = [Buf("vs%d" % i) for i in range(cfg.nscr)]
    B_x1 = [Buf("x1_%d" % i) for i in range(NSTEP + 1)]

    es = contextlib.ExitStack()
    with es:
        def sb(name, shape, dt):
            return es.enter_context(nc.sbuf_tensor("s_" + name, list(shape), dt))

        xb_t = sb("xb", [128, 4, D], F32)
        hT_t = sb("hT", [128, 16, 512], BF16)
        big_t = sb("big", [128, 44 * 512], BF16)
        w16_t = [sb("w16_%d" % i, [128, 4096], BF16) for i in range(4)]
        w4_t = [sb("w4_%d" % i, [128, 2, 512], BF16) for i in range(4)]
        sq_t = sb("sq", [128, D], F32)
        utm_t = [sb("utm%d" % i, [128, D], BF16) for i in range(2)]
        tabs_t = sb("tabs", [128, 3 * D], F32)
        misc_t = sb("misc", [128, 8192], BF16)
        ptr_t = sb("ptr", [128, 3 * 512], BF16)
        ident_b = sb("ident_b", [128, 128], BF16)
        gT = sb("gT", [128, 64], F32)
        stat_t = [sb("stat%d" % i, [128, 16], F32) for i in range(4)]
        consts = sb("consts", [128, 8], F32)
        qg_row = sb("qg_row", [128, 128], F32)
        kg_row = sb("kg_row", [128, 128], F32)
        sub_row = sb("sub_row", [128, 256], F32)
        kpos_t = sb("kpos_t", [128, NBLK * 4], F32)
        qref_t = sb("qref_t", [128, NSTEP], F32)
        omask_t = sb("omask_t", [128, NSTEP], F32)
        slopes_t = sb("slopes_t", [128, 8], F32)
        kdiff_t = sb("kdiff_t", [128, NBLK * 4], F32)
        kcol_t = sb("kcol_t", [128, NBLK * 4, 8], F32)
        kcols_t = sb("kcols_t", [128, NCB * 4, 8], F32)
        dms_t = sb("dms_t", [128, 64], F32)
        qrows_t = sb("qrows_t", [128, 8, 64], F32)
        pcorr_t = [sb("pcorr%d" % i, [128, 64], F32) for i in range(2)]
        rs_t = sb("rs_t", [128, 8], F32)
        ps_t = es.enter_context(nc.psum_tensor("ps", [128, 8, 512], F32))

        xb = xb_t[:]
        hT = hT_t[:]
        big = big_t[:]

        R0, R1, R2 = Buf("R0"), Buf("R1"), Buf("R2")

        def rbuf(c):
            return R0 if c < 17 else (R1 if c < 34 else R2)

        actT = big.rearrange("p (c n) -> p c n", n=512)
        uTe = big[:, 0:16 * 528].rearrange("p (k n) -> p k n", n=528)
        sums_f = big[:, 17 * 512:34 * 512].bitcast(F32)
        sumA = sums_f[:, 0:4 * 528].rearrange("p (k n) -> p k n", n=528)
        sumB = sums_f[:, 4 * 528:8 * 528].rearrange("p (k n) -> p k n", n=528)
        ktT = big[:, 0:16 * 512].rearrange("p (k n) -> p k n", n=512)
        vstage = [big[:, 17 * 512 + i * D:17 * 512 + (i + 1) * D] for i in range(4)]
        o_tm = big[:, 0:4 * D].rearrange("p (t n) -> p t n", n=D)
        qT = big[:, 17 * 512:33 * 512].rearrange("p (k n) -> p k n", n=512)
        r2f = big[:, 34 * 512:44 * 512].bitcast(F32)
        tmp0 = r2f[:, 0:1024].rearrange("p (t n) -> p t n", n=256)
        ein = [r2f[:, 1024 + i * 512:1024 + (i + 1) * 512] for i in range(3)]
        B_ein = [Buf("ein%d" % i) for i in range(3)]
        B_tmp0 = Buf("tmp0")

        tabs = tabs_t[:]
        B_tabs = Buf("tabs")
        pbs_row = tabs[:, 0:D]
        psc_row = tabs[:, D:2 * D]
        g0row = tabs[:, 2 * D:3 * D]
        qrow = tabs[:, 0:8 * 512].rearrange("p (h n) -> p h n", n=512)
        dm = tabs[:, 8 * 512:12 * 512].rearrange("p (k n) -> p k n", n=512)

        misc = misc_t[:]
        miscf = misc.bitcast(F32)
        kf = [miscf[:, i * 512:(i + 1) * 512] for i in range(2)]
        kout = [miscf[:, 1024 + i * 512:1024 + (i + 1) * 512] for i in range(2)] + [miscf[:, 3584:4096]]
        vf = [miscf[:, 2048 + i * 512:2048 + (i + 1) * 512] for i in range(2)]
        kb16 = [misc[:, 6144 + i * 512:6144 + (i + 1) * 512] for i in range(2)]
        B_kf = [Buf("kf%d" % i) for i in range(2)]
        B_kout = [Buf("kout%d" % i) for i in range(3)]
        B_vf = [Buf("vf%d" % i) for i in range(2)]
        B_kb16 = [Buf("kb16_%d" % i) for i in range(2)]
        ktr = [misc[:, 2048 + i * 512:2048 + (i + 1) * 512] for i in range(4)]
        vtr = [misc[:, 4096 + i * 1056:4096 + (i + 1) * 1056].rearrange("p (k n) -> p k n", n=264) for i in range(3)]
        ptr = [ptr_t[:, i * 512:(i + 1) * 512] for i in range(3)]
        B_ktr = [Buf("ktr%d" % i) for i in range(4)]
        B_vtr = [Buf("vtr%d" % i) for i in range(3)]
        B_ptr = [Buf("ptr%d" % i) for i in range(3)]

        B_xb, B_hT, B_sq = Buf("xb"), Buf("hT"), Buf("sq")
        B_utm = [Buf("utm0"), Buf("utm1")]
        B_w16 = [Buf("w16_%d" % i) for i in range(4)]
        B_w4 = [Buf("w4_%d" % i) for i in range(4)]
        B_stat = [Buf("stat%d" % i) for i in range(4)]
        B_const = Buf("const")
        B_pcorr = [Buf("pcorr0"), Buf("pcorr1")]
        B_kcol = Buf("kcol")
        B_rs = Buf("rs")
        B_bank = [Buf("bank%d" % i) for i in range(8)]
        bank = [ps_t[:, i, :] for i in range(8)]
        bankb = [ps_t[:, i, :].bitcast(BF16) for i in range(8)]
        B_out = Buf("outsem")

        cnt = {"stat": 0, "utm": 0, "bank": 0, "kf": 0, "kout": 0, "vf": 0, "kb": 0, "vst": 0, "ein": 0, "pt": 0, "sb": 0, "tb": 0}

        def rot(name, n):
            v = cnt[name]
            cnt[name] = (v + 1) % n
            return v

        def dma(eng, out_ap, in_ap, reads, writes, sem):
            S.op(eng, lambda e: e.dma_start(out=out_ap, in_=in_ap), reads=reads, writes=writes, dma_sem=sem)

        def load_const(dst_ap, src_ap, buf):
            dma("sp", dst_ap, src_ap, [], [buf], buf)

        def conv(dst, src, rows, step, buf):
            for r0 in range(0, rows, step):
                r1 = min(rows, r0 + step)
                dma("pool", dst[r0:r1, :], src[r0:r1, :], [], [buf], buf)

        conv(pw_b, pool_w, 2048, 2048, B_pw)
        conv(wg_b[0], w_gate[0], D, 256, B_wg[0])
        conv(wu_b[0], w_up[0], D, 256, B_wu[0])
        conv(wd_b[0], w_down[0], DFF, 512, B_wd[0])
        conv(wqkv_b, w_qkv, D, 256, B_wqkv)
        for s_ in range(NS):
            for cb in range(NCB):
                blk = NBLK + s_ * NCB + cb
                dma("pool", vS[blk], cv[s_, cb * 512:(cb + 1) * 512, :], [], [B_vs[blk]], B_vs[blk])
        conv(wo_b, w_o, D, 512, B_wo)
        conv(wg_b[1], w_gate[1], D, 256, B_wg[1])
        conv(wu_b[1], w_up[1], D, 256, B_wu[1])
        conv(wd_b[1], w_down[1], DFF, 512, B_wd[1])

        B_c = {n: Buf(n) for n in ["identf", "identb", "gT", "qg", "kg", "sub", "lam", "kpos", "qref", "omask",
                                   "slopes", "kcols", "dms", "qrows", "kdiff"]}
        ident_f = sq_t[:, 0:128]
        lam_t = sq_t[:, 128:640].rearrange("p (j n) -> p j n", n=128)
        load_const(ident_f, ident_d, B_sq)
        load_const(gT[:], gT_d, B_c["gT"])
        load_const(qg_row[:], qn_d.partition_broadcast(128), B_c["qg"])
        load_const(kg_row[:], kn_d.partition_broadcast(128), B_c["kg"])
        load_const(sub_row[:], sub_d.partition_broadcast(128), B_c["sub"])
        for j in range(4):
            load_const(lam_t[:, j, :], lam_d[j:j + 1, :].partition_broadcast(128), B_sq)
        load_const(kpos_t[:], kpos_d, B_c["kpos"])
        load_const(qref_t[:], qref_d, B_c["qref"])
        load_const(omask_t[:], omask_d, B_c["omask"])
        load_const(slopes_t[:], slopes_d.partition_broadcast(128), B_c["slopes"])
        load_const(kcols_t[:].rearrange("p k h -> p (k h)"), kcols_d, B_c["kcols"])
        load_const(dms_t[:], dms_d, B_c["dms"])
        load_const(qrows_t[:].rearrange("p h n -> p (h n)"), qrows_d.partition_broadcast(128), B_c["qrows"])

        S.op("dve", lambda e: e.tensor_copy(out=ident_b[:], in_=ident_f), [B_sq], [B_c["identb"]])
        S.op("pool", lambda e: e.memset(consts[:, 0:1], -0.5), [], [B_const])
        S.op("pool", lambda e: e.memset(big, 0.0), [], [R0, R1, R2])
        for i_ in range(4):
            S.op("pool", lambda e, i_=i_: e.memset(stat_t[i_][:], 1.0), [], [B_stat[i_]])
        S.op("dve", lambda e: e.tensor_scalar(out=sub_row[:], in0=sub_row[:], scalar1=1.0 - LAM_INIT, scalar2=None,
                                              op0=ALU.mult), [B_c["sub"]], [B_c["sub"]])
        S.op("dve", lambda e: e.tensor_tensor(out=lam_t[:, 0, :], in0=lam_t[:, 0, :], in1=lam_t[:, 1, :], op=ALU.mult),
             [B_sq], [B_sq])
        S.op("dve", lambda e: e.tensor_tensor(out=lam_t[:, 2, :], in0=lam_t[:, 2, :], in1=lam_t[:, 3, :], op=ALU.mult),
             [B_sq], [B_sq])
        S.op("dve", lambda e: e.reduce_sum(out=consts[:, 2:3], in_=lam_t[:, 0, :], axis=AX.X), [B_sq], [B_const])
        S.op("dve", lambda e: e.reduce_sum(out=consts[:, 3:4], in_=lam_t[:, 2, :], axis=AX.X), [B_sq], [B_const])
        S.op("act", lambda e: e.activation(out=consts[:, 4:6], in_=consts[:, 2:4], func=AF.Exp), [B_const], [B_const])
        S.op("dve", lambda e: e.tensor_tensor(out=consts[:, 6:7], in0=consts[:, 5:6], in1=consts[:, 4:5], op=ALU.subtract),
             [B_const], [B_const])
        S.op("dve", lambda e: e.tensor_scalar(out=consts[:, 1:2], in0=consts[:, 6:7], scalar1=-LAM_INIT, scalar2=None,
                                              op0=ALU.add), [B_const], [B_const])
        mhalf = consts[:, 0:1]
        lamneg = consts[:, 1:2]

        class Stream:
            def __init__(self, bufs, depth, slack=0):
                self.bufs = bufs
                self.depth = depth
                self.slack = slack
                self.items = []
                self.issued = 0

            def add(self, fn):
                self.items.append(fn)
                return len(self.items) - 1

            def acquire(self, i):
                lim = min(len(self.items), i + self.depth - self.slack)
                while self.issued < lim:
                    j = self.issued
                    self.items[j](j % self.depth)
                    self.issued += 1
                return i % self.depth

        w16s = Stream(B_w16, 4, slack=1)
        w4s = Stream(B_w4, 4)
        w16v = [t[:] for t in w16_t]
        w4v = [t[:] for t in w4_t]

        def rstd_of(ss_ap, st, stb, ncol, inv_n, nr=128):
            S.op("dve", lambda e: e.tensor_scalar(out=st[0:nr, 4:4 + ncol], in0=ss_ap[0:nr, :], scalar1=inv_n, scalar2=EPS,
                                                  op0=ALU.mult, op1=ALU.add), [stb], [stb])
            S.op("pool", lambda e: e.tensor_tensor(out=st[0:nr, 8:8 + ncol], in0=st[0:nr, 4:4 + ncol],
                                                   in1=mhalf[0:nr, :].to_broadcast([nr, ncol]) if ncol > 1 else mhalf[0:nr, :],
                                                   op=ALU.pow), [stb, B_const], [stb])
            return st[:, 8:8 + ncol]

        def transpose16(src, srcb, nrows, dst, dstbufs, gain_idx, evac_eng="dve"):
            for half in range(2):
                bi = half
                def pe_fn(e, half=half, bi=bi):
                    ins = None
                    for j in range(8):
                        k = half * 8 + j
                        ins = e.transpose(out=bankb[bi][:, j * 128:j * 128 + nrows],
                                          in_=src[0:nrows, k * 128:(k + 1) * 128],
                                          identity=ident_b[0:nrows, 0:nrows])
                    return ins
                S.op("pe", pe_fn, [srcb, B_c["identb"]], [B_bank[bi]])
                pview = bankb[bi].rearrange("p (k n) -> p k n", n=128)[:, :, 0:nrows]
                dview = dst[:, half * 8:half * 8 + 8, :]
                if gain_idx is None:
                    S.op("dve", lambda e, pview=pview, dview=dview: e.tensor_copy(out=dview, in_=pview),
                         [B_bank[bi]], dstbufs)
                else:
                    g = gT[:, gain_idx * 16 + half * 8:gain_idx * 16 + half * 8 + 8].unsqueeze(2).to_broadcast([128, 8, nrows])
                    S.op("dve", lambda e, pview=pview, dview=dview, g=g: e.tensor_tensor(out=dview, in0=pview, in1=g, op=ALU.mult),
                         [B_bank[bi], B_c["gT"]], dstbufs)

        def norm_T(x_ap, xbuf, nrows, gain_idx, dst, dstbufs, u32_out=None):
            si = rot("stat", 4)
            st, stb = stat_t[si][:], B_stat[si]
            S.op("dve", lambda e: e.tensor_tensor(out=sq_t[0:nrows, :], in0=x_ap[0:nrows, :], in1=x_ap[0:nrows, :], op=ALU.mult),
                 [xbuf], [B_sq])
            S.op("dve", lambda e: e.reduce_sum(out=st[0:nrows, 0:1], in_=sq_t[0:nrows, :], axis=AX.X), [B_sq], [stb])
            rstd = rstd_of(st[:, 0:1], st, stb, 1, 1.0 / D, nrows)
            ui = rot("utm", 2)
            utm, utmb = utm_t[ui][:], B_utm[ui]
            S.op("act", lambda e: e.activation(out=utm[0:nrows, :], in_=x_ap[0:nrows, :], func=AF.Copy, scale=rstd[0:nrows, :]),
                 [xbuf, stb], [utmb])
            if u32_out is not None:
                u32_out(rstd)
            transpose16(utm, utmb, nrows, dst, dstbufs, gain_idx)

        def next_bank():
            return 2 + rot("bank", 6)

        def ffn_items(layer):
            gu = []
            for cg in range(NFC // 2):
                pair = []
                for gi, (wsrc, wbuf) in enumerate(((wg_b, B_wg), (wu_b, B_wu))):
                    def ld(slot, cg=cg, wsrc=wsrc, wbuf=wbuf):
                        dstv = w16v[slot].rearrange("p (k n) -> p k n", k=16)
                        src = wsrc[layer][:, cg * 256:(cg + 1) * 256].rearrange("(k p) n -> p k n", p=128)
                        dma("sp", dstv, src, [wbuf[layer]], [B_w16[slot]], B_w16[slot])
                    pair.append(w16s.add(ld))
                gu.append(pair)
            wd = []
            for n in range(4):
                for kg in range(22):
                    def ld(slot, n=n, kg=kg):
                        src = wd_b[layer][kg * 256:(kg + 1) * 256, n * 512:(n + 1) * 512].rearrange("(k p) n -> p k n", p=128)
                        dma("sp", w4v[slot], src, [B_wd[layer]], [B_w4[slot]], B_w4[slot])
                    wd.append(w4s.add(ld))
            return gu, wd

        def ffn(layer, NT, gain_idx, items):
            gu, wd = items
            ncol = NT * 128
            for t in range(NT):
                norm_T(xb[:, t, :], B_xb, 128, gain_idx, hT[:, :, t * 128:(t + 1) * 128], [B_hT])
            for cg in range(NFC // 2):
                slots = [w16s.acquire(gu[cg][0]), w16s.acquire(gu[cg][1])]
                wvs = [w16v[sl].rearrange("p (k n) -> p k n", k=16) for sl in slots]
                for cc in range(2):
                    c = cg * 2 + cc
                    pr = (c % 2) * 2 + 2
                    for gi in range(2):
                        def pe_fn(e, gi=gi, cc=cc, pr=pr, wv=wvs[gi]):
                            ins = None
                            for k in range(16):
                                ins = e.matmul(out=bank[pr + gi][:, 0:ncol], lhsT=wv[:, k, cc * 128:(cc + 1) * 128],
                                               rhs=hT[:, k, 0:ncol], start=(k == 0), stop=(k == 15))
                            return ins
                        S.op("pe", pe_fn, [B_w16[slots[gi]], B_hT], [B_bank[pr + gi]])
                    si = rot("kf", 2)
                    sg, sgb = kf[si], B_kf[si]
                    S.op("act", lambda e, pr=pr, sg=sg: e.activation(out=sg[:, 0:ncol], in_=bank[pr][:, 0:ncol], func=AF.Silu),
                         [B_bank[pr]], [sgb])
                    S.op("dve", lambda e, pr=pr, sg=sg, c=c: e.tensor_tensor(out=actT[:, c, 0:ncol], in0=sg[:, 0:ncol],
                                                                             in1=bank[pr + 1][:, 0:ncol], op=ALU.mult),
                         [sgb, B_bank[pr + 1]], [rbuf(c)])
            for n in range(4):
                for kg in range(22):
                    slot = w4s.acquire(wd[n * 22 + kg])
                    def pe_fn(e, kg=kg, slot=slot):
                        ins = None
                        for kk in range(2):
                            k = kg * 2 + kk
                            for t in range(NT):
                                ins = e.matmul(out=bank[2 + t][:, :], lhsT=actT[:, k, t * 128:(t + 1) * 128],
                                               rhs=w4v[slot][:, kk, :], start=(k == 0), stop=(k == 43))
                        return ins
                    S.op("pe", pe_fn, [B_w4[slot], R0, R1, R2], [B_bank[2 + t] for t in range(NT)])
                for t in range(NT):
                    S.op("dve", lambda e, t=t, n=n: e.tensor_tensor(out=xb[:, t, n * 512:(n + 1) * 512],
                                                                    in0=xb[:, t, n * 512:(n + 1) * 512],
                                                                    in1=bank[2 + t][:, :], op=ALU.add),
                         [B_xb, B_bank[2 + t]], [B_xb])

        def proj_items(wsrc, wbuf, col0, nblocks):
            ids = []
            for n in range(nblocks):
                pair = []
                for hf in range(2):
                    def ld(slot, n=n, hf=hf):
                        dstv = w16v[slot].rearrange("p (k n) -> p k n", n=512)
                        src = wsrc[hf * 1024:(hf + 1) * 1024, col0 + n * 512:col0 + (n + 1) * 512].rearrange("(k p) n -> p k n", p=128)
                        dma("sp", dstv, src, [wbuf], [B_w16[slot]], B_w16[slot])
                    pair.append(w16s.add(ld))
                ids.append(pair)
            return ids

        def proj(srcT, srcbufs, NT, ids, evac, rows_last=128):
            for n, pair in enumerate(ids):
                slots = [w16s.acquire(pair[0]), w16s.acquire(pair[1])]
                wvs = [w16v[sl].rearrange("p (k n) -> p k n", n=512) for sl in slots]
                for t in range(NT):
                    nr = 128 if t < NT - 1 else rows_last
                    bi = next_bank()
                    def pe_fn(e, t=t, bi=bi, wvs=wvs, nr=nr):
                        ins = None
                        for k in range(16):
                            ins = e.matmul(out=bank[bi][0:nr, :], lhsT=srcT[:, k, t * 128:t * 128 + nr], rhs=wvs[k // 8][:, k % 8, :],
                                           start=(k == 0), stop=(k == 15))
                        return ins
                    S.op("pe", pe_fn, [B_w16[slots[0]], B_w16[slots[1]]] + srcbufs, [B_bank[bi]])
                    evac(t, n, bi, nr)

        def qk_norm(tile, tb, nr, grow, growb):
            si = rot("stat", 4)
            st, stb = stat_t[si][:], B_stat[si]
            S.op("dve", lambda e: e.tensor_tensor(out=sq_t[0:nr, 0:512], in0=tile[0:nr, :], in1=tile[0:nr, :], op=ALU.mult),
                 [tb], [B_sq])
            S.op("dve", lambda e: e.reduce_sum(out=st[0:nr, 0:4], in_=sq_t[0:nr, 0:512].rearrange("p (g d) -> p g d", g=4), axis=AX.X),
                 [B_sq], [stb])
            rstd = rstd_of(st[:, 0:4], st, stb, 4, 1.0 / 128, nr)
            S.op("dve", lambda e: e.tensor_tensor(out=tile[0:nr, :].rearrange("p (g d) -> p g d", g=4),
                                                  in0=tile[0:nr, :].rearrange("p (g d) -> p g d", g=4),
                                                  in1=rstd[0:nr, :].unsqueeze(2).to_broadcast([nr, 4, 128]), op=ALU.mult),
                 [tb, stb], [tb])
            S.op("dve", lambda e: e.tensor_tensor(out=tile[0:nr, :].rearrange("p (g d) -> p g d", g=4),
                                                  in0=tile[0:nr, :].rearrange("p (g d) -> p g d", g=4),
                                                  in1=grow[0:nr, :].unsqueeze(1).to_broadcast([nr, 4, 128]), op=ALU.mult),
                 [tb, growb], [tb])

        def transpose4(src, srcb, nr, dst, dstbufs):
            bi = rot("tb", 2)
            def pe_fn(e):
                ins = None
                for j in range(4):
                    ins = e.transpose(out=bankb[bi][:, j * 128:j * 128 + nr], in_=src[0:nr, j * 128:(j + 1) * 128],
                                      identity=ident_b[0:nr, 0:nr])
                return ins
            S.op("pe", pe_fn, [srcb, B_c["identb"]], [B_bank[bi]])
            pview = bankb[bi][:, 0:512].rearrange("p (k n) -> p k n", n=128)[:, :, 0:nr]
            S.op("act", lambda e: e.activation(out=dst, in_=pview, func=AF.Copy), [B_bank[bi]], dstbufs)

        def load_tabs_p1():
            load_const(psc_row, psc_d.partition_broadcast(128), B_tabs)
            load_const(pbs_row, pb_d.partition_broadcast(128), B_tabs)
            load_const(g0row, g0row_d.partition_broadcast(128), B_tabs)
            S.op("dve", lambda e: e.tensor_tensor(out=pbs_row, in0=pbs_row, in1=psc_row, op=ALU.mult), [B_tabs], [B_tabs])

        def pool_items():
            ids = []
            for g in range(4):
                def ld(slot, g=g):
                    dstv = w16v[slot][:, 0:2048].rearrange("p (k n) -> p k n", n=512)
                    src = pw_b[g * 512:(g + 1) * 512, :].rearrange("(k p) n -> p k n", p=128)
                    dma("sp", dstv, src, [B_pw], [B_w16[slot]], B_w16[slot])
                ids.append(w16s.add(ld))
            return ids

        def phase1_items():
            return dict(pool=pool_items(), ffn=ffn_items(0), kv=proj_items(wqkv_b, B_wqkv, D, 8))

        def phase1(items, NT, segs, x_src, halo_src, halo_is_x, pcorr_src, own_idx, scr_blk, k_dst, v_dst, x1_idx,
                   ps_dst):
            ntok = NT * 128 if NT * 128 <= sum(s[1] for s in segs) else sum(s[1] for s in segs)
            nseg = len(segs)
            seglen = segs[0][1]
            EXT = nseg * (16 + seglen)
            rows_last = ntok - (NT - 1) * 128
            dma("sp", xb[:, 0:NT, :] if rows_last == 128 else xb[0:rows_last, 0:1, :],
                x_src.rearrange("(t p) d -> p t d", p=min(128, ntok)), [], [B_xb], B_xb)
            pi = rot("vst", 2)
            if pcorr_src is not None:
                dma("sp", pcorr_t[pi][:], pcorr_src.partition_broadcast(128), [], [B_pcorr[pi]], B_pcorr[pi])
            for s_, (c0, _) in enumerate(segs):
                ui = rot("utm", 2)
                utm, utmb = utm_t[ui][:], B_utm[ui]
                hx = miscf[0:16, 0:2048]
                hxb = B_kf + B_kout[0:2]
                dma("sp", hx, halo_src[s_], [], hxb, B_kf[0])
                if halo_is_x:
                    si = rot("stat", 4)
                    st, stb = stat_t[si][:], B_stat[si]
                    S.op("dve", lambda e: e.tensor_tensor(out=sq_t[0:16, :], in0=hx, in1=hx, op=ALU.mult), hxb, [B_sq])
                    S.op("dve", lambda e, st=st: e.reduce_sum(out=st[0:16, 0:1], in_=sq_t[0:16, :], axis=AX.X), [B_sq], [stb])
                    rstd = rstd_of(st[:, 0:1], st, stb, 1, 1.0 / D, 16)
                    S.op("act", lambda e, utm=utm, rstd=rstd: e.activation(out=utm[0:16, :], in_=hx, func=AF.Copy,
                                                                           scale=rstd[0:16, :]), hxb + [stb], [utmb])
                    transpose16(utm, utmb, 16, uTe[:, :, c0:c0 + 16], [R0], 0)
                else:
                    S.op("act", lambda e, utm=utm: e.activation(out=utm[0:16, :], in_=hx, func=AF.Copy), hxb, [utmb])
                    transpose16(utm, utmb, 16, uTe[:, :, c0:c0 + 16], [R0], None)
            ckpt("halo")
            tiles_per_seg = max(1, seglen // 128)
            for t in range(NT):
                nr = 128 if t < NT - 1 else rows_last
                if seglen >= 128:
                    s_ = t // tiles_per_seg
                    cbase = segs[s_][0] + 16 + (t % tiles_per_seg) * 128
                    dst = uTe[:, :, cbase:cbase + 128]
                else:
                    nsg = nr // seglen
                    s0 = t * (128 // seglen)
                    cb0 = segs[s0][0] + 16
                    dst = uTe[:, :, cb0:cb0 + nsg * (16 + seglen)].rearrange("p k (s n) -> p k s n", n=16 + seglen)[:, :, :, 0:seglen]
                u32 = None
                if ps_dst is not None and ps_dst(t) is not None:
                    def u32(rstd, t=t, nr=nr):
                        S.op("dve", lambda e: e.scalar_tensor_tensor(out=sq_t[0:nr, :], in0=xb[0:nr, t, :], scalar=rstd[0:nr, :],
                                                                      in1=g0row[0:nr, :], op0=ALU.mult, op1=ALU.mult),
                             [B_xb, B_tabs] + B_stat, [B_sq])
                        for (r0, dst_ap) in ps_dst(t):
                            dma("sp", dst_ap, sq_t[r0:r0 + 16, :], [B_sq], [B_out], B_sq)
                if seglen >= 128:
                    norm_T(xb[:, t, :], B_xb, nr, 0, dst, [R0], u32)
                else:
                    norm_T_seg(xb[:, t, :], nr, dst, u32, nr // seglen, seglen)
            ckpt("norm0")
            for g in range(4):
                w = 2 << g
                cur = None
                sh = 1
                flip = 0
                for lvl in range(g + 1):
                    outb = sumA if flip == 0 else sumB
                    src = uTe[:, 4 * g:4 * g + 4, :] if cur is None else cur
                    S.op("dve" if g % 2 == 0 else "pool",
                         lambda e, outb=outb, src=src, sh=sh: e.tensor_tensor(out=outb[:, :, sh:EXT], in0=src[:, :, sh:EXT],
                                                                              in1=src[:, :, 0:EXT - sh], op=ALU.add),
                         [R0, R1], [R1])
                    cur = outb
                    sh *= 2
                    flip ^= 1
                if pcorr_src is not None:
                    c0 = segs[0][0] + 16
                    S.op("dve", lambda e, cur=cur, g=g, c0=c0: e.tensor_tensor(
                        out=cur[:, :, c0:c0 + 16], in0=cur[:, :, c0:c0 + 16],
                        in1=pcorr_t[pi][:, g * 16:(g + 1) * 16].unsqueeze(1).to_broadcast([128, 4, 16]), op=ALU.mult),
                         [R1, B_pcorr[pi]], [R1])
                if nseg == 1:
                    c0 = segs[0][0] + 16
                    S.op("dve", lambda e, cur=cur, g=g, c0=c0, w=w: e.scalar_tensor_tensor(
                        out=hT[:, 4 * g:4 * g + 4, 0:ntok], in0=cur[:, :, c0:c0 + ntok], scalar=1.0 / w,
                        in1=uTe[:, 4 * g:4 * g + 4, c0:c0 + ntok], op0=ALU.mult, op1=ALU.subtract), [R0, R1], [B_hT])
                else:
                    for kk in range(4):
                        sv = cur[:, kk, 0:EXT].rearrange("p (s n) -> p s n", n=16 + seglen)[:, :, 16:16 + seglen]
                        uv = uTe[:, 4 * g + kk, 0:EXT].rearrange("p (s n) -> p s n", n=16 + seglen)[:, :, 16:16 + seglen]
                        ov = hT[:, 4 * g + kk, 0:ntok].rearrange("p (s n) -> p s n", n=seglen)
                        S.op("dve", lambda e, sv=sv, uv=uv, ov=ov, w=w: e.scalar_tensor_tensor(
                            out=ov, in0=sv, scalar=1.0 / w, in1=uv, op0=ALU.mult, op1=ALU.subtract), [R0, R1], [B_hT])
            ckpt("sums")
            for t in range(NT):
                nr = 128 if t < NT - 1 else rows_last
                S.op("pool", lambda e, t=t, nr=nr: e.tensor_tensor(out=xb[0:nr, t, :], in0=xb[0:nr, t, :], in1=pbs_row[0:nr, :], op=ALU.add),
                     [B_xb, B_tabs], [B_xb])
            for g in range(4):
                slot = w16s.acquire(items["pool"][g])
                wv = w16v[slot][:, 0:2048].rearrange("p (k n) -> p k n", n=512)
                for t in range(NT):
                    nr = 128 if t < NT - 1 else rows_last
                    bi = next_bank()
                    def pe_fn(e, t=t, bi=bi, wv=wv, nr=nr, g=g):
                        ins = None
                        for kk in range(4):
                            ins = e.matmul(out=bank[bi][0:nr, :], lhsT=hT[:, 4 * g + kk, t * 128:t * 128 + nr], rhs=wv[:, kk, :],
                                           start=(kk == 0), stop=(kk == 3))
                        return ins
                    S.op("pe", pe_fn, [B_w16[slot], B_hT], [B_bank[bi]])
                    si = rot("kf", 2)
                    tmp, tmpb = kf[si], B_kf[si]
                    S.op("dve", lambda e, bi=bi, tmp=tmp, nr=nr, g=g: e.tensor_tensor(out=tmp[0:nr, :], in0=bank[bi][0:nr, :],
                                                                                      in1=psc_row[0:nr, g * 512:(g + 1) * 512], op=ALU.mult),
                         [B_bank[bi], B_tabs], [tmpb])
                    S.op("pool", lambda e, t=t, tmp=tmp, nr=nr, g=g: e.tensor_tensor(out=xb[0:nr, t, g * 512:(g + 1) * 512],
                                                                                     in0=xb[0:nr, t, g * 512:(g + 1) * 512],
                                                                                     in1=tmp[0:nr, :], op=ALU.add),
                         [B_xb, tmpb], [B_xb])
            ckpt("poolmm")
            ffn_rows(0, NT, 1, items["ffn"], rows_last)
            ckpt("ffn0")
            if x1_idx is not None:
                dma("sp", x1S[x1_idx, 0:ntok, :].rearrange("(t p) d -> p t d", p=min(128, ntok)),
                    xb[:, 0:NT, :] if rows_last == 128 else xb[0:rows_last, 0:1, :], [B_xb], [B_x1[x1_idx]], B_xb)
            for t in range(NT):
                nr = 128 if t < NT - 1 else rows_last
                norm_T(xb[:, t, :], B_xb, nr, 2, hT[:, :, t * 128:t * 128 + nr], [B_hT])

            ckpt("norm1")

            def evac(t, n, bi, nr):
                if n < 4:
                    ki = rot("kout", 3)
                    ko, kob = kout[ki], B_kout[ki]
                    S.op("act", lambda e: e.activation(out=ko[0:nr, :], in_=bank[bi][0:nr, :], func=AF.Copy), [B_bank[bi]], [kob])
                    qk_norm(ko, kob, nr, kg_row[:], B_c["kg"])
                    bi2 = rot("kb", 2)
                    kb, kbb = kb16[bi2], B_kb16[bi2]
                    S.op("act", lambda e: e.activation(out=kb[0:nr, :], in_=ko[0:nr, :], func=AF.Copy), [kob], [kbb])
                    if k_dst is not None:
                        dma("sp", k_dst[t * 128:t * 128 + nr, n * 512:(n + 1) * 512], ko[0:nr, :], [kob], [B_out], kob)
                    transpose4(kb, kbb, nr, ktT[:, 4 * n:4 * n + 4, t * 128:t * 128 + nr], [R0])
                else:
                    m = n - 4
                    vi = rot("vf", 2)
                    vv, vvb = vf[vi], B_vf[vi]
                    S.op("act", lambda e: e.activation(out=vv[0:nr, :], in_=bank[bi][0:nr, :], func=AF.Copy), [B_bank[bi]], [vvb])
                    if v_dst is not None:
                        dma("sp", v_dst[t * 128:t * 128 + nr, m * 512:(m + 1) * 512], vv[0:nr, :], [vvb], [B_out], vvb)
                    S.op("dve", lambda e: e.tensor_copy(out=vstage[t][0:nr, m * 512:(m + 1) * 512], in_=vv[0:nr, :]),
                         [vvb], [R1])

            proj(hT, [B_hT], NT, items["kv"], evac, rows_last)
            dma("sp", ktS[scr_blk][:, :, 0:ntok].rearrange("k p n -> p k n"), ktT[:, :, 0:ntok], [R0], [B_kt[scr_blk]], R0)
            for t in range(NT):
                nr = 128 if t < NT - 1 else rows_last
                dma("sp", vS[scr_blk, t * 128:t * 128 + nr, :], vstage[t][0:nr, :], [R1], [B_vs[scr_blk]], R1)

        def ffn_rows(layer, NT, gain_idx, items, rows_last):
            ffn(layer, NT, gain_idx, items)

        def norm_T_seg(x_ap, nr, dst4, u32, nsg, seglen):
            si = rot("stat", 4)
            st, stb = stat_t[si][:], B_stat[si]
            S.op("dve", lambda e: e.tensor_tensor(out=sq_t[0:nr, :], in0=x_ap[0:nr, :], in1=x_ap[0:nr, :], op=ALU.mult), [B_xb], [B_sq])
            S.op("dve", lambda e: e.reduce_sum(out=st[0:nr, 0:1], in_=sq_t[0:nr, :], axis=AX.X), [B_sq], [stb])
            rstd = rstd_of(st[:, 0:1], st, stb, 1, 1.0 / D, nr)
            ui = rot("utm", 2)
            utm, utmb = utm_t[ui][:], B_utm[ui]
            S.op("act", lambda e: e.activation(out=utm[0:nr, :], in_=x_ap[0:nr, :], func=AF.Copy, scale=rstd[0:nr, :]), [B_xb, stb], [utmb])
            if u32 is not None:
                u32(rstd)
            for half in range(2):
                bi = half
                def pe_fn(e, half=half, bi=bi):
                    ins = None
                    for j in range(8):
                        k = half * 8 + j
                        ins = e.transpose(out=bankb[bi][:, j * 128:j * 128 + nr], in_=utm[0:nr, k * 128:(k + 1) * 128],
                                          identity=ident_b[0:nr, 0:nr])
                    return ins
                S.op("pe", pe_fn, [utmb, B_c["identb"]], [B_bank[bi]])
                for j in range(8):
                    k = half * 8 + j
                    pview = bankb[bi][:, j * 128:j * 128 + nr].rearrange("p (s n) -> p s n", n=seglen)
                    dview = dst4[:, k, :, :]
                    S.op("dve", lambda e, pview=pview, dview=dview, k=k: e.tensor_scalar(out=dview, in0=pview, scalar1=gT[:, k:k + 1],
                                                                                       scalar2=None, op0=ALU.mult),
                         [B_bank[bi], B_c["gT"]], [R0])

        def load_tabs_p3():
            load_const(qrow.rearrange("p h n -> p (h n)"), qrow_d.partition_broadcast(128), B_tabs)
            load_const(dm.rearrange("p k n -> p (k n)"), dm_d, B_tabs)

        kts = Stream(B_ktr, 4)
        vts = Stream(B_vtr, 3)

        def init_vones():
            for i in range(3):
                S.op("pool", lambda e, i=i: e.memset(vtr[i][:, :, 256:257], 1.0), [], [B_vtr[i]])

        def attention(NQ, q_cols, ktiles, o_dst_rows, o_col_t):
            NTq = max(1, NQ // 128)
            nqr = min(128, NQ)
            blocks = []
            for kt in ktiles:
                if not blocks or blocks[-1][0] != (kt["blk"], kt.get("koff", 0)):
                    blocks.append(((kt["blk"], kt.get("koff", 0)), []))
                blocks[-1][1].append(kt)
            r0 = o_dst_rows
            ids = {}
            for h in range(NH):
                for c in range(2):
                    chunk = 2 * h + c
                    kid, vid = [], []
                    for (blk, koff), kl in blocks:
                        nkeys = sum(k_["nk"] for k_ in kl)
                        def ldk(slot, blk=blk, koff=koff, nkeys=nkeys, chunk=chunk):
                            dma("sp", ktr[slot][:, 0:nkeys], ktS[blk, chunk][:, koff:koff + nkeys], [B_kt[blk]], [B_ktr[slot]], B_ktr[slot])
                        def ldv(slot, blk=blk, koff=koff, kl=kl, h=h):
                            nkt = len(kl)
                            nk0 = kl[0]["nk"]
                            src = vS[blk, koff:koff + nkt * nk0, h * 256:(h + 1) * 256].rearrange("(k p) n -> p k n", p=nk0)
                            dma("sp", vtr[slot][0:nk0, 0:nkt, 0:256], src, [B_vs[blk]], [B_vtr[slot]], B_vtr[slot])
                        kid.append(kts.add(ldk))
                        vid.append(vts.add(ldv))
                    ids[(h, c)] = (kid, vid)
            LA = 2
            units = []
            for h in range(NH):
                for c in range(2):
                    kid, vid = ids[(h, c)]
                    nblocks = len(blocks)
                    for bi_, ((blk, koff), kl) in enumerate(blocks):
                        for j, kt in enumerate(kl):
                            units.append(dict(h=h, c=c, bi=bi_, j=j, kt=kt, kid=kid[bi_], vid=vid[bi_],
                                              first=(bi_ == 0 and j == 0),
                                              last=(bi_ == nblocks - 1 and j == len(kl) - 1)))
            U = len(units)

            def emit_front(u):
                h, c, j, kt = u["h"], u["c"], u["j"], u["kt"]
                chunk = 2 * h + c
                ks = kts.acquire(u["kid"])
                nk = kt["nk"]
                sbk = rot("sb", 4)
                def pe_s(e):
                    return e.matmul(out=bank[sbk][0:nk, 0:NQ], lhsT=ktr[ks][:, j * nk:(j + 1) * nk],
                                    rhs=qT[:, chunk, q_cols[0]:q_cols[0] + NQ], start=True, stop=True)
                S.op("pe", pe_s, [B_ktr[ks], R1], [B_bank[sbk]])
                ei = rot("ein", 3)
                eb, ebb = ein[ei], B_ein[ei]
                if kt["kind"] == "vis":
                    kc = kt["kcol"](h)
                    qr = kt["qrow"](h)
                    S.op("dve", lambda e: e.scalar_tensor_tensor(
                        out=eb[0:nk, 0:NQ], in0=bank[sbk][0:nk, 0:NQ], scalar=kc[0:nk, :], in1=qr[0:nk, 0:NQ],
                        op0=ALU.add, op1=ALU.add), [B_bank[sbk], B_kcol, B_tabs, B_c["qrows"], B_c["kcols"]], [ebb])
                else:
                    dmt = kt["dm"]
                    sl = kt["nslope"](h)
                    S.op("dve", lambda e: e.scalar_tensor_tensor(
                        out=eb[0:nk, 0:NQ], in0=dmt[0:nk, 0:NQ], scalar=sl, in1=bank[sbk][0:nk, 0:NQ],
                        op0=ALU.mult, op1=ALU.add), [B_bank[sbk], B_tabs, B_c["dms"]], [ebb])
                pi_ = rot("pt", 3)
                pb_, pbb = ptr[pi_], B_ptr[pi_]
                u["pb"] = (pb_, pbb)
                S.op("act", lambda e: e.activation(out=pb_[0:nk, 0:NQ], in_=eb[0:nk, 0:NQ], func=AF.Exp, scale=SCALE), [ebb], [pbb])

            def emit_back(u):
                h, c, j, kt = u["h"], u["c"], u["j"], u["kt"]
                nk = kt["nk"]
                pb_, pbb = u["pb"]
                vs_ = vts.acquire(u["vid"])
                first, last = u["first"], u["last"]
                def pe_av(e):
                    ins = None
                    for t in range(NTq):
                        ins = e.matmul(out=bank[4 + t][r0:r0 + nqr, 0:257], lhsT=pb_[0:nk, t * 128:t * 128 + nqr],
                                       rhs=vtr[vs_][0:nk, j, 0:257], start=first, stop=last)
                    return ins
                S.op("pe", pe_av, [pbb, B_vtr[vs_]], [B_bank[4 + t] for t in range(NTq)])
                if not last:
                    return
                for t in range(NTq):
                    S.op("dve", lambda e, t=t: e.reciprocal(out=rs_t[r0:r0 + nqr, t:t + 1], in_=bank[4 + t][r0:r0 + nqr, 256:257]),
                         [B_bank[4 + t]], [B_rs])
                    if c == 0:
                        S.op("dve", lambda e, t=t: e.tensor_scalar(out=tmp0[r0:r0 + nqr, t, :], in0=bank[4 + t][r0:r0 + nqr, 0:256],
                                                                   scalar1=rs_t[r0:r0 + nqr, t:t + 1], scalar2=None, op0=ALU.mult),
                             [B_bank[4 + t], B_rs], [B_tmp0])
                    else:
                        S.op("dve", lambda e, t=t: e.tensor_tensor(out=rs_t[r0:r0 + nqr, 4 + t:5 + t], in0=rs_t[r0:r0 + nqr, t:t + 1],
                                                                   in1=lamneg[r0:r0 + nqr, :], op=ALU.mult), [B_rs, B_const], [B_rs])
                        S.op("dve", lambda e, t=t: e.scalar_tensor_tensor(
                            out=o_tm[r0:r0 + nqr, o_col_t + t, h * 256:(h + 1) * 256], in0=bank[4 + t][r0:r0 + nqr, 0:256],
                            scalar=rs_t[r0:r0 + nqr, 4 + t:5 + t], in1=tmp0[r0:r0 + nqr, t, :], op0=ALU.mult, op1=ALU.add),
                             [B_bank[4 + t], B_rs, B_tmp0], [R0])

            for ui in range(U + LA):
                if ui < U:
                    emit_front(units[ui])
                if ui - LA >= 0:
                    emit_back(units[ui - LA])

        def subln_tile(t, nr):
            si = rot("stat", 4)
            st, stb = stat_t[si][:], B_stat[si]
            ov = o_tm[0:nr, t, :]
            S.op("dve", lambda e: e.tensor_tensor(out=sq_t[0:nr, :], in0=ov, in1=ov, op=ALU.mult), [R0], [B_sq])
            S.op("dve", lambda e: e.reduce_sum(out=st[0:nr, 0:8], in_=sq_t[0:nr, :].rearrange("p (h d) -> p h d", h=8), axis=AX.X),
                 [B_sq], [stb])
            S.op("dve", lambda e: e.tensor_scalar(out=st[0:nr, 8:16], in0=st[0:nr, 0:8], scalar1=1.0 / 256, scalar2=EPS,
                                                  op0=ALU.mult, op1=ALU.add), [stb], [stb])
            S.op("pool", lambda e: e.tensor_tensor(out=st[0:nr, 0:8], in0=st[0:nr, 8:16], in1=mhalf[0:nr, :].to_broadcast([nr, 8]),
                                                   op=ALU.pow), [stb, B_const], [stb])
            S.op("dve", lambda e: e.tensor_tensor(out=sq_t[0:nr, :].rearrange("p (h d) -> p h d", h=8),
                                                  in0=ov.rearrange("p (h d) -> p h d", h=8),
                                                  in1=st[0:nr, 0:8].unsqueeze(2).to_broadcast([nr, 8, 256]), op=ALU.mult),
                 [R0, stb], [B_sq])
            ui = rot("utm", 2)
            utm, utmb = utm_t[ui][:], B_utm[ui]
            S.op("dve", lambda e: e.tensor_tensor(out=utm[0:nr, :].rearrange("p (h d) -> p h d", h=8),
                                                  in0=sq_t[0:nr, :].rearrange("p (h d) -> p h d", h=8),
                                                  in1=sub_row[0:nr, :].unsqueeze(1).to_broadcast([nr, 8, 256]), op=ALU.mult),
                 [B_sq, B_c["sub"]], [utmb])
            return utm, utmb

        def phase3_items():
            return dict(q=proj_items(wqkv_b, B_wqkv, 0, 4), o=proj_items(wo_b, B_wo, 0, 4), ffn=ffn_items(1))

        def phase3(items, NT, ntok, x1_idx, attn_fn, y_dst):
            rows_last = ntok - (NT - 1) * 128
            dma("sp", xb[:, 0:NT, :] if rows_last == 128 else xb[0:rows_last, 0:1, :],
                x1S[x1_idx, 0:ntok, :].rearrange("(t p) d -> p t d", p=min(128, ntok)), [B_x1[x1_idx]], [B_xb], B_xb)
            for t in range(NT):
                nr = 128 if t < NT - 1 else rows_last
                norm_T(xb[:, t, :], B_xb, nr, 2, hT[:, :, t * 128:t * 128 + nr], [B_hT])

            def evac_q(t, n, bi, nr):
                qf, qfb = ein[2], B_ein[2]
                S.op("act", lambda e: e.activation(out=qf[0:nr, :], in_=bank[bi][0:nr, :], func=AF.Copy), [B_bank[bi]], [qfb])
                qk_norm(qf, qfb, nr, qg_row[:], B_c["qg"])
                ui = rot("utm", 2)
                qb, qbb = utm_t[ui][:, 0:512], B_utm[ui]
                S.op("act", lambda e: e.activation(out=qb[0:nr, :], in_=qf[0:nr, :], func=AF.Copy), [qfb], [qbb])
                transpose4(qb, qbb, nr, qT[:, 4 * n:4 * n + 4, t * 128:t * 128 + nr], [R1])
            proj(hT, [B_hT], NT, items["q"], evac_q, rows_last)
            attn_fn()
            for t in range(NT):
                nr = 128 if t < NT - 1 else rows_last
                utm, utmb = subln_tile(t, nr)
                transpose16(utm, utmb, nr, hT[:, :, t * 128:t * 128 + nr], [B_hT], None)

            def evac_o(t, n, bi, nr):
                S.op("dve", lambda e: e.tensor_tensor(out=xb[0:nr, t, n * 512:(n + 1) * 512], in0=xb[0:nr, t, n * 512:(n + 1) * 512],
                                                      in1=bank[bi][0:nr, :], op=ALU.add), [B_xb, B_bank[bi]], [B_xb])
            proj(hT, [B_hT], NT, items["o"], evac_o, rows_last)
            ffn_rows(1, NT, 3, items["ffn"], rows_last)
            dma("sp", y_dst.rearrange("(t p) d -> p t d", p=min(128, ntok)),
                xb[:, 0:NT, :] if rows_last == 128 else xb[0:rows_last, 0:1, :], [B_xb], [B_out], B_xb)

        class _Stop(Exception):
            pass

        ck_state = {"n": 0}

        def ckpt(tag):
            ck_state["n"] += 1
            if cfg.substop is not None and ck_state["n"] >= cfg.substop:
                print("STOP at checkpoint", ck_state["n"], tag, flush=True)
                raise _Stop()

        def program():
            try:
                program_inner()
            except _Stop:
                pass

        def program_inner():
            if cfg.stop == 0:
                return
            load_tabs_p1()
            reg = {}

            def reg_p1(key):
                reg[key] = phase1_items()

            order = [("p", lb) for lb in range(NBLK)] + [("s", 0)]
            reg_p1(order[0])
            for oi, key in enumerate(order):
                if cfg.stop is not None and oi + 1 >= cfg.stop:
                    return
                if oi + 1 < len(order):
                    reg_p1(order[oi + 1])
                if key[0] == "p":
                    lb = key[1]
                    own = (lb % 2 == 0)
                    i = lb // 2
                    ps_fn = None
                    if lb >= NBLK - 2:
                        def ps_fn(t, lb=lb):
                            return [(112, ps_p[lb - (NBLK - 2)])] if t == 3 else None
                    phase1(reg[key], 4, [(0, 512)], xloc[lb], [xhalo[lb]], True, pcorr[lb:lb + 1, :], own, lb,
                           k_own[i] if own else None, v_own[i] if own else None, i if own else None, ps_fn)
                else:
                    for s_ in range(NS):
                        for cb in range(NCB):
                            blk = NBLK + s_ * NCB + cb
                            dma("sp", xb[:, :, :], ck[s_, cb * 512:(cb + 1) * 512, :].rearrange("(t p) d -> p t d", p=128), [], [B_xb], B_xb)
                            for t in range(4):
                                ui = rot("utm", 2)
                                utm, utmb = utm_t[ui][:], B_utm[ui]
                                S.op("act", lambda e, utm=utm, t=t: e.activation(out=utm, in_=xb[:, t, :], func=AF.Copy), [B_xb], [utmb])
                                transpose16(utm, utmb, 128, ktT[:, :, t * 128:(t + 1) * 128], [R0], None)
                            dma("sp", ktS[blk].rearrange("k p n -> p k n"), ktT, [R0], [B_kt[blk]], R0)
                    segs = [(s_ * 80, 64) for s_ in range(NS)]

                    def ps_fn_s(t):
                        out = []
                        for s_ in range(NS):
                            if (s_ * 64) // 128 == t:
                                out.append((((s_ * 64) % 128) + 48, ps_s[s_]))
                        return out
                    phase1(reg[key], NTS, segs, xs, [shalo[s_] for s_ in range(NS)], False, None, True, cfg.nscr - 1,
                           k_s, v_s, NSTEP, ps_fn_s)

            S.op("pool", lambda e: e.memset(rs_t[:, 0:8], 1.0), [], B_kout + B_vf + B_kb16 + B_kf + B_ktr + B_vtr + [B_rs])
            load_tabs_p3()
            init_vones()

            def slope_neg(h):
                return -float(2.0 ** (-(h + 1))) * SQ128

            reg3 = {}
            order3 = [("p", i) for i in range(NSTEP)] + [("s", 0)]
            reg3[order3[0]] = phase3_items()
            for oi, key in enumerate(order3):
                if oi + 1 < len(order3):
                    reg3[order3[oi + 1]] = phase3_items()
                if key[0] == "p":
                    i = key[1]
                    nkt = (2 * i + 2) * 4

                    def attn_fn(i=i, nkt=nkt):
                        S.op("pool", lambda e: e.tensor_scalar(out=kdiff_t[:, 0:nkt], in0=kpos_t[:, 0:nkt], scalar1=qref_t[:, i:i + 1],
                                                               scalar2=None, op0=ALU.subtract), [B_c["kpos"], B_c["qref"]], [B_c["kdiff"]])
                        S.op("pool", lambda e: e.tensor_tensor(out=kcol_t[:, 0:nkt, :],
                                                               in0=kdiff_t[:, 0:nkt].unsqueeze(2).to_broadcast([128, nkt, 8]),
                                                               in1=slopes_t[:].unsqueeze(1).to_broadcast([128, nkt, 8]), op=ALU.mult),
                             [B_c["kdiff"], B_c["slopes"]], [B_kcol])
                        S.op("pool", lambda e: e.tensor_scalar(out=kcol_t[:, nkt - 4:nkt, :], in0=kcol_t[:, nkt - 4:nkt, :],
                                                               scalar1=omask_t[:, i:i + 1], scalar2=None, op0=ALU.add),
                             [B_kcol, B_c["omask"]], [B_kcol])
                        ktiles = []
                        for lb in range(2 * i + 2):
                            for kt in range(4):
                                g = lb * 4 + kt
                                if lb == 2 * i:
                                    ktiles.append(dict(blk=lb, kt=kt, nk=128, kind="diag", dm=dm[:, kt, :], nslope=slope_neg))
                                else:
                                    ktiles.append(dict(blk=lb, kt=kt, nk=128, kind="vis",
                                                       kcol=(lambda h, g=g: kcol_t[:, g, h:h + 1]),
                                                       qrow=(lambda h: qrow[:, h, :])))
                        attention(512, (0, 512), ktiles, 0, 0)
                    phase3(reg3[key], 4, 512, i, attn_fn, y_own[i])
                else:
                    def attn_fn_s():
                        for s_ in range(NS):
                            ktiles = []
                            for cb in range(NCB):
                                for kt in range(4):
                                    g = cb * 4 + kt
                                    ktiles.append(dict(blk=NBLK + s_ * NCB + cb, kt=kt, nk=128, kind="vis",
                                                       kcol=(lambda h, g=g: kcols_t[:, g, h:h + 1]),
                                                       qrow=(lambda h: qrows_t[:, h, :])))
                            ktiles.append(dict(blk=cfg.nscr - 1, koff=s_ * 64, kt=0, nk=64, kind="diag", dm=dms_t[:, :], nslope=slope_neg))
                            attention(64, (s_ * 64, 64), ktiles, (s_ * 64) % 128, (s_ * 64) // 128)
                    phase3(reg3[key], NTS, NSTOK, NSTEP, attn_fn_s, y_s)


        program()

        S.finalize()
        engsem = {}
        for e_ in ("pe", "act", "dve", "pool"):
            engsem[e_] = es.enter_context(nc.semaphore("sem_" + e_))
        for b_ in S.sembufs:
            b_.sem = es.enter_context(nc.semaphore("d_" + b_.name))
        block = es.enter_context(nc.Block())

        def run(name, e):
            for o in S.q[name]:
                for (sm, v) in o.waits:
                    e.wait_ge(engsem[sm] if isinstance(sm, str) else sm.sem, v)
                ins = o.fn(e)
                if o.isdma:
                    ins.then_inc(o.sembuf.sem, 16)
                elif o.inc:
                    ins.then_inc(engsem[name], 1)
            if name == "sp":
                for b_ in S.sembufs:
                    e.wait_ge(b_.sem, 16 * b_.dcnt)

        @block.sync
        def _(e):
            run("sp", e)

        @block.tensor
        def _(e):
            run("pe", e)

        @block.scalar
        def _(e):
            run("act", e)

        @block.vector
        def _(e):
            run("dve", e)

        @block.gpsimd
        def _(e):
            run("pool", e)
    return nc


def _zigzag(nstep, role):
    own, other = [], []
    for i in range(nstep):
        a, b = 2 * i, 2 * i + 1
        first_owner = 0 if i % 2 == 0 else 1
        if role == first_owner:
            own.append(a); other.append(b)
        else:
            own.append(b); other.append(a)
    return own, other


def _const_tables(cfg):
    slopes = np.array([2.0 ** (-(h + 1)) for h in range(8)], np.float64) * SQ128
    p = np.arange(128)
    j = np.arange(512)
    dm = np.zeros((128, 4, 512), np.float32)
    for kt in range(4):
        k = kt * 128 + p[:, None]
        vis = (k // 64) <= (j[None, :] // 64)
        dm[:, kt, :] = np.where(vis, np.abs(j[None, :] - k), BIG)
    qrow = (-slopes[:, None] * (j[None, :] - 256)).astype(np.float32).reshape(1, 8 * 512)
    ref = cfg.past + 32
    kcols = np.zeros((128, cfg.ncb * 4, 8), np.float32)
    for g in range(cfg.ncb * 4):
        kcols[:, g, :] = ((g * 128 + p[:, None]) - ref) * slopes[None, :]
    jq = np.arange(64)
    qrows = (-slopes[:, None] * (cfg.past + jq[None, :] - ref)).astype(np.float32).reshape(1, 8 * 64)
    dms = np.zeros((128, 64), np.float32)
    dms[0:64, :] = np.abs(jq[None, :] - np.arange(64)[:, None])
    return dict(dm=dm.reshape(128, 4 * 512), qrow=qrow, slopes=slopes.astype(np.float32).reshape(1, 8),
                kcols=kcols.reshape(128, -1), qrows=qrows, dms=dms, ident=np.eye(128, dtype=np.float32))


def _prepare(inputs, cfg):
    f = lambda a: np.ascontiguousarray(np.asarray(a, dtype=np.float32))
    xp, xsmp, sp_ = f(inputs["x_prompt"]), f(inputs["x_sample"]), f(inputs["state_pool"])
    ck, cv = f(inputs["cache_k"]), f(inputs["cache_v"])
    NBLK, NSTEP, NS = cfg.nblk, cfg.nstep, cfg.ns
    consts = _const_tables(cfg)
    nm, nf = f(inputs["norm_mix"]), f(inputs["norm_ffn"])
    gains = [nm[0], nf[0], nm[1], nf[1]]
    gT = np.concatenate([g.reshape(16, 128).T for g in gains], axis=1)
    shared = dict(consts)
    shared.update(
        gT=np.ascontiguousarray(gT), g0row=nm[0].reshape(1, D), pool_b=f(inputs["pool_b"]).reshape(1, D),
        pool_scale=f(inputs["pool_scale"]).reshape(1, D), q_norm=f(inputs["q_norm"]).reshape(1, 128),
        k_norm=f(inputs["k_norm"]).reshape(1, 128), subln=f(inputs["subln"]).reshape(1, 256),
        lams=np.stack([f(inputs["lambda_q1"]), f(inputs["lambda_k1"]), f(inputs["lambda_q2"]), f(inputs["lambda_k2"])]),
        pool_w=f(inputs["pool_w"]).reshape(4 * 512, 512), w_qkv=f(inputs["w_qkv"]), w_o=f(inputs["w_o"]),
        w_gate=f(inputs["w_gate"]), w_up=f(inputs["w_up"]), w_down=f(inputs["w_down"]))
    in_maps, meta = [], []
    for c in range(cfg.ncores):
        b, role = c // 2, c % 2
        own, other = _zigzag(NSTEP, role)
        L = []
        for i in range(NSTEP):
            L += [own[i], other[i]]
        xloc = np.stack([xp[b, g * 512:(g + 1) * 512] for g in L])
        xhalo = np.zeros((NBLK, 16, D), np.float32)
        pcorr = np.ones((NBLK, 4, 16), np.float32)
        for li, g in enumerate(L):
            if g > 0:
                xhalo[li] = xp[b, g * 512 - 16:g * 512]
            else:
                t = np.arange(16)
                for gi, w in enumerate((2, 4, 8, 16)):
                    pcorr[li, gi] = w / np.minimum(w, t + 1)
        kpos = np.zeros((128, NBLK * 4), np.float32)
        for li, g in enumerate(L):
            for kt in range(4):
                kpos[:, li * 4 + kt] = g * 512 + kt * 128 + np.arange(128)
        qref = np.tile(np.array([g * 512 + 256 for g in own], np.float32)[None, :], (128, 1))
        omask = np.tile(np.array([0.0 if other[i] < own[i] else -BIG for i in range(NSTEP)], np.float32)[None, :], (128, 1))
        shalo = np.zeros((NS, 16, D), np.float32)
        shalo[:, 1:, :] = sp_[c * NS:(c + 1) * NS]
        m = dict(shared)
        m.update(xloc=xloc, xhalo=xhalo, pcorr=pcorr.reshape(NBLK, 64), xs=xsmp[c * NS:(c + 1) * NS].reshape(NS * 64, D),
                 shalo=shalo, ck=ck[c * NS:(c + 1) * NS].reshape(NS, cfg.past, D), cv=cv[c * NS:(c + 1) * NS].reshape(NS, cfg.past, D),
                 kpos=kpos, qref=np.ascontiguousarray(qref), omask=np.ascontiguousarray(omask))
        in_maps.append(m)
        meta.append((b, role, own, L))
    return in_maps, meta


def _assemble(results, meta, cfg, nb_prompt, nb_sample):
    NS, NSTEP = cfg.ns, cfg.nstep
    T = cfg.nblk * 512
    y_p = np.zeros((nb_prompt, T, D), np.float32)
    k_p = np.zeros((nb_prompt, T, D), np.float32)
    v_p = np.zeros((nb_prompt, T, D), np.float32)
    ps_p = np.zeros((nb_prompt, 15, D), np.float32)
    y_s = np.zeros((nb_sample, 64, D), np.float32)
    k_s = np.zeros((nb_sample, 64, D), np.float32)
    v_s = np.zeros((nb_sample, 64, D), np.float32)
    ps_s = np.zeros((nb_sample, 15, D), np.float32)
    for c, r in enumerate(results):
        b, role, own, L = meta[c]
        for i, g in enumerate(own):
            y_p[b, g * 512:(g + 1) * 512] = r["y_own"][i]
            k_p[b, g * 512:(g + 1) * 512] = r["k_own"][i]
            v_p[b, g * 512:(g + 1) * 512] = r["v_own"][i]
        for j in range(2):
            if L[cfg.nblk - 2 + j] == cfg.nblk - 1 and role == 0:
                ps_p[b] = r["ps_p"][j][1:16]
        y_s[c * NS:(c + 1) * NS] = r["y_s"].reshape(NS, 64, D)
        k_s[c * NS:(c + 1) * NS] = r["k_s"].reshape(NS, 64, D)
        v_s[c * NS:(c + 1) * NS] = r["v_s"].reshape(NS, 64, D)
        ps_s[c * NS:(c + 1) * NS] = r["ps_s"][:, 1:16, :]
    return (y_p, y_s, ps_p, ps_s, k_p.reshape(nb_prompt, T, 8, 256), v_p.reshape(nb_prompt, T, 8, 256),
            k_s.reshape(nb_sample, 64, 8, 256), v_s.reshape(nb_sample, 64, 8, 256))


def run_cfg(inputs, cfg):
    in_maps, meta = _prepare(inputs, cfg)
    nc = build(cfg)
    res = run_bass_kernel_spmd(nc, in_maps, core_ids=list(range(cfg.ncores)))
    return _assemble(res.results, meta, cfg, cfg.ncores // 2, cfg.ncores * cfg.ns)


def kernel(**inputs):
    cfg = Cfg(nblk=16, ns=4, past=1024, ncores=8)
    return run_cfg(inputs, cfg)
```

```python
import contextlib
import numpy as np
import concourse.bass as bass
import concourse.mybir as mybir
from concourse.bass_utils import run_bass_kernel_spmd

F32 = mybir.dt.float32
BF16 = mybir.dt.bfloat16
ALU = mybir.AluOpType
AF = mybir.ActivationFunctionType
AX = mybir.AxisListType

D = 2048
DFF = 5632
NKC = 16
NFC = 44
NH = 8
EPS = 1e-6
SQ128 = float(np.sqrt(128.0))
SCALE = float(128.0 ** -0.5)
LAM_INIT = float(0.8 - 0.6 * np.exp(-0.3 * 1))
BIG = 1.0e30
SAME_ENGINE_SYNC = True


class Cfg:
    def __init__(self, nblk=16, ns=4, past=1024, ncores=8):
        self.nblk = nblk
        self.nstep = nblk // 2
        self.ns = ns
        self.past = past
        self.ncores = ncores
        self.ncb = past // 512
        self.nscr = nblk + ns * self.ncb + 1
        self.stop = None
        self.substop = None


class SemSlot:
    __slots__ = ("name", "sem", "dcnt")

    def __init__(self, name):
        self.name = name
        self.sem = None
        self.dcnt = 0


class Buf:
    __slots__ = ("name", "lw", "rd", "ds")

    def __init__(self, name):
        self.name = name
        self.lw = None
        self.rd = {}
        self.ds = {}


class Op:
    __slots__ = ("eng", "fn", "deps", "isdma", "sembuf", "dval", "inc", "cnt", "waits")


ENGS = ("pe", "act", "dve", "pool", "sp")


class Sched:
    def __init__(self):
        self.q = {e: [] for e in ENGS}
        self.sembufs = []
        self.nops = 0

    def op(self, eng, fn, reads=(), writes=(), dma_sem=None):
        o = Op()
        o.eng = eng
        o.fn = fn
        o.isdma = dma_sem is not None
        o.inc = False
        o.cnt = 0
        o.sembuf = None
        o.dval = 0
        deps = []
        for b in reads:
            if b.lw is not None:
                deps.append(b.lw)
        for b in writes:
            if b.lw is not None:
                deps.append(b.lw)
            deps.extend(b.rd.values())
        key = ("d", self.nops) if o.isdma else eng
        for b in reads:
            b.rd[key] = o
        for b in writes:
            b.lw = o
            b.rd = {}
        o.deps = [d for d in set(deps) if d is not o]
        if o.isdma:
            slot = dma_sem.ds.get(eng)
            if slot is None:
                slot = SemSlot(dma_sem.name + "_" + eng)
                dma_sem.ds[eng] = slot
                self.sembufs.append(slot)
            slot.dcnt += 1
            o.sembuf = slot
            o.dval = 16 * slot.dcnt
        self.q[eng].append(o)
        self.nops += 1
        return o

    def _skip(self, o, d):
        if d.isdma:
            return False
        if d.eng == o.eng and not o.isdma:
            if d.eng == "pe":
                return True
            return not SAME_ENGINE_SYNC
        return False

    def finalize(self):
        for e in ENGS:
            for o in self.q[e]:
                for d in o.deps:
                    if not d.isdma and not self._skip(o, d):
                        d.inc = True
        for e in ENGS:
            c = 0
            for o in self.q[e]:
                if not o.isdma and o.inc:
                    c += 1
                    o.cnt = c
        for e in ENGS:
            seen = {}
            for o in self.q[e]:
                need = {}
                for d in o.deps:
                    if self._skip(o, d):
                        continue
                    if d.isdma:
                        k = ("d", id(d.sembuf))
                        v = d.dval
                        s = d.sembuf
                    else:
                        k = ("e", d.eng)
                        v = d.cnt
                        s = d.eng
                    if seen.get(k, 0) >= v:
                        continue
                    if k not in need or need[k][1] < v:
                        need[k] = (s, v)
                for k, (s, v) in need.items():
                    seen[k] = v
                o.waits = list(need.values())


def build(cfg):
    nc = bass.Bass("TRN2", target_bir_lowering=False)
    S = Sched()
    NBLK, NSTEP, NS, NCB = cfg.nblk, cfg.nstep, cfg.ns, cfg.ncb
    NSTOK = NS * 64
    NTS = NSTOK // 128
    assert NSTOK % 128 == 0 or NS == 1
    if NS == 1:
        NTS = 1
    SROWS = min(128, NSTOK)

    def din(name, shape, dt=F32):
        return nc.dram_tensor(name, list(shape), dt, kind="ExternalInput").ap()

    def dout(name, shape, dt=F32):
        return nc.dram_tensor(name, list(shape), dt, kind="ExternalOutput").ap()

    def dint(name, shape, dt):
        return nc.dram_tensor(name, list(shape), dt, kind="Internal").ap()

    xloc = din("xloc", [NBLK, 512, D])
    xhalo = din("xhalo", [NBLK, 16, D])
    pcorr = din("pcorr", [NBLK, 64])
    xs = din("xs", [NS * 64, D])
    shalo = din("shalo", [NS, 16, D])
    ck = din("ck", [NS, cfg.past, D])
    cv = din("cv", [NS, cfg.past, D])
    kpos_d = din("kpos", [128, NBLK * 4])
    qref_d = din("qref", [128, NSTEP])
    omask_d = din("omask", [128, NSTEP])
    ident_d = din("ident", [128, 128])
    dm_d = din("dm", [128, 4 * 512])
    qrow_d = din("qrow", [1, 8 * 512])
    slopes_d = din("slopes", [1, 8])
    kcols_d = din("kcols", [128, NCB * 4 * 8])
    qrows_d = din("qrows", [1, 8 * 64])
    dms_d = din("dms", [128, 64])
    gT_d = din("gT", [128, 64])
    g0row_d = din("g0row", [1, D])
    pb_d = din("pool_b", [1, D])
    psc_d = din("pool_scale", [1, D])
    qn_d = din("q_norm", [1, 128])
    kn_d = din("k_norm", [1, 128])
    sub_d = din("subln", [1, 256])
    lam_d = din("lams", [4, 128])
    pool_w = din("pool_w", [4 * 512, 512])
    w_qkv = din("w_qkv", [D, 3 * D])
    w_o = din("w_o", [D, D])
    w_gate = din("w_gate", [2, D, DFF])
    w_up = din("w_up", [2, D, DFF])
    w_down = din("w_down", [2, DFF, D])

    y_own = dout("y_own", [NSTEP, 512, D])
    y_s = dout("y_s", [NS * 64, D])
    ps_p = dout("ps_p", [2, 16, D])
    ps_s = dout("ps_s", [NS, 16, D])
    k_own = dout("k_own", [NSTEP, 512, D])
    v_own = dout("v_own", [NSTEP, 512, D])
    k_s = dout("k_s", [NS * 64, D])
    v_s = dout("v_s", [NS * 64, D])

    pw_b = dint("pw_b", [4 * 512, 512], BF16)
    wqkv_b = dint("wqkv_b", [D, 3 * D], BF16)
    wo_b = dint("wo_b", [D, D], BF16)
    wg_b = dint("wg_b", [2, D, DFF], BF16)
    wu_b = dint("wu_b", [2, D, DFF], BF16)
    wd_b = dint("wd_b", [2, DFF, D], BF16)
    ktS = dint("ktS", [cfg.nscr, 16, 128, 512], BF16)
    vS = dint("vS", [cfg.nscr, 512, D], BF16)
    x1S = dint("x1S", [NSTEP + 1, 512, D], F32)

    B_pw, B_wqkv, B_wo = Buf("pw"), Buf("wqkv"), Buf("wo")
    GRP = [(0, 1536), (1536, 3072), (3072, 4608), (4608, 5632)]
    B_wg = [[Buf("wg%d_%d" % (l, g)) for g in range(4)] for l in range(2)]
    B_wu = [[Buf("wu%d_%d" % (l, g)) for g in range(4)] for l in range(2)]
    B_wd = [Buf("wd0"), Buf("wd1")]
    B_kt = [Buf("kt%d" % i) for i in range(cfg.nscr)]
    B_vs = [Buf("vs%d" % i) for i in range(cfg.nscr)]
    B_x1 = [Buf("x1_%d" % i) for i in range(NSTEP + 1)]

    es = contextlib.ExitStack()
    with es:
        def sb(name, shape, dt):
            return es.enter_context(nc.sbuf_tensor("s_" + name, list(shape), dt))

        xb_t = sb("xb", [128, 4, D], F32)
        hT_t = sb("hT", [128, 16, 512], BF16)
        big_t = sb("big", [128, 44 * 512], BF16)
        w16_t = [sb("w16_%d" % i, [128, 4096], BF16) for i in range(4)]
        w4_t = [sb("w4_%d" % i, [128, 2, 512], BF16) for i in range(4)]
        sq_t = sb("sq", [128, D], F32)
        utm_t = [sb("utm%d" % i, [128, D], BF16) for i in range(2)]
        tabs_t = sb("tabs", [128, 3 * D], F32)
        misc_t = sb("misc", [128, 8192], BF16)
        ptr_t = sb("ptr", [128, 3 * 512], BF16)
        ident_b = sb("ident_b", [128, 128], BF16)
        gT = sb("gT", [128, 64], F32)
        stat_t = [sb("stat%d" % i, [128, 16], F32) for i in range(4)]
        consts = sb("consts", [128, 8], F32)
        qg_row = sb("qg_row", [128, 128], F32)
        kg_row = sb("kg_row", [128, 128], F32)
        sub_row = sb("sub_row", [128, 256], F32)
        kpos_t = sb("kpos_t", [128, NBLK * 4], F32)
        qref_t = sb("qref_t", [128, NSTEP], F32)
        omask_t = sb("omask_t", [128, NSTEP], F32)
        slopes_t = sb("slopes_t", [128, 8], F32)
        kdiff_t = sb("kdiff_t", [128, NBLK * 4], F32)
        kcol_t = sb("kcol_t", [128, NBLK * 4, 8], F32)
        kcolS_t = sb("kcolS_t", [128, NBLK * 4, 8], F32)
        kcols_t = sb("kcols_t", [128, NCB * 4, 8], F32)
        dms_t = sb("dms_t", [128, 64], F32)
        qrows_t = sb("qrows_t", [128, 8, 64], F32)
        pcorr_t = [sb("pcorr%d" % i, [128, 64], F32) for i in range(2)]
        rs_t = sb("rs_t", [128, 8], F32)
        ps_t = es.enter_context(nc.psum_tensor("ps", [128, 8, 512], F32))

        xb = xb_t[:]
        hT = hT_t[:]
        big = big_t[:]

        R0, R1, R2 = Buf("R0"), Buf("R1"), Buf("R2")

        def rbuf(c):
            return R0 if c < 17 else (R1 if c < 34 else R2)

        actT = big.rearrange("p (c n) -> p c n", n=512)
        uTe = big[:, 0:16 * 528].rearrange("p (k n) -> p k n", n=528)
        sums_f = big[:, 17 * 512:34 * 512].bitcast(F32)
        sumA = sums_f[:, 0:4 * 528].rearrange("p (k n) -> p k n", n=528)
        sumB = sums_f[:, 4 * 528:8 * 528].rearrange("p (k n) -> p k n", n=528)
        ktT = big[:, 0:16 * 512].rearrange("p (k n) -> p k n", n=512)
        vstage = [big[:, 17 * 512 + i * D:17 * 512 + (i + 1) * D] for i in range(4)]
        o_tm = big[:, 0:4 * D].rearrange("p (t n) -> p t n", n=D)
        qT = big[:, 17 * 512:33 * 512].rearrange("p (k n) -> p k n", n=512)
        r2f = big[:, 34 * 512:44 * 512].bitcast(F32)
        tmp0 = r2f[:, 0:1024].rearrange("p (t n) -> p t n", n=256)
        ein = [r2f[:, 1024 + i * 512:1024 + (i + 1) * 512] for i in range(3)]
        B_ein = [Buf("ein%d" % i) for i in range(3)]
        B_tmp0 = Buf("tmp0")

        tabs = tabs_t[:]
        B_tabs = Buf("tabs")
        pbs_row = tabs[:, 0:D]
        psc_row = tabs[:, D:2 * D]
        g0row = tabs[:, 2 * D:3 * D]
        qrow = tabs[:, 0:8 * 512].rearrange("p (h n) -> p h n", n=512)
        dm = tabs[:, 8 * 512:12 * 512].rearrange("p (k n) -> p k n", n=512)

        misc = misc_t[:]
        miscf = misc.bitcast(F32)
        kf = [miscf[:, i * 512:(i + 1) * 512] for i in range(2)]
        kout = [miscf[:, 1024 + i * 512:1024 + (i + 1) * 512] for i in range(2)] + [miscf[:, 3584:4096]]
        vf = [miscf[:, 2048 + i * 512:2048 + (i + 1) * 512] for i in range(2)]
        kb16 = [misc[:, 6144 + i * 512:6144 + (i + 1) * 512] for i in range(2)]
        B_kf = [Buf("kf%d" % i) for i in range(2)]
        B_kout = [Buf("kout%d" % i) for i in range(3)]
        B_vf = [Buf("vf%d" % i) for i in range(2)]
        B_kb16 = [Buf("kb16_%d" % i) for i in range(2)]
        ktr = [misc[:, 2048 + i * 512:2048 + (i + 1) * 512] for i in range(4)]
        vtr = [misc[:, 4096 + i * 1056:4096 + (i + 1) * 1056].rearrange("p (k n) -> p k n", n=264) for i in range(3)]
        ptr = [ptr_t[:, i * 512:(i + 1) * 512] for i in range(3)]
        B_ktr = [Buf("ktr%d" % i) for i in range(4)]
        B_vtr = [Buf("vtr%d" % i) for i in range(3)]
        B_ptr = [Buf("ptr%d" % i) for i in range(3)]
        qstage = [(miscf[:, 1024:1536], [B_ktr[0], B_ktr[1]]), (miscf[:, 1536:2048], [B_ktr[2], B_ktr[3]]), (ein[2], [B_ein[2]])]

        B_xb, B_hT, B_sq = Buf("xb"), Buf("hT"), Buf("sq")
        B_utm = [Buf("utm0"), Buf("utm1")]
        B_w16 = [Buf("w16_%d" % i) for i in range(4)]
        B_w4 = [Buf("w4_%d" % i) for i in range(4)]
        B_stat = [Buf("stat%d" % i) for i in range(4)]
        B_const = Buf("const")
        B_pcorr = [Buf("pcorr0"), Buf("pcorr1")]
        B_kcol = Buf("kcol")
        B_rs = Buf("rs")
        B_bank = [Buf("bank%d" % i) for i in range(8)]
        bank = [ps_t[:, i, :] for i in range(8)]
        bankb = [ps_t[:, i, :].bitcast(BF16) for i in range(8)]
        B_out = Buf("outsem")

        cnt = {"stat": 0, "utm": 0, "bank": 0, "kf": 0, "kout": 0, "vf": 0, "kb": 0, "vst": 0, "ein": 0, "pt": 0, "sb": 0, "tb": 0, "qf": 0}

        def rot(name, n):
            v = cnt[name]
            cnt[name] = (v + 1) % n
            return v

        def dma(eng, out_ap, in_ap, reads, writes, sem):
            S.op(eng, lambda e: e.dma_start(out=out_ap, in_=in_ap), reads=reads, writes=writes, dma_sem=sem)

        def load_const(dst_ap, src_ap, buf):
            dma("sp", dst_ap, src_ap, [], [buf], buf)

        def conv(dst, src, rows, step, buf):
            for r0 in range(0, rows, step):
                r1 = min(rows, r0 + step)
                dma("pool", dst[r0:r1, :], src[r0:r1, :], [], [buf], buf)

        conv(pw_b, pool_w, 2048, 2048, B_pw)
        def conv_gu(layer):
            for g, (c0, c1) in enumerate(GRP):
                for (dst, src, bufs) in ((wg_b, w_gate, B_wg), (wu_b, w_up, B_wu)):
                    for r0 in range(0, D, 1024):
                        dma("pool", dst[layer][r0:r0 + 1024, c0:c1], src[layer][r0:r0 + 1024, c0:c1], [], [bufs[layer][g]], bufs[layer][g])

        conv_gu(0)
        conv(wd_b[0], w_down[0], DFF, 512, B_wd[0])
        conv(wqkv_b, w_qkv, D, 256, B_wqkv)
        def conv_late():
            for s_ in range(NS):
                for cb in range(NCB):
                    blk = NBLK + s_ * NCB + cb
                    dma("pool", vS[blk], cv[s_, cb * 512:(cb + 1) * 512, :], [], [B_vs[blk]], B_vs[blk])
            conv(wo_b, w_o, D, 512, B_wo)
            conv_gu(1)
            conv(wd_b[1], w_down[1], DFF, 512, B_wd[1])

        B_c = {n: Buf(n) for n in ["identf", "identb", "gT", "qg", "kg", "sub", "lam", "kpos", "qref", "omask",
                                   "slopes", "kcols", "dms", "qrows", "kdiff"]}
        ident_f = sq_t[:, 0:128]
        lam_t = sq_t[:, 128:640].rearrange("p (j n) -> p j n", n=128)
        load_const(ident_f, ident_d, B_sq)
        load_const(gT[:], gT_d, B_c["gT"])
        load_const(qg_row[:], qn_d.partition_broadcast(128), B_c["qg"])
        load_const(kg_row[:], kn_d.partition_broadcast(128), B_c["kg"])
        load_const(sub_row[:], sub_d.partition_broadcast(128), B_c["sub"])
        for j in range(4):
            load_const(lam_t[:, j, :], lam_d[j:j + 1, :].partition_broadcast(128), B_sq)
        load_const(kpos_t[:], kpos_d, B_c["kpos"])
        load_const(qref_t[:], qref_d, B_c["qref"])
        load_const(omask_t[:], omask_d, B_c["omask"])
        load_const(slopes_t[:], slopes_d.partition_broadcast(128), B_c["slopes"])
        load_const(kcols_t[:].rearrange("p k h -> p (k h)"), kcols_d, B_c["kcols"])
        load_const(dms_t[:], dms_d, B_c["dms"])
        load_const(qrows_t[:].rearrange("p h n -> p (h n)"), qrows_d.partition_broadcast(128), B_c["qrows"])

        S.op("dve", lambda e: e.tensor_copy(out=ident_b[:], in_=ident_f), [B_sq], [B_c["identb"]])
        S.op("pool", lambda e: e.memset(consts[:, 0:1], -0.5), [], [B_const])
        S.op("pool", lambda e: e.memset(big, 0.0), [], [R0, R1, R2])
        for i_ in range(4):
            S.op("pool", lambda e, i_=i_: e.memset(stat_t[i_][:], 1.0), [], [B_stat[i_]])
        S.op("dve", lambda e: e.tensor_scalar(out=sub_row[:], in0=sub_row[:], scalar1=1.0 - LAM_INIT, scalar2=None,
                                              op0=ALU.mult), [B_c["sub"]], [B_c["sub"]])
        S.op("dve", lambda e: e.tensor_tensor(out=lam_t[:, 0, :], in0=lam_t[:, 0, :], in1=lam_t[:, 1, :], op=ALU.mult),
             [B_sq], [B_sq])
        S.op("dve", lambda e: e.tensor_tensor(out=lam_t[:, 2, :], in0=lam_t[:, 2, :], in1=lam_t[:, 3, :], op=ALU.mult),
             [B_sq], [B_sq])
        S.op("dve", lambda e: e.reduce_sum(out=consts[:, 2:3], in_=lam_t[:, 0, :], axis=AX.X), [B_sq], [B_const])
        S.op("dve", lambda e: e.reduce_sum(out=consts[:, 3:4], in_=lam_t[:, 2, :], axis=AX.X), [B_sq], [B_const])
        S.op("act", lambda e: e.activation(out=consts[:, 4:6], in_=consts[:, 2:4], func=AF.Exp), [B_const], [B_const])
        S.op("dve", lambda e: e.tensor_tensor(out=consts[:, 6:7], in0=consts[:, 5:6], in1=consts[:, 4:5], op=ALU.subtract),
             [B_const], [B_const])
        S.op("dve", lambda e: e.tensor_scalar(out=consts[:, 1:2], in0=consts[:, 6:7], scalar1=-LAM_INIT, scalar2=None,
                                              op0=ALU.add), [B_const], [B_const])
        mhalf = consts[:, 0:1]
        lamneg = consts[:, 1:2]

        class Stream:
            def __init__(self, bufs, depth, slack=0):
                self.bufs = bufs
                self.depth = depth
                self.slack = slack
                self.items = []
                self.issued = 0

            def add(self, fn):
                self.items.append(fn)
                return len(self.items) - 1

            def acquire(self, i):
                lim = min(len(self.items), i + self.depth - self.slack)
                while self.issued < lim:
                    j = self.issued
                    self.items[j](j % self.depth)
                    self.issued += 1
                return i % self.depth

        w16s = Stream(B_w16, 4, slack=1)
        w4s = Stream(B_w4, 4)
        w16v = [t[:] for t in w16_t]
        w4v = [t[:] for t in w4_t]

        def rstd_of(ss_ap, st, stb, ncol, inv_n, nr=128):
            S.op("dve", lambda e: e.tensor_scalar(out=st[0:nr, 4:4 + ncol], in0=ss_ap[0:nr, :], scalar1=inv_n, scalar2=EPS,
                                                  op0=ALU.mult, op1=ALU.add), [stb], [stb])
            S.op("pool", lambda e: e.tensor_tensor(out=st[0:nr, 8:8 + ncol], in0=st[0:nr, 4:4 + ncol],
                                                   in1=mhalf[0:nr, :].to_broadcast([nr, ncol]) if ncol > 1 else mhalf[0:nr, :],
                                                   op=ALU.pow), [stb, B_const], [stb])
            return st[:, 8:8 + ncol]

        def transpose16(src, srcb, nrows, dst, dstbufs, gain_idx, evac_eng="dve"):
            for half in range(2):
                bi = half
                def pe_fn(e, half=half, bi=bi):
                    ins = None
                    for j in range(8):
                        k = half * 8 + j
                        ins = e.transpose(out=bankb[bi][:, j * 128:j * 128 + nrows],
                                          in_=src[0:nrows, k * 128:(k + 1) * 128],
                                          identity=ident_b[0:nrows, 0:nrows])
                    return ins
                S.op("pe", pe_fn, [srcb, B_c["identb"]], [B_bank[bi]])
                pview = bankb[bi].rearrange("p (k n) -> p k n", n=128)[:, :, 0:nrows]
                dview = dst[:, half * 8:half * 8 + 8, :]
                if gain_idx is None:
                    S.op("dve", lambda e, pview=pview, dview=dview: e.tensor_copy(out=dview, in_=pview),
                         [B_bank[bi]], dstbufs)
                else:
                    g = gT[:, gain_idx * 16 + half * 8:gain_idx * 16 + half * 8 + 8].unsqueeze(2).to_broadcast([128, 8, nrows])
                    S.op("dve", lambda e, pview=pview, dview=dview, g=g: e.tensor_tensor(out=dview, in0=pview, in1=g, op=ALU.mult),
                         [B_bank[bi], B_c["gT"]], dstbufs)

        def norm_A(x_ap, xbuf, nrows, u32_out=None):
            si = rot("stat", 4)
            st, stb = stat_t[si][:], B_stat[si]
            S.op("dve", lambda e: e.scalar_tensor_tensor(out=sq_t[0:nrows, :], in0=x_ap[0:nrows, :], scalar=1.0, in1=x_ap[0:nrows, :],
                                                          op0=ALU.mult, op1=ALU.mult, accum_out=st[0:nrows, 0:1]),
                 [xbuf], [B_sq, stb])
            rstd = rstd_of(st[:, 0:1], st, stb, 1, 1.0 / D, nrows)
            ui = rot("utm", 2)
            utm, utmb = utm_t[ui][:], B_utm[ui]
            S.op("act", lambda e: e.activation(out=utm[0:nrows, :], in_=x_ap[0:nrows, :], func=AF.Copy, scale=rstd[0:nrows, :]),
                 [xbuf, stb], [utmb])
            if u32_out is not None:
                u32_out(rstd)
            return utm, utmb

        def norm_T(x_ap, xbuf, nrows, gain_idx, dst, dstbufs, u32_out=None):
            utm, utmb = norm_A(x_ap, xbuf, nrows, u32_out)
            transpose16(utm, utmb, nrows, dst, dstbufs, gain_idx)

        def norm_multi(tiles, xbuf, gain_idx, dstbufs):
            prev = None
            for (x_ap, nrows, dst, u32) in tiles:
                cur = (norm_A(x_ap, xbuf, nrows, u32), nrows, dst)
                if prev is not None:
                    (utm, utmb), nr_, dst_ = prev
                    transpose16(utm, utmb, nr_, dst_, dstbufs, gain_idx)
                prev = cur
            if prev is not None:
                (utm, utmb), nr_, dst_ = prev
                transpose16(utm, utmb, nr_, dst_, dstbufs, gain_idx)

        def next_bank():
            return 2 + rot("bank", 6)

        def ffn_items(layer):
            gu = []
            for cg in range(NFC // 2):
                pair = []
                for gi, (wsrc, wbuf) in enumerate(((wg_b, B_wg), (wu_b, B_wu))):
                    def ld(slot, cg=cg, wsrc=wsrc, wbuf=wbuf):
                        dstv = w16v[slot].rearrange("p (k n) -> p k n", k=16)
                        src = wsrc[layer][:, cg * 256:(cg + 1) * 256].rearrange("(k p) n -> p k n", p=128)
                        dma("sp", dstv, src, [wbuf[layer][(cg * 256) // 1536]], [B_w16[slot]], B_w16[slot])
                    pair.append(w16s.add(ld))
                gu.append(pair)
            wd = []
            for n in range(4):
                for kg in range(22):
                    def ld(slot, n=n, kg=kg):
                        src = wd_b[layer][kg * 256:(kg + 1) * 256, n * 512:(n + 1) * 512].rearrange("(k p) n -> p k n", p=128)
                        dma("sp", w4v[slot], src, [B_wd[layer]], [B_w4[slot]], B_w4[slot])
                    wd.append(w4s.add(ld))
            return gu, wd

        def ffn(layer, NT, gain_idx, items):
            gu, wd = items
            ncol = NT * 128
            norm_multi([(xb[:, t, :], 128, hT[:, :, t * 128:(t + 1) * 128], None) for t in range(NT)], B_xb, gain_idx, [B_hT])
            for cg in range(NFC // 2):
                slots = [w16s.acquire(gu[cg][0]), w16s.acquire(gu[cg][1])]
                wvs = [w16v[sl].rearrange("p (k n) -> p k n", k=16) for sl in slots]
                for cc in range(2):
                    c = cg * 2 + cc
                    pr = (c % 2) * 2 + 2
                    for gi in range(2):
                        def pe_fn(e, gi=gi, cc=cc, pr=pr, wv=wvs[gi]):
                            ins = None
                            for k in range(16):
                                ins = e.matmul(out=bank[pr + gi][:, 0:ncol], lhsT=wv[:, k, cc * 128:(cc + 1) * 128],
                                               rhs=hT[:, k, 0:ncol], start=(k == 0), stop=(k == 15))
                            return ins
                        S.op("pe", pe_fn, [B_w16[slots[gi]], B_hT], [B_bank[pr + gi]])
                    si = rot("kf", 2)
                    sg, sgb = kf[si], B_kf[si]
                    S.op("act", lambda e, pr=pr, sg=sg: e.activation(out=sg[:, 0:ncol], in_=bank[pr][:, 0:ncol], func=AF.Silu),
                         [B_bank[pr]], [sgb])
                    S.op("dve", lambda e, pr=pr, sg=sg, c=c: e.tensor_tensor(out=actT[:, c, 0:ncol], in0=sg[:, 0:ncol],
                                                                             in1=bank[pr + 1][:, 0:ncol], op=ALU.mult),
                         [sgb, B_bank[pr + 1]], [rbuf(c)])
            for n in range(4):
                for kg in range(22):
                    slot = w4s.acquire(wd[n * 22 + kg])
                    def pe_fn(e, kg=kg, slot=slot):
                        ins = None
                        for kk in range(2):
                            k = kg * 2 + kk
                            for t in range(NT):
                                ins = e.matmul(out=bank[2 + t][:, :], lhsT=actT[:, k, t * 128:(t + 1) * 128],
                                               rhs=w4v[slot][:, kk, :], start=(k == 0), stop=(k == 43))
                        return ins
                    S.op("pe", pe_fn, [B_w4[slot], R0, R1, R2], [B_bank[2 + t] for t in range(NT)])
                for t in range(NT):
                    S.op("dve", lambda e, t=t, n=n: e.tensor_tensor(out=xb[:, t, n * 512:(n + 1) * 512],
                                                                    in0=xb[:, t, n * 512:(n + 1) * 512],
                                                                    in1=bank[2 + t][:, :], op=ALU.add),
                         [B_xb, B_bank[2 + t]], [B_xb])

        def proj_items(wsrc, wbuf, col0, nblocks):
            ids = []
            for n in range(nblocks):
                pair = []
                for hf in range(2):
                    def ld(slot, n=n, hf=hf):
                        dstv = w16v[slot].rearrange("p (k n) -> p k n", n=512)
                        src = wsrc[hf * 1024:(hf + 1) * 1024, col0 + n * 512:col0 + (n + 1) * 512].rearrange("(k p) n -> p k n", p=128)
                        dma("sp", dstv, src, [wbuf], [B_w16[slot]], B_w16[slot])
                    pair.append(w16s.add(ld))
                ids.append(pair)
            return ids

        def proj(srcT, srcbufs, NT, ids, evac, rows_last=128):
            for n, pair in enumerate(ids):
                slots = [w16s.acquire(pair[0]), w16s.acquire(pair[1])]
                wvs = [w16v[sl].rearrange("p (k n) -> p k n", n=512) for sl in slots]
                for t in range(NT):
                    nr = 128 if t < NT - 1 else rows_last
                    bi = next_bank()
                    def pe_fn(e, t=t, bi=bi, wvs=wvs, nr=nr):
                        ins = None
                        for k in range(16):
                            ins = e.matmul(out=bank[bi][0:nr, :], lhsT=srcT[:, k, t * 128:t * 128 + nr], rhs=wvs[k // 8][:, k % 8, :],
                                           start=(k == 0), stop=(k == 15))
                        return ins
                    S.op("pe", pe_fn, [B_w16[slots[0]], B_w16[slots[1]]] + srcbufs, [B_bank[bi]])
                    evac(t, n, bi, nr)

        def qk_norm(tile, tb, nr, grow, growb, tbl=None):
            tbl = tbl if tbl is not None else [tb]
            si = rot("stat", 4)
            st, stb = stat_t[si][:], B_stat[si]
            S.op("dve", lambda e: e.tensor_tensor(out=sq_t[0:nr, 0:512], in0=tile[0:nr, :], in1=tile[0:nr, :], op=ALU.mult),
                 tbl, [B_sq])
            S.op("dve", lambda e: e.reduce_sum(out=st[0:nr, 0:4], in_=sq_t[0:nr, 0:512].rearrange("p (g d) -> p g d", g=4), axis=AX.X),
                 [B_sq], [stb])
            rstd = rstd_of(st[:, 0:4], st, stb, 4, 1.0 / 128, nr)
            S.op("dve", lambda e: e.tensor_tensor(out=tile[0:nr, :].rearrange("p (g d) -> p g d", g=4),
                                                  in0=tile[0:nr, :].rearrange("p (g d) -> p g d", g=4),
                                                  in1=rstd[0:nr, :].unsqueeze(2).to_broadcast([nr, 4, 128]), op=ALU.mult),
                 tbl + [stb], tbl)
            S.op("dve", lambda e: e.tensor_tensor(out=tile[0:nr, :].rearrange("p (g d) -> p g d", g=4),
                                                  in0=tile[0:nr, :].rearrange("p (g d) -> p g d", g=4),
                                                  in1=grow[0:nr, :].unsqueeze(1).to_broadcast([nr, 4, 128]), op=ALU.mult),
                 tbl + [growb], tbl)

        def transpose4(src, srcb, nr, dst, dstbufs):
            bi = rot("tb", 2)
            def pe_fn(e):
                ins = None
                for j in range(4):
                    ins = e.transpose(out=bankb[bi][:, j * 128:j * 128 + nr], in_=src[0:nr, j * 128:(j + 1) * 128],
                                      identity=ident_b[0:nr, 0:nr])
                return ins
            S.op("pe", pe_fn, [srcb, B_c["identb"]], [B_bank[bi]])
            pview = bankb[bi][:, 0:512].rearrange("p (k n) -> p k n", n=128)[:, :, 0:nr]
            S.op("act", lambda e: e.activation(out=dst, in_=pview, func=AF.Copy), [B_bank[bi]], dstbufs)

        def load_tabs_p1():
            load_const(psc_row, psc_d.partition_broadcast(128), B_tabs)
            load_const(pbs_row, pb_d.partition_broadcast(128), B_tabs)
            load_const(g0row, g0row_d.partition_broadcast(128), B_tabs)
            S.op("dve", lambda e: e.tensor_tensor(out=pbs_row, in0=pbs_row, in1=psc_row, op=ALU.mult), [B_tabs], [B_tabs])

        def pool_items():
            ids = []
            for g in range(4):
                def ld(slot, g=g):
                    dstv = w16v[slot][:, 0:2048].rearrange("p (k n) -> p k n", n=512)
                    src = pw_b[g * 512:(g + 1) * 512, :].rearrange("(k p) n -> p k n", p=128)
                    dma("sp", dstv, src, [B_pw], [B_w16[slot]], B_w16[slot])
                ids.append(w16s.add(ld))
            return ids

        def phase1_items():
            return dict(pool=pool_items(), ffn=ffn_items(0), kv=proj_items(wqkv_b, B_wqkv, D, 8))

        def p1_loads(NT, ntok, rows_last, x_src, pcorr_src):
            dma("sp", xb[:, 0:NT, :] if rows_last == 128 else xb[0:rows_last, 0:1, :],
                x_src.rearrange("(t p) d -> p t d", p=min(128, ntok)), [], [B_xb], B_xb)
            pi = rot("vst", 2)
            if pcorr_src is not None:
                dma("sp", pcorr_t[pi][:], pcorr_src.partition_broadcast(128), [], [B_pcorr[pi]], B_pcorr[pi])
            return pi

        def phase1(items, NT, segs, x_src, halo_src, halo_is_x, pcorr_src, own_idx, scr_blk, k_dst, v_dst, x1_idx,
                   ps_dst, preloaded=None, next_loads=None):
            ntok = NT * 128 if NT * 128 <= sum(s[1] for s in segs) else sum(s[1] for s in segs)
            nseg = len(segs)
            seglen = segs[0][1]
            EXT = nseg * (16 + seglen)
            rows_last = ntok - (NT - 1) * 128
            if not preloaded:
                pi = p1_loads(NT, ntok, rows_last, x_src, pcorr_src)
            else:
                pi = preloaded[0]
            for s_, (c0, _) in enumerate(segs):
                ui = rot("utm", 2)
                utm, utmb = utm_t[ui][:], B_utm[ui]
                hx = miscf[0:16, 0:2048]
                hxb = B_kf + B_kout[0:2]
                dma("sp", hx, halo_src[s_], [], hxb, B_kf[0])
                if halo_is_x:
                    si = rot("stat", 4)
                    st, stb = stat_t[si][:], B_stat[si]
                    S.op("dve", lambda e: e.tensor_tensor(out=sq_t[0:16, :], in0=hx, in1=hx, op=ALU.mult), hxb, [B_sq])
                    S.op("dve", lambda e, st=st: e.reduce_sum(out=st[0:16, 0:1], in_=sq_t[0:16, :], axis=AX.X), [B_sq], [stb])
                    rstd = rstd_of(st[:, 0:1], st, stb, 1, 1.0 / D, 16)
                    S.op("act", lambda e, utm=utm, rstd=rstd: e.activation(out=utm[0:16, :], in_=hx, func=AF.Copy,
                                                                           scale=rstd[0:16, :]), hxb + [stb], [utmb])
                    transpose16(utm, utmb, 16, uTe[:, :, c0:c0 + 16], [R0], 0)
                else:
                    S.op("act", lambda e, utm=utm: e.activation(out=utm[0:16, :], in_=hx, func=AF.Copy), hxb, [utmb])
                    transpose16(utm, utmb, 16, uTe[:, :, c0:c0 + 16], [R0], None)
            ckpt("halo")
            tiles_per_seg = max(1, seglen // 128)
            mix0_tiles = []
            for t in range(NT):
                nr = 128 if t < NT - 1 else rows_last
                if seglen >= 128:
                    s_ = t // tiles_per_seg
                    cbase = segs[s_][0] + 16 + (t % tiles_per_seg) * 128
                    dst = uTe[:, :, cbase:cbase + 128]
                else:
                    nsg = nr // seglen
                    s0 = t * (128 // seglen)
                    cb0 = segs[s0][0] + 16
                    dst = uTe[:, :, cb0:cb0 + nsg * (16 + seglen)].rearrange("p k (s n) -> p k s n", n=16 + seglen)[:, :, :, 0:seglen]
                u32 = None
                if ps_dst is not None and ps_dst(t) is not None:
                    def u32(rstd, t=t, nr=nr):
                        S.op("dve", lambda e: e.scalar_tensor_tensor(out=sq_t[0:nr, :], in0=xb[0:nr, t, :], scalar=rstd[0:nr, :],
                                                                      in1=g0row[0:nr, :], op0=ALU.mult, op1=ALU.mult),
                             [B_xb, B_tabs] + B_stat, [B_sq])
                        for (r0, dst_ap) in ps_dst(t):
                            dma("pool", dst_ap, sq_t[r0:r0 + 16, :], [B_sq], [B_out], B_sq)
                if seglen >= 128:
                    mix0_tiles.append((xb[:, t, :], nr, dst, u32))
                else:
                    norm_T_seg(xb[:, t, :], nr, dst, u32, nr // seglen, seglen)
            if mix0_tiles:
                norm_multi(mix0_tiles, B_xb, 0, [R0])
            ckpt("norm0")
            for g in range(4):
                w = 2 << g
                cur = None
                sh = 1
                flip = 0
                for lvl in range(g + 1):
                    outb = sumA if flip == 0 else sumB
                    src = uTe[:, 4 * g:4 * g + 4, :] if cur is None else cur
                    S.op("dve" if g % 2 == 0 else "pool",
                         lambda e, outb=outb, src=src, sh=sh: e.tensor_tensor(out=outb[:, :, sh:EXT], in0=src[:, :, sh:EXT],
                                                                              in1=src[:, :, 0:EXT - sh], op=ALU.add),
                         [R0, R1], [R1])
                    cur = outb
                    sh *= 2
                    flip ^= 1
                if pcorr_src is not None:
                    c0 = segs[0][0] + 16
                    S.op("dve", lambda e, cur=cur, g=g, c0=c0: e.tensor_tensor(
                        out=cur[:, :, c0:c0 + 16], in0=cur[:, :, c0:c0 + 16],
                        in1=pcorr_t[pi][:, g * 16:(g + 1) * 16].unsqueeze(1).to_broadcast([128, 4, 16]), op=ALU.mult),
                         [R1, B_pcorr[pi]], [R1])
                if nseg == 1:
                    c0 = segs[0][0] + 16
                    S.op("dve", lambda e, cur=cur, g=g, c0=c0, w=w: e.scalar_tensor_tensor(
                        out=hT[:, 4 * g:4 * g + 4, 0:ntok], in0=cur[:, :, c0:c0 + ntok], scalar=1.0 / w,
                        in1=uTe[:, 4 * g:4 * g + 4, c0:c0 + ntok], op0=ALU.mult, op1=ALU.subtract), [R0, R1], [B_hT])
                else:
                    for kk in range(4):
                        sv = cur[:, kk, 0:EXT].rearrange("p (s n) -> p s n", n=16 + seglen)[:, :, 16:16 + seglen]
                        uv = uTe[:, 4 * g + kk, 0:EXT].rearrange("p (s n) -> p s n", n=16 + seglen)[:, :, 16:16 + seglen]
                        ov = hT[:, 4 * g + kk, 0:ntok].rearrange("p (s n) -> p s n", n=seglen)
                        S.op("dve", lambda e, sv=sv, uv=uv, ov=ov, w=w: e.scalar_tensor_tensor(
                            out=ov, in0=sv, scalar=1.0 / w, in1=uv, op0=ALU.mult, op1=ALU.subtract), [R0, R1], [B_hT])
            ckpt("sums")
            for t in range(NT):
                nr = 128 if t < NT - 1 else rows_last
                S.op("pool", lambda e, t=t, nr=nr: e.tensor_tensor(out=xb[0:nr, t, :], in0=xb[0:nr, t, :], in1=pbs_row[0:nr, :], op=ALU.add),
                     [B_xb, B_tabs], [B_xb])
            for g in range(4):
                slot = w16s.acquire(items["pool"][g])
                wv = w16v[slot][:, 0:2048].rearrange("p (k n) -> p k n", n=512)
                for t in range(NT):
                    nr = 128 if t < NT - 1 else rows_last
                    bi = next_bank()
                    def pe_fn(e, t=t, bi=bi, wv=wv, nr=nr, g=g):
                        ins = None
                        for kk in range(4):
                            ins = e.matmul(out=bank[bi][0:nr, :], lhsT=hT[:, 4 * g + kk, t * 128:t * 128 + nr], rhs=wv[:, kk, :],
                                           start=(kk == 0), stop=(kk == 3))
                        return ins
                    S.op("pe", pe_fn, [B_w16[slot], B_hT], [B_bank[bi]])
                    si = rot("kf", 2)
                    tmp, tmpb = kf[si], B_kf[si]
                    S.op("dve", lambda e, bi=bi, tmp=tmp, nr=nr, g=g: e.tensor_tensor(out=tmp[0:nr, :], in0=bank[bi][0:nr, :],
                                                                                      in1=psc_row[0:nr, g * 512:(g + 1) * 512], op=ALU.mult),
                         [B_bank[bi], B_tabs], [tmpb])
                    S.op("pool", lambda e, t=t, tmp=tmp, nr=nr, g=g: e.tensor_tensor(out=xb[0:nr, t, g * 512:(g + 1) * 512],
                                                                                     in0=xb[0:nr, t, g * 512:(g + 1) * 512],
                                                                                     in1=tmp[0:nr, :], op=ALU.add),
                         [B_xb, tmpb], [B_xb])
            ckpt("poolmm")
            ffn_rows(0, NT, 1, items["ffn"], rows_last)
            ckpt("ffn0")
            if x1_idx is not None:
                dma("pool", x1S[x1_idx, 0:ntok, :].rearrange("(t p) d -> p t d", p=min(128, ntok)),
                    xb[:, 0:NT, :] if rows_last == 128 else xb[0:rows_last, 0:1, :], [B_xb], [B_x1[x1_idx]], B_xb)
            norm_multi([(xb[:, t, :], (128 if t < NT - 1 else rows_last),
                         hT[:, :, t * 128:t * 128 + (128 if t < NT - 1 else rows_last)], None) for t in range(NT)], B_xb, 2, [B_hT])

            ckpt("norm1")
            if next_loads is not None:
                next_loads()

            pend_tr = []

            def flush_tr(keep):
                while len(pend_tr) > keep:
                    a_ = pend_tr.pop(0)
                    transpose4(*a_)

            def evac(t, n, bi, nr):
                flush_tr(1 if n < 4 else 0)
                if n < 4:
                    ki = rot("kout", 3)
                    ko, kob = kout[ki], B_kout[ki]
                    S.op("act", lambda e: e.activation(out=ko[0:nr, :], in_=bank[bi][0:nr, :], func=AF.Copy), [B_bank[bi]], [kob])
                    qk_norm(ko, kob, nr, kg_row[:], B_c["kg"])
                    bi2 = rot("kb", 2)
                    kb, kbb = kb16[bi2], B_kb16[bi2]
                    S.op("act", lambda e: e.activation(out=kb[0:nr, :], in_=ko[0:nr, :], func=AF.Copy), [kob], [kbb])
                    if k_dst is not None:
                        dma("pool", k_dst[t * 128:t * 128 + nr, n * 512:(n + 1) * 512], ko[0:nr, :], [kob], [B_out], kob)
                    pend_tr.append((kb, kbb, nr, ktT[:, 4 * n:4 * n + 4, t * 128:t * 128 + nr], [R0]))
                else:
                    m = n - 4
                    vi = rot("vf", 2)
                    vv, vvb = vf[vi], B_vf[vi]
                    S.op("act", lambda e: e.activation(out=vv[0:nr, :], in_=bank[bi][0:nr, :], func=AF.Copy), [B_bank[bi]], [vvb])
                    if v_dst is not None:
                        dma("pool", v_dst[t * 128:t * 128 + nr, m * 512:(m + 1) * 512], vv[0:nr, :], [vvb], [B_out], vvb)
                    S.op("dve", lambda e: e.tensor_copy(out=vstage[t][0:nr, m * 512:(m + 1) * 512], in_=vv[0:nr, :]),
                         [vvb], [R1])

            proj(hT, [B_hT], NT, items["kv"], evac, rows_last)
            flush_tr(0)
            dma("pool", ktS[scr_blk][:, :, 0:ntok].rearrange("k p n -> p k n"), ktT[:, :, 0:ntok], [R0], [B_kt[scr_blk]], R0)
            for t in range(NT):
                nr = 128 if t < NT - 1 else rows_last
                dma("pool", vS[scr_blk, t * 128:t * 128 + nr, :], vstage[t][0:nr, :], [R1], [B_vs[scr_blk]], R1)

        def ffn_rows(layer, NT, gain_idx, items, rows_last):
            ffn(layer, NT, gain_idx, items)

        def norm_T_seg(x_ap, nr, dst4, u32, nsg, seglen):
            si = rot("stat", 4)
            st, stb = stat_t[si][:], B_stat[si]
            S.op("dve", lambda e: e.tensor_tensor(out=sq_t[0:nr, :], in0=x_ap[0:nr, :], in1=x_ap[0:nr, :], op=ALU.mult), [B_xb], [B_sq])
            S.op("dve", lambda e: e.reduce_sum(out=st[0:nr, 0:1], in_=sq_t[0:nr, :], axis=AX.X), [B_sq], [stb])
            rstd = rstd_of(st[:, 0:1], st, stb, 1, 1.0 / D, nr)
            ui = rot("utm", 2)
            utm, utmb = utm_t[ui][:], B_utm[ui]
            S.op("act", lambda e: e.activation(out=utm[0:nr, :], in_=x_ap[0:nr, :], func=AF.Copy, scale=rstd[0:nr, :]), [B_xb, stb], [utmb])
            if u32 is not None:
                u32(rstd)
            for half in range(2):
                bi = half
                def pe_fn(e, half=half, bi=bi):
                    ins = None
                    for j in range(8):
                        k = half * 8 + j
                        ins = e.transpose(out=bankb[bi][:, j * 128:j * 128 + nr], in_=utm[0:nr, k * 128:(k + 1) * 128],
                                          identity=ident_b[0:nr, 0:nr])
                    return ins
                S.op("pe", pe_fn, [utmb, B_c["identb"]], [B_bank[bi]])
                for j in range(8):
                    k = half * 8 + j
                    pview = bankb[bi][:, j * 128:j * 128 + nr].rearrange("p (s n) -> p s n", n=seglen)
                    dview = dst4[:, k, :, :]
                    S.op("dve", lambda e, pview=pview, dview=dview, k=k: e.tensor_scalar(out=dview, in0=pview, scalar1=gT[:, k:k + 1],
                                                                                       scalar2=None, op0=ALU.mult),
                         [B_bank[bi], B_c["gT"]], [R0])

        def load_tabs_p3():
            load_const(qrow.rearrange("p h n -> p (h n)"), qrow_d.partition_broadcast(128), B_tabs)
            load_const(dm.rearrange("p k n -> p (k n)"), dm_d, B_tabs)

        kts = Stream(B_ktr, 4)
        vts = Stream(B_vtr, 3)

        def init_vones():
            for i in range(3):
                S.op("pool", lambda e, i=i: e.memset(vtr[i][:, :, 256:257], 1.0), [], [B_vtr[i]])

        def attention(NQ, q_cols, ktiles, o_dst_rows, o_col_t, fast_heads=False):
            NTq = max(1, NQ // 128)
            nqr = min(128, NQ)
            blocks = []
            for kt in ktiles:
                if not blocks or blocks[-1][0] != (kt["blk"], kt.get("koff", 0)):
                    blocks.append(((kt["blk"], kt.get("koff", 0)), []))
                blocks[-1][1].append(kt)
            r0 = o_dst_rows
            ids = {}
            for h in range(NH):
                for c in range(2):
                    chunk = 2 * h + c
                    kid, vid = [], []
                    for (blk, koff), kl in blocks:
                        nkeys = sum(k_["nk"] for k_ in kl)
                        def ldk(slot, blk=blk, koff=koff, nkeys=nkeys, chunk=chunk):
                            dma("sp", ktr[slot][:, 0:nkeys], ktS[blk, chunk][:, koff:koff + nkeys], [B_kt[blk]], [B_ktr[slot]], B_ktr[slot])
                        def ldv(slot, blk=blk, koff=koff, kl=kl, h=h):
                            nkt = len(kl)
                            nk0 = kl[0]["nk"]
                            src = vS[blk, koff:koff + nkt * nk0, h * 256:(h + 1) * 256].rearrange("(k p) n -> p k n", p=nk0)
                            dma("sp", vtr[slot][0:nk0, 0:nkt, 0:256], src, [B_vs[blk]], [B_vtr[slot]], B_vtr[slot])
                        kid.append(kts.add(ldk))
                        vid.append(vts.add(ldv))
                    ids[(h, c)] = (kid, vid)
            LA = 2
            units = []
            for h in range(NH):
                for c in range(2):
                    kid, vid = ids[(h, c)]
                    nblocks = len(blocks)
                    for bi_, ((blk, koff), kl) in enumerate(blocks):
                        for j, kt in enumerate(kl):
                            units.append(dict(h=h, c=c, bi=bi_, j=j, kt=kt, kid=kid[bi_], vid=vid[bi_],
                                              first=(bi_ == 0 and j == 0),
                                              last=(bi_ == nblocks - 1 and j == len(kl) - 1)))
            U = len(units)

            def emit_front(u):
                h, c, j, kt = u["h"], u["c"], u["j"], u["kt"]
                chunk = 2 * h + c
                ks = kts.acquire(u["kid"])
                nk = kt["nk"]
                sbk = rot("sb", 4)
                def pe_s(e):
                    return e.matmul(out=bank[sbk][0:nk, 0:NQ], lhsT=ktr[ks][:, j * nk:(j + 1) * nk],
                                    rhs=qT[:, chunk, q_cols[0]:q_cols[0] + NQ], start=True, stop=True)
                S.op("pe", pe_s, [B_ktr[ks], R1], [B_bank[sbk]])
                ei = rot("ein", 3)
                eb, ebb = ein[ei], B_ein[ei]
                pi_ = rot("pt", 3)
                pb_, pbb = ptr[pi_], B_ptr[pi_]
                u["pb"] = (pb_, pbb)
                fast = fast_heads and h >= 2
                if kt["kind"] == "vis" and fast:
                    kcs = kt["kcols"](h)
                    S.op("act", lambda e: e.activation(out=pb_[0:nk, 0:NQ], in_=bank[sbk][0:nk, 0:NQ], func=AF.Exp, scale=SCALE,
                                                       bias=kcs[0:nk, :]), [B_bank[sbk], B_kcol], [pbb])
                    return
                if kt["kind"] == "vis":
                    kc = kt["kcol"](h)
                    qr = kt["qrow"](h)
                    S.op("dve", lambda e: e.scalar_tensor_tensor(
                        out=eb[0:nk, 0:NQ], in0=bank[sbk][0:nk, 0:NQ], scalar=kc[0:nk, :], in1=qr[0:nk, 0:NQ],
                        op0=ALU.add, op1=ALU.add), [B_bank[sbk], B_kcol, B_tabs, B_c["qrows"], B_c["kcols"]], [ebb])
                else:
                    dmt = kt["dm"]
                    sl = kt["nslope"](h)
                    S.op("dve", lambda e: e.scalar_tensor_tensor(
                        out=eb[0:nk, 0:NQ], in0=dmt[0:nk, 0:NQ], scalar=sl, in1=bank[sbk][0:nk, 0:NQ],
                        op0=ALU.mult, op1=ALU.add), [B_bank[sbk], B_tabs, B_c["dms"]], [ebb])
                    if fast:
                        qr = kt["qrow"](h)
                        S.op("dve", lambda e: e.tensor_tensor(out=eb[0:nk, 0:NQ], in0=eb[0:nk, 0:NQ], in1=qr[0:nk, 0:NQ],
                                                              op=ALU.subtract), [ebb, B_tabs], [ebb])
                S.op("act", lambda e: e.activation(out=pb_[0:nk, 0:NQ], in_=eb[0:nk, 0:NQ], func=AF.Exp, scale=SCALE), [ebb], [pbb])

            def emit_back(u):
                h, c, j, kt = u["h"], u["c"], u["j"], u["kt"]
                nk = kt["nk"]
                pb_, pbb = u["pb"]
                vs_ = vts.acquire(u["vid"])
                first, last = u["first"], u["last"]
                def pe_av(e):
                    ins = None
                    for t in range(NTq):
                        ins = e.matmul(out=bank[4 + t][r0:r0 + nqr, 0:257], lhsT=pb_[0:nk, t * 128:t * 128 + nqr],
                                       rhs=vtr[vs_][0:nk, j, 0:257], start=first, stop=last)
                    return ins
                S.op("pe", pe_av, [pbb, B_vtr[vs_]], [B_bank[4 + t] for t in range(NTq)])
                if not last:
                    return
                for t in range(NTq):
                    S.op("dve", lambda e, t=t: e.reciprocal(out=rs_t[r0:r0 + nqr, t:t + 1], in_=bank[4 + t][r0:r0 + nqr, 256:257]),
                         [B_bank[4 + t]], [B_rs])
                    if c == 0:
                        S.op("dve", lambda e, t=t: e.tensor_scalar(out=tmp0[r0:r0 + nqr, t, :], in0=bank[4 + t][r0:r0 + nqr, 0:256],
                                                                   scalar1=rs_t[r0:r0 + nqr, t:t + 1], scalar2=None, op0=ALU.mult),
                             [B_bank[4 + t], B_rs], [B_tmp0])
                    else:
                        S.op("dve", lambda e, t=t: e.tensor_tensor(out=rs_t[r0:r0 + nqr, 4 + t:5 + t], in0=rs_t[r0:r0 + nqr, t:t + 1],
                                                                   in1=lamneg[r0:r0 + nqr, :], op=ALU.mult), [B_rs, B_const], [B_rs])
                        S.op("dve", lambda e, t=t: e.scalar_tensor_tensor(
                            out=o_tm[r0:r0 + nqr, o_col_t + t, h * 256:(h + 1) * 256], in0=bank[4 + t][r0:r0 + nqr, 0:256],
                            scalar=rs_t[r0:r0 + nqr, 4 + t:5 + t], in1=tmp0[r0:r0 + nqr, t, :], op0=ALU.mult, op1=ALU.add),
                             [B_bank[4 + t], B_rs, B_tmp0], [R0])

            for ui in range(U + LA):
                if ui < U:
                    emit_front(units[ui])
                if ui - LA >= 0:
                    emit_back(units[ui - LA])

        def subln_tile(t, nr):
            si = rot("stat", 4)
            st, stb = stat_t[si][:], B_stat[si]
            ov = o_tm[0:nr, t, :]
            S.op("dve", lambda e: e.tensor_tensor(out=sq_t[0:nr, :], in0=ov, in1=ov, op=ALU.mult), [R0], [B_sq])
            S.op("dve", lambda e: e.reduce_sum(out=st[0:nr, 0:8], in_=sq_t[0:nr, :].rearrange("p (h d) -> p h d", h=8), axis=AX.X),
                 [B_sq], [stb])
            S.op("dve", lambda e: e.tensor_scalar(out=st[0:nr, 8:16], in0=st[0:nr, 0:8], scalar1=1.0 / 256, scalar2=EPS,
                                                  op0=ALU.mult, op1=ALU.add), [stb], [stb])
            S.op("pool", lambda e: e.tensor_tensor(out=st[0:nr, 0:8], in0=st[0:nr, 8:16], in1=mhalf[0:nr, :].to_broadcast([nr, 8]),
                                                   op=ALU.pow), [stb, B_const], [stb])
            S.op("dve", lambda e: e.tensor_tensor(out=sq_t[0:nr, :].rearrange("p (h d) -> p h d", h=8),
                                                  in0=ov.rearrange("p (h d) -> p h d", h=8),
                                                  in1=st[0:nr, 0:8].unsqueeze(2).to_broadcast([nr, 8, 256]), op=ALU.mult),
                 [R0, stb], [B_sq])
            ui = rot("utm", 2)
            utm, utmb = utm_t[ui][:], B_utm[ui]
            S.op("dve", lambda e: e.tensor_tensor(out=utm[0:nr, :].rearrange("p (h d) -> p h d", h=8),
                                                  in0=sq_t[0:nr, :].rearrange("p (h d) -> p h d", h=8),
                                                  in1=sub_row[0:nr, :].unsqueeze(1).to_broadcast([nr, 8, 256]), op=ALU.mult),
                 [B_sq, B_c["sub"]], [utmb])
            return utm, utmb

        def phase3_items():
            return dict(q=proj_items(wqkv_b, B_wqkv, 0, 4), o=proj_items(wo_b, B_wo, 0, 4), ffn=ffn_items(1))

        def phase3(items, NT, ntok, x1_idx, attn_fn, y_dst):
            rows_last = ntok - (NT - 1) * 128
            dma("sp", xb[:, 0:NT, :] if rows_last == 128 else xb[0:rows_last, 0:1, :],
                x1S[x1_idx, 0:ntok, :].rearrange("(t p) d -> p t d", p=min(128, ntok)), [B_x1[x1_idx]], [B_xb], B_xb)
            norm_multi([(xb[:, t, :], (128 if t < NT - 1 else rows_last),
                         hT[:, :, t * 128:t * 128 + (128 if t < NT - 1 else rows_last)], None) for t in range(NT)], B_xb, 2, [B_hT])

            pend_q = []

            def flush_q(keep):
                while len(pend_q) > keep:
                    a_ = pend_q.pop(0)
                    transpose4(*a_)

            def evac_q(t, n, bi, nr):
                flush_q(1)
                qi = rot("qf", 3)
                qf, qfl = qstage[qi]
                qfb = qfl[0]
                S.op("act", lambda e: e.activation(out=qf[0:nr, :], in_=bank[bi][0:nr, :], func=AF.Copy), [B_bank[bi]], qfl)
                qk_norm(qf, qfb, nr, qg_row[:], B_c["qg"], qfl)
                ui = rot("utm", 2)
                qb, qbb = utm_t[ui][:, 0:512], B_utm[ui]
                S.op("act", lambda e: e.activation(out=qb[0:nr, :], in_=qf[0:nr, :], func=AF.Copy), qfl, [qbb])
                pend_q.append((qb, qbb, nr, qT[:, 4 * n:4 * n + 4, t * 128:t * 128 + nr], [R1]))
            proj(hT, [B_hT], NT, items["q"], evac_q, rows_last)
            flush_q(0)
            attn_fn()
            for t in range(NT):
                nr = 128 if t < NT - 1 else rows_last
                utm, utmb = subln_tile(t, nr)
                transpose16(utm, utmb, nr, hT[:, :, t * 128:t * 128 + nr], [B_hT], None)

            def evac_o(t, n, bi, nr):
                S.op("dve", lambda e: e.tensor_tensor(out=xb[0:nr, t, n * 512:(n + 1) * 512], in0=xb[0:nr, t, n * 512:(n + 1) * 512],
                                                      in1=bank[bi][0:nr, :], op=ALU.add), [B_xb, B_bank[bi]], [B_xb])
            proj(hT, [B_hT], NT, items["o"], evac_o, rows_last)
            ffn_rows(1, NT, 3, items["ffn"], rows_last)
            dma("pool", y_dst.rearrange("(t p) d -> p t d", p=min(128, ntok)),
                xb[:, 0:NT, :] if rows_last == 128 else xb[0:rows_last, 0:1, :], [B_xb], [B_out], B_xb)

        class _Stop(Exception):
            pass

        ck_state = {"n": 0}

        def ckpt(tag):
            ck_state["n"] += 1
            if cfg.substop is not None and ck_state["n"] >= cfg.substop:
                print("STOP at checkpoint", ck_state["n"], tag, flush=True)
                raise _Stop()

        def program():
            try:
                program_inner()
            except _Stop:
                pass

        def program_inner():
            if cfg.stop == 0:
                return
            load_tabs_p1()
            reg = {}
            pre_state = {}

            def reg_p1(key):
                reg[key] = phase1_items()

            order = [("p", lb) for lb in range(NBLK)] + [("s", 0)]
            reg_p1(order[0])
            for oi, key in enumerate(order):
                if cfg.stop is not None and oi + 1 >= cfg.stop:
                    return
                if oi + 1 < len(order):
                    reg_p1(order[oi + 1])
                if oi == 1:
                    conv_late()
                if key[0] == "p":
                    lb = key[1]
                    own = (lb % 2 == 0)
                    i = lb // 2
                    ps_fn = None
                    if lb >= NBLK - 2:
                        def ps_fn(t, lb=lb):
                            return [(112, ps_p[lb - (NBLK - 2)])] if t == 3 else None
                    nl = None
                    if lb + 1 < NBLK:
                        def nl(lb=lb):
                            pre_state["pi"] = [p1_loads(4, 512, 128, xloc[lb + 1], pcorr[lb + 1:lb + 2, :])]
                    pre = pre_state.pop("pi", None)
                    phase1(reg[key], 4, [(0, 512)], xloc[lb], [xhalo[lb]], True, pcorr[lb:lb + 1, :], own, lb,
                           k_own[i] if own else None, v_own[i] if own else None, i if own else None, ps_fn,
                           preloaded=pre, next_loads=nl)
                else:
                    for s_ in range(NS):
                        for cb in range(NCB):
                            blk = NBLK + s_ * NCB + cb
                            dma("sp", xb[:, :, :], ck[s_, cb * 512:(cb + 1) * 512, :].rearrange("(t p) d -> p t d", p=128), [], [B_xb], B_xb)
                            for t in range(4):
                                ui = rot("utm", 2)
                                utm, utmb = utm_t[ui][:], B_utm[ui]
                                S.op("act", lambda e, utm=utm, t=t: e.activation(out=utm, in_=xb[:, t, :], func=AF.Copy), [B_xb], [utmb])
                                transpose16(utm, utmb, 128, ktT[:, :, t * 128:(t + 1) * 128], [R0], None)
                            dma("pool", ktS[blk].rearrange("k p n -> p k n"), ktT, [R0], [B_kt[blk]], R0)
                    segs = [(s_ * 80, 64) for s_ in range(NS)]

                    def ps_fn_s(t):
                        out = []
                        for s_ in range(NS):
                            if (s_ * 64) // 128 == t:
                                out.append((((s_ * 64) % 128) + 48, ps_s[s_]))
                        return out
                    phase1(reg[key], NTS, segs, xs, [shalo[s_] for s_ in range(NS)], False, None, True, cfg.nscr - 1,
                           k_s, v_s, NSTEP, ps_fn_s)

            S.op("pool", lambda e: e.memset(rs_t[:, 0:8], 1.0), [], B_kout + B_vf + B_kb16 + B_kf + B_ktr + B_vtr + [B_rs])
            load_tabs_p3()
            init_vones()

            def slope_neg(h):
                return -float(2.0 ** (-(h + 1))) * SQ128

            reg3 = {}
            order3 = [("p", i) for i in range(NSTEP)] + [("s", 0)]
            reg3[order3[0]] = phase3_items()
            for oi, key in enumerate(order3):
                if oi + 1 < len(order3):
                    reg3[order3[oi + 1]] = phase3_items()
                if key[0] == "p":
                    i = key[1]
                    nkt = (2 * i + 2) * 4

                    def attn_fn(i=i, nkt=nkt):
                        S.op("pool", lambda e: e.tensor_scalar(out=kdiff_t[:, 0:nkt], in0=kpos_t[:, 0:nkt], scalar1=qref_t[:, i:i + 1],
                                                               scalar2=None, op0=ALU.subtract), [B_c["kpos"], B_c["qref"]], [B_c["kdiff"]])
                        S.op("pool", lambda e: e.tensor_tensor(out=kcol_t[:, 0:nkt, :],
                                                               in0=kdiff_t[:, 0:nkt].unsqueeze(2).to_broadcast([128, nkt, 8]),
                                                               in1=slopes_t[:].unsqueeze(1).to_broadcast([128, nkt, 8]), op=ALU.mult),
                             [B_c["kdiff"], B_c["slopes"]], [B_kcol])
                        S.op("pool", lambda e: e.tensor_scalar(out=kcol_t[:, nkt - 4:nkt, :], in0=kcol_t[:, nkt - 4:nkt, :],
                                                               scalar1=omask_t[:, i:i + 1], scalar2=None, op0=ALU.add),
                             [B_kcol, B_c["omask"]], [B_kcol])
                        S.op("pool", lambda e: e.tensor_scalar(out=kcolS_t[:, 0:nkt, :], in0=kcol_t[:, 0:nkt, :], scalar1=SCALE,
                                                               scalar2=None, op0=ALU.mult), [B_kcol], [B_kcol])
                        ktiles = []
                        for lb in range(2 * i + 2):
                            for kt in range(4):
                                g = lb * 4 + kt
                                if lb == 2 * i:
                                    ktiles.append(dict(blk=lb, kt=kt, nk=128, kind="diag", dm=dm[:, kt, :], nslope=slope_neg))
                                else:
                                    ktiles.append(dict(blk=lb, kt=kt, nk=128, kind="vis",
                                                       kcol=(lambda h, g=g: kcol_t[:, g, h:h + 1]),
                                                       kcols=(lambda h, g=g: kcolS_t[:, g, h:h + 1]),
                                                       qrow=(lambda h: qrow[:, h, :])))
                        for kt_ in ktiles:
                            kt_.setdefault("qrow", (lambda h: qrow[:, h, :]))
                        attention(512, (0, 512), ktiles, 0, 0, fast_heads=True)
                    phase3(reg3[key], 4, 512, i, attn_fn, y_own[i])
                else:
                    def attn_fn_s():
                        for s_ in range(NS):
                            ktiles = []
                            for cb in range(NCB):
                                for kt in range(4):
                                    g = cb * 4 + kt
                                    ktiles.append(dict(blk=NBLK + s_ * NCB + cb, kt=kt, nk=128, kind="vis",
                                                       kcol=(lambda h, g=g: kcols_t[:, g, h:h + 1]),
                                                       qrow=(lambda h: qrows_t[:, h, :])))
                            ktiles.append(dict(blk=cfg.nscr - 1, koff=s_ * 64, kt=0, nk=64, kind="diag", dm=dms_t[:, :], nslope=slope_neg))
                            attention(64, (s_ * 64, 64), ktiles, (s_ * 64) % 128, (s_ * 64) // 128)
                    phase3(reg3[key], NTS, NSTOK, NSTEP, attn_fn_s, y_s)


        program()

        S.finalize()
        engsem = {}
        for e_ in ("pe", "act", "dve", "pool"):
            engsem[e_] = es.enter_context(nc.semaphore("sem_" + e_))
        for b_ in S.sembufs:
            b_.sem = es.enter_context(nc.semaphore("d_" + b_.name))
        block = es.enter_context(nc.Block())

        def run(name, e):
            for o in S.q[name]:
                for (sm, v) in o.waits:
                    e.wait_ge(engsem[sm] if isinstance(sm, str) else sm.sem, v)
                ins = o.fn(e)
                if o.isdma:
                    ins.then_inc(o.sembuf.sem, 16)
                elif o.inc:
                    ins.then_inc(engsem[name], 1)
            if name == "sp":
                for b_ in S.sembufs:
                    e.wait_ge(b_.sem, 16 * b_.dcnt)

        @block.sync
        def _(e):
            run("sp", e)

        @block.tensor
        def _(e):
            run("pe", e)

        @block.scalar
        def _(e):
            run("act", e)

        @block.vector
        def _(e):
            run("dve", e)

        @block.gpsimd
        def _(e):
            run("pool", e)
    return nc


def _zigzag(nstep, role):
    own, other = [], []
    for i in range(nstep):
        a, b = 2 * i, 2 * i + 1
        first_owner = 0 if i % 2 == 0 else 1
        if role == first_owner:
            own.append(a); other.append(b)
        else:
            own.append(b); other.append(a)
    return own, other


def _const_tables(cfg):
    slopes = np.array([2.0 ** (-(h + 1)) for h in range(8)], np.float64) * SQ128
    p = np.arange(128)
    j = np.arange(512)
    dm = np.zeros((128, 4, 512), np.float32)
    for kt in range(4):
        k = kt * 128 + p[:, None]
        vis = (k // 64) <= (j[None, :] // 64)
        dm[:, kt, :] = np.where(vis, np.abs(j[None, :] - k), BIG)
    qrow = (-slopes[:, None] * (j[None, :] - 256)).astype(np.float32).reshape(1, 8 * 512)
    ref = cfg.past + 32
    kcols = np.zeros((128, cfg.ncb * 4, 8), np.float32)
    for g in range(cfg.ncb * 4):
        kcols[:, g, :] = ((g * 128 + p[:, None]) - ref) * slopes[None, :]
    jq = np.arange(64)
    qrows = (-slopes[:, None] * (cfg.past + jq[None, :] - ref)).astype(np.float32).reshape(1, 8 * 64)
    dms = np.zeros((128, 64), np.float32)
    dms[0:64, :] = np.abs(jq[None, :] - np.arange(64)[:, None])
    return dict(dm=dm.reshape(128, 4 * 512), qrow=qrow, slopes=slopes.astype(np.float32).reshape(1, 8),
                kcols=kcols.reshape(128, -1), qrows=qrows, dms=dms, ident=np.eye(128, dtype=np.float32))


def _prepare(inputs, cfg):
    f = lambda a: np.ascontiguousarray(np.asarray(a, dtype=np.float32))
    xp, xsmp, sp_ = f(inputs["x_prompt"]), f(inputs["x_sample"]), f(inputs["state_pool"])
    ck, cv = f(inputs["cache_k"]), f(inputs["cache_v"])
    NBLK, NSTEP, NS = cfg.nblk, cfg.nstep, cfg.ns
    consts = _const_tables(cfg)
    nm, nf = f(inputs["norm_mix"]), f(inputs["norm_ffn"])
    gains = [nm[0], nf[0], nm[1], nf[1]]
    gT = np.concatenate([g.reshape(16, 128).T for g in gains], axis=1)
    shared = dict(consts)
    shared.update(
        gT=np.ascontiguousarray(gT), g0row=nm[0].reshape(1, D), pool_b=f(inputs["pool_b"]).reshape(1, D),
        pool_scale=f(inputs["pool_scale"]).reshape(1, D), q_norm=f(inputs["q_norm"]).reshape(1, 128),
        k_norm=f(inputs["k_norm"]).reshape(1, 128), subln=f(inputs["subln"]).reshape(1, 256),
        lams=np.stack([f(inputs["lambda_q1"]), f(inputs["lambda_k1"]), f(inputs["lambda_q2"]), f(inputs["lambda_k2"])]),
        pool_w=f(inputs["pool_w"]).reshape(4 * 512, 512), w_qkv=f(inputs["w_qkv"]), w_o=f(inputs["w_o"]),
        w_gate=f(inputs["w_gate"]), w_up=f(inputs["w_up"]), w_down=f(inputs["w_down"]))
    in_maps, meta = [], []
    for c in range(cfg.ncores):
        b, role = c // 2, c % 2
        own, other = _zigzag(NSTEP, role)
        L = []
        for i in range(NSTEP):
            L += [own[i], other[i]]
        xloc = np.stack([xp[b, g * 512:(g + 1) * 512] for g in L])
        xhalo = np.zeros((NBLK, 16, D), np.float32)
        pcorr = np.ones((NBLK, 4, 16), np.float32)
        for li, g in enumerate(L):
            if g > 0:
                xhalo[li] = xp[b, g * 512 - 16:g * 512]
            else:
                t = np.arange(16)
                for gi, w in enumerate((2, 4, 8, 16)):
                    pcorr[li, gi] = w / np.minimum(w, t + 1)
        kpos = np.zeros((128, NBLK * 4), np.float32)
        for li, g in enumerate(L):
            for kt in range(4):
                kpos[:, li * 4 + kt] = g * 512 + kt * 128 + np.arange(128)
        qref = np.tile(np.array([g * 512 + 256 for g in own], np.float32)[None, :], (128, 1))
        omask = np.tile(np.array([0.0 if other[i] < own[i] else -BIG for i in range(NSTEP)], np.float32)[None, :], (128, 1))
        shalo = np.zeros((NS, 16, D), np.float32)
        shalo[:, 1:, :] = sp_[c * NS:(c + 1) * NS]
        m = dict(shared)
        m.update(xloc=xloc, xhalo=xhalo, pcorr=pcorr.reshape(NBLK, 64), xs=xsmp[c * NS:(c + 1) * NS].reshape(NS * 64, D),
                 shalo=shalo, ck=ck[c * NS:(c + 1) * NS].reshape(NS, cfg.past, D), cv=cv[c * NS:(c + 1) * NS].reshape(NS, cfg.past, D),
                 kpos=kpos, qref=np.ascontiguousarray(qref), omask=np.ascontiguousarray(omask))
        in_maps.append(m)
        meta.append((b, role, own, L))
    return in_maps, meta


def _assemble(results, meta, cfg, nb_prompt, nb_sample):
    NS, NSTEP = cfg.ns, cfg.nstep
    T = cfg.nblk * 512
    y_p = np.zeros((nb_prompt, T, D), np.float32)
    k_p = np.zeros((nb_prompt, T, D), np.float32)
    v_p = np.zeros((nb_prompt, T, D), np.float32)
    ps_p = np.zeros((nb_prompt, 15, D), np.float32)
    y_s = np.zeros((nb_sample, 64, D), np.float32)
    k_s = np.zeros((nb_sample, 64, D), np.float32)
    v_s = np.zeros((nb_sample, 64, D), np.float32)
    ps_s = np.zeros((nb_sample, 15, D), np.float32)
    for c, r in enumerate(results):
        b, role, own, L = meta[c]
        for i, g in enumerate(own):
            y_p[b, g * 512:(g + 1) * 512] = r["y_own"][i]
            k_p[b, g * 512:(g + 1) * 512] = r["k_own"][i]
            v_p[b, g * 512:(g + 1) * 512] = r["v_own"][i]
        for j in range(2):
            if L[cfg.nblk - 2 + j] == cfg.nblk - 1 and role == 0:
                ps_p[b] = r["ps_p"][j][1:16]
        y_s[c * NS:(c + 1) * NS] = r["y_s"].reshape(NS, 64, D)
        k_s[c * NS:(c + 1) * NS] = r["k_s"].reshape(NS, 64, D)
        v_s[c * NS:(c + 1) * NS] = r["v_s"].reshape(NS, 64, D)
        ps_s[c * NS:(c + 1) * NS] = r["ps_s"][:, 1:16, :]
    return (y_p, y_s, ps_p, ps_s, k_p.reshape(nb_prompt, T, 8, 256), v_p.reshape(nb_prompt, T, 8, 256),
            k_s.reshape(nb_sample, 64, 8, 256), v_s.reshape(nb_sample, 64, 8, 256))


def run_cfg(inputs, cfg):
    in_maps, meta = _prepare(inputs, cfg)
    nc = build(cfg)
    res = run_bass_kernel_spmd(nc, in_maps, core_ids=list(range(cfg.ncores)))
    return _assemble(res.results, meta, cfg, cfg.ncores // 2, cfg.ncores * cfg.ns)


def kernel(**inputs):
    cfg = Cfg(nblk=16, ns=4, past=1024, ncores=8)
    return run_cfg(inputs, cfg)
```

```python
import contextlib
import numpy as np
import concourse.bass as bass
import concourse.mybir as mybir
from concourse.bass_utils import run_bass_kernel_spmd

F32 = mybir.dt.float32
BF16 = mybir.dt.bfloat16
ALU = mybir.AluOpType
AF = mybir.ActivationFunctionType
AX = mybir.AxisListType

D = 2048
DFF = 5632
NKC = 16
NFC = 44
NH = 8
EPS = 1e-6
SQ128 = float(np.sqrt(128.0))
SCALE = float(128.0 ** -0.5)
LAM_INIT = float(0.8 - 0.6 * np.exp(-0.3 * 1))
BIG = 1.0e30
SAME_ENGINE_SYNC = True


class Cfg:
    def __init__(self, nblk=16, ns=4, past=1024, ncores=8):
        self.nblk = nblk
        self.nstep = nblk // 2
        self.ns = ns
        self.past = past
        self.ncores = ncores
        self.ncb = past // 512
        self.nscr = nblk + ns * self.ncb + 1
        self.stop = None
        self.substop = None


class SemSlot:
    __slots__ = ("name", "sem", "dcnt")

    def __init__(self, name):
        self.name = name
        self.sem = None
        self.dcnt = 0


class Buf:
    __slots__ = ("name", "lw", "rd", "ds")

    def __init__(self, name):
        self.name = name
        self.lw = None
        self.rd = {}
        self.ds = {}


class Op:
    __slots__ = ("eng", "fn", "deps", "isdma", "sembuf", "dval", "inc", "cnt", "waits")


ENGS = ("pe", "act", "dve", "pool", "sp")


class Sched:
    def __init__(self):
        self.q = {e: [] for e in ENGS}
        self.sembufs = []
        self.nops = 0

    def op(self, eng, fn, reads=(), writes=(), dma_sem=None):
        o = Op()
        o.eng = eng
        o.fn = fn
        o.isdma = dma_sem is not None
        o.inc = False
        o.cnt = 0
        o.sembuf = None
        o.dval = 0
        deps = []
        for b in reads:
            if b.lw is not None:
                deps.append(b.lw)
        for b in writes:
            if b.lw is not None:
                deps.append(b.lw)
            deps.extend(b.rd.values())
        key = ("d", self.nops) if o.isdma else eng
        for b in reads:
            b.rd[key] = o
        for b in writes:
            b.lw = o
            b.rd = {}
        o.deps = [d for d in set(deps) if d is not o]
        if o.isdma:
            slot = dma_sem.ds.get(eng)
            if slot is None:
                slot = SemSlot(dma_sem.name + "_" + eng)
                dma_sem.ds[eng] = slot
                self.sembufs.append(slot)
            slot.dcnt += 1
            o.sembuf = slot
            o.dval = 16 * slot.dcnt
        self.q[eng].append(o)
        self.nops += 1
        return o

    def _skip(self, o, d):
        if d.isdma:
            return False
        if d.eng == o.eng and not o.isdma:
            if d.eng == "pe":
                return True
            return not SAME_ENGINE_SYNC
        return False

    def finalize(self):
        for e in ENGS:
            for o in self.q[e]:
                for d in o.deps:
                    if not d.isdma and not self._skip(o, d):
                        d.inc = True
        for e in ENGS:
            c = 0
            for o in self.q[e]:
                if not o.isdma and o.inc:
                    c += 1
                    o.cnt = c
        for e in ENGS:
            seen = {}
            for o in self.q[e]:
                need = {}
                for d in o.deps:
                    if self._skip(o, d):
                        continue
                    if d.isdma:
                        k = ("d", id(d.sembuf))
                        v = d.dval
                        s = d.sembuf
                    else:
                        k = ("e", d.eng)
                        v = d.cnt
                        s = d.eng
                    if seen.get(k, 0) >= v:
                        continue
                    if k not in need or need[k][1] < v:
                        need[k] = (s, v)
                for k, (s, v) in need.items():
                    seen[k] = v
                o.waits = list(need.values())


def build(cfg):
    nc = bass.Bass("TRN2", target_bir_lowering=False)
    S = Sched()
    NBLK, NSTEP, NS, NCB = cfg.nblk, cfg.nstep, cfg.ns, cfg.ncb
    NSTOK = NS * 64
    NTS = NSTOK // 128
    assert NSTOK % 128 == 0 or NS == 1
    if NS == 1:
        NTS = 1
    SROWS = min(128, NSTOK)

    def din(name, shape, dt=F32):
        return nc.dram_tensor(name, list(shape), dt, kind="ExternalInput").ap()

    def dout(name, shape, dt=F32):
        return nc.dram_tensor(name, list(shape), dt, kind="ExternalOutput").ap()

    def dint(name, shape, dt):
        return nc.dram_tensor(name, list(shape), dt, kind="Internal").ap()

    xloc = din("xloc", [NBLK, 512, D])
    xhalo = din("xhalo", [NBLK, 16, D])
    pcorr = din("pcorr", [NBLK, 64])
    xs = din("xs", [NS * 64, D])
    shalo = din("shalo", [NS, 16, D])
    ck = din("ck", [NS, cfg.past, D])
    cv = din("cv", [NS, cfg.past, D])
    kpos_d = din("kpos", [128, NBLK * 4])
    qref_d = din("qref", [128, NSTEP])
    omask_d = din("omask", [128, NSTEP])
    ident_d = din("ident", [128, 128])
    dm_d = din("dm", [128, 4 * 512])
    qrow_d = din("qrow", [1, 8 * 512])
    slopes_d = din("slopes", [1, 8])
    kcols_d = din("kcols", [128, NCB * 4 * 8])
    qrows_d = din("qrows", [1, 8 * 64])
    dms_d = din("dms", [128, 64])
    gT_d = din("gT", [128, 64])
    g0row_d = din("g0row", [1, D])
    pb_d = din("pool_b", [1, D])
    psc_d = din("pool_scale", [1, D])
    qn_d = din("q_norm", [1, 128])
    kn_d = din("k_norm", [1, 128])
    sub_d = din("subln", [1, 256])
    lam_d = din("lams", [4, 128])
    pool_w = din("pool_w", [4 * 512, 512])
    w_qkv = din("w_qkv", [D, 3 * D])
    w_o = din("w_o", [D, D])
    w_gate = din("w_gate", [2, D, DFF])
    w_up = din("w_up", [2, D, DFF])
    w_down = din("w_down", [2, DFF, D])

    y_own = dout("y_own", [NSTEP, 512, D])
    y_s = dout("y_s", [NS * 64, D])
    ps_p = dout("ps_p", [2, 16, D])
    ps_s = dout("ps_s", [NS, 16, D])
    k_own = dout("k_own", [NSTEP, 512, D])
    v_own = dout("v_own", [NSTEP, 512, D])
    k_s = dout("k_s", [NS * 64, D])
    v_s = dout("v_s", [NS * 64, D])

    pw_b = dint("pw_b", [4 * 512, 512], BF16)
    wqkv_b = dint("wqkv_b", [D, 3 * D], BF16)
    wo_b = dint("wo_b", [D, D], BF16)
    wg_b = dint("wg_b", [2, D, DFF], BF16)
    wu_b = dint("wu_b", [2, D, DFF], BF16)
    wd_b = dint("wd_b", [2, DFF, D], BF16)
    ktS = dint("ktS", [cfg.nscr, 16, 128, 512], BF16)
    vS = dint("vS", [cfg.nscr, 512, D], BF16)
    x1S = dint("x1S", [NSTEP + 1, 512, D], F32)

    B_pw, B_wqkv, B_wo = Buf("pw"), Buf("wqkv"), Buf("wo")
    GRP = [(0, 1536), (1536, 3072), (3072, 4608), (4608, 5632)]
    B_wg = [[Buf("wg%d_%d" % (l, g)) for g in range(4)] for l in range(2)]
    B_wu = [[Buf("wu%d_%d" % (l, g)) for g in range(4)] for l in range(2)]
    B_wd = [Buf("wd0"), Buf("wd1")]
    B_kt = [Buf("kt%d" % i) for i in range(cfg.nscr)]
    B_vs = [Buf("vs%d" % i) for i in range(cfg.nscr)]
    B_x1 = [Buf("x1_%d" % i) for i in range(NSTEP + 1)]

    es = contextlib.ExitStack()
    with es:
        def sb(name, shape, dt):
            return es.enter_context(nc.sbuf_tensor("s_" + name, list(shape), dt))

        xb_t = sb("xb", [128, 4, D], F32)
        hT_t = sb("hT", [128, 16, 512], BF16)
        big_t = sb("big", [128, 44 * 512], BF16)
        w16_t = [sb("w16_%d" % i, [128, 4096], BF16) for i in range(4)]
        w4_t = [sb("w4_%d" % i, [128, 2, 512], BF16) for i in range(4)]
        sq_t = sb("sq", [128, D], F32)
        utm_t = [sb("utm%d" % i, [128, D], BF16) for i in range(2)]
        tabs_t = sb("tabs", [128, 3 * D], F32)
        misc_t = sb("misc", [128, 8192], BF16)
        ptr_t = sb("ptr", [128, 3 * 512], BF16)
        ident_b = sb("ident_b", [128, 128], BF16)
        gT = sb("gT", [128, 64], F32)
        stat_t = [sb("stat%d" % i, [128, 16], F32) for i in range(4)]
        consts = sb("consts", [128, 8], F32)
        qg_row = sb("qg_row", [128, 128], F32)
        kg_row = sb("kg_row", [128, 128], F32)
        sub_row = sb("sub_row", [128, 256], F32)
        kpos_t = sb("kpos_t", [128, NBLK * 4], F32)
        qref_t = sb("qref_t", [128, NSTEP], F32)
        omask_t = sb("omask_t", [128, NSTEP], F32)
        slopes_t = sb("slopes_t", [128, 8], F32)
        kdiff_t = sb("kdiff_t", [128, NBLK * 4], F32)
        kcol_t = sb("kcol_t", [128, NBLK * 4, 8], F32)
        kcolS_t = sb("kcolS_t", [128, NBLK * 4, 8], F32)
        kcols_t = sb("kcols_t", [128, NCB * 4, 8], F32)
        dms_t = sb("dms_t", [128, 64], F32)
        qrows_t = sb("qrows_t", [128, 8, 64], F32)
        pcorr_t = [sb("pcorr%d" % i, [128, 64], F32) for i in range(2)]
        rs_t = sb("rs_t", [128, 8], F32)
        ps_t = es.enter_context(nc.psum_tensor("ps", [128, 8, 512], F32))

        xb = xb_t[:]
        hT = hT_t[:]
        big = big_t[:]

        R0, R1, R2 = Buf("R0"), Buf("R1"), Buf("R2")

        def rbuf(c):
            return R0 if c < 17 else (R1 if c < 34 else R2)

        actT = big.rearrange("p (c n) -> p c n", n=512)
        uTe = big[:, 0:16 * 528].rearrange("p (k n) -> p k n", n=528)
        sums_f = big[:, 17 * 512:34 * 512].bitcast(F32)
        sumA = sums_f[:, 0:4 * 528].rearrange("p (k n) -> p k n", n=528)
        sumB = sums_f[:, 4 * 528:8 * 528].rearrange("p (k n) -> p k n", n=528)
        ktT = big[:, 0:16 * 512].rearrange("p (k n) -> p k n", n=512)
        vstage = [big[:, 17 * 512 + i * D:17 * 512 + (i + 1) * D] for i in range(4)]
        o_tm = big[:, 0:4 * D].rearrange("p (t n) -> p t n", n=D)
        qT = big[:, 17 * 512:33 * 512].rearrange("p (k n) -> p k n", n=512)
        r2f = big[:, 34 * 512:44 * 512].bitcast(F32)
        tmp0 = r2f[:, 0:1024].rearrange("p (t n) -> p t n", n=256)
        ein = [r2f[:, 1024 + i * 512:1024 + (i + 1) * 512] for i in range(3)]
        B_ein = [Buf("ein%d" % i) for i in range(3)]
        B_tmp0 = Buf("tmp0")

        tabs = tabs_t[:]
        B_tabs = Buf("tabs")
        pbs_row = tabs[:, 0:D]
        psc_row = tabs[:, D:2 * D]
        g0row = tabs[:, 2 * D:3 * D]
        qrow = tabs[:, 0:8 * 512].rearrange("p (h n) -> p h n", n=512)
        dm = tabs[:, 8 * 512:12 * 512].rearrange("p (k n) -> p k n", n=512)

        misc = misc_t[:]
        miscf = misc.bitcast(F32)
        kf = [miscf[:, i * 512:(i + 1) * 512] for i in range(2)]
        kout = [miscf[:, 1024 + i * 512:1024 + (i + 1) * 512] for i in range(2)] + [miscf[:, 3584:4096]]
        vf = [miscf[:, 2048 + i * 512:2048 + (i + 1) * 512] for i in range(2)]
        kb16 = [misc[:, 6144 + i * 512:6144 + (i + 1) * 512] for i in range(2)]
        B_kf = [Buf("kf%d" % i) for i in range(2)]
        B_kout = [Buf("kout%d" % i) for i in range(3)]
        B_vf = [Buf("vf%d" % i) for i in range(2)]
        B_kb16 = [Buf("kb16_%d" % i) for i in range(2)]
        ktr = [misc[:, 2048 + i * 512:2048 + (i + 1) * 512] for i in range(4)]
        vtr = [misc[:, 4096 + i * 1056:4096 + (i + 1) * 1056].rearrange("p (k n) -> p k n", n=264) for i in range(3)]
        ptr = [ptr_t[:, i * 512:(i + 1) * 512] for i in range(3)]
        B_ktr = [Buf("ktr%d" % i) for i in range(4)]
        B_vtr = [Buf("vtr%d" % i) for i in range(3)]
        B_ptr = [Buf("ptr%d" % i) for i in range(3)]
        qstage = [(miscf[:, 1024:1536], [B_ktr[0], B_ktr[1]]), (miscf[:, 1536:2048], [B_ktr[2], B_ktr[3]]), (ein[2], [B_ein[2]])]

        B_xb, B_hT, B_sq = Buf("xb"), Buf("hT"), Buf("sq")
        B_utm = [Buf("utm0"), Buf("utm1")]
        B_w16 = [Buf("w16_%d" % i) for i in range(4)]
        B_w4 = [Buf("w4_%d" % i) for i in range(4)]
        B_stat = [Buf("stat%d" % i) for i in range(4)]
        B_const = Buf("const")
        B_pcorr = [Buf("pcorr0"), Buf("pcorr1")]
        B_kcol = Buf("kcol")
        B_rs = Buf("rs")
        B_bank = [Buf("bank%d" % i) for i in range(8)]
        bank = [ps_t[:, i, :] for i in range(8)]
        bankb = [ps_t[:, i, :].bitcast(BF16) for i in range(8)]
        B_out = Buf("outsem")

        cnt = {"stat": 0, "utm": 0, "bank": 0, "kf": 0, "kout": 0, "vf": 0, "kb": 0, "vst": 0, "ein": 0, "pt": 0, "sb": 0, "tb": 0, "qf": 0}

        def rot(name, n):
            v = cnt[name]
            cnt[name] = (v + 1) % n
            return v

        def dma(eng, out_ap, in_ap, reads, writes, sem):
            S.op(eng, lambda e: e.dma_start(out=out_ap, in_=in_ap), reads=reads, writes=writes, dma_sem=sem)

        def load_const(dst_ap, src_ap, buf):
            dma("sp", dst_ap, src_ap, [], [buf], buf)

        def conv(dst, src, rows, step, buf):
            for r0 in range(0, rows, step):
                r1 = min(rows, r0 + step)
                dma("pool", dst[r0:r1, :], src[r0:r1, :], [], [buf], buf)

        conv(pw_b, pool_w, 2048, 2048, B_pw)
        def conv_gu(layer):
            for g, (c0, c1) in enumerate(GRP):
                for (dst, src, bufs) in ((wg_b, w_gate, B_wg), (wu_b, w_up, B_wu)):
                    for r0 in range(0, D, 1024):
                        dma("pool", dst[layer][r0:r0 + 1024, c0:c1], src[layer][r0:r0 + 1024, c0:c1], [], [bufs[layer][g]], bufs[layer][g])

        conv_gu(0)
        conv(wd_b[0], w_down[0], DFF, 512, B_wd[0])
        conv(wqkv_b, w_qkv, D, 256, B_wqkv)
        def conv_late():
            for s_ in range(NS):
                for cb in range(NCB):
                    blk = NBLK + s_ * NCB + cb
                    dma("pool", vS[blk], cv[s_, cb * 512:(cb + 1) * 512, :], [], [B_vs[blk]], B_vs[blk])
            conv(wo_b, w_o, D, 512, B_wo)
            conv_gu(1)
            conv(wd_b[1], w_down[1], DFF, 512, B_wd[1])

        B_c = {n: Buf(n) for n in ["identf", "identb", "gT", "qg", "kg", "sub", "lam", "kpos", "qref", "omask",
                                   "slopes", "kcols", "dms", "qrows", "kdiff"]}
        ident_f = sq_t[:, 0:128]
        lam_t = sq_t[:, 128:640].rearrange("p (j n) -> p j n", n=128)
        load_const(ident_f, ident_d, B_sq)
        load_const(gT[:], gT_d, B_c["gT"])
        load_const(qg_row[:], qn_d.partition_broadcast(128), B_c["qg"])
        load_const(kg_row[:], kn_d.partition_broadcast(128), B_c["kg"])
        load_const(sub_row[:], sub_d.partition_broadcast(128), B_c["sub"])
        for j in range(4):
            load_const(lam_t[:, j, :], lam_d[j:j + 1, :].partition_broadcast(128), B_sq)
        load_const(kpos_t[:], kpos_d, B_c["kpos"])
        load_const(qref_t[:], qref_d, B_c["qref"])
        load_const(omask_t[:], omask_d, B_c["omask"])
        load_const(slopes_t[:], slopes_d.partition_broadcast(128), B_c["slopes"])
        load_const(kcols_t[:].rearrange("p k h -> p (k h)"), kcols_d, B_c["kcols"])
        load_const(dms_t[:], dms_d, B_c["dms"])
        load_const(qrows_t[:].rearrange("p h n -> p (h n)"), qrows_d.partition_broadcast(128), B_c["qrows"])

        S.op("dve", lambda e: e.tensor_copy(out=ident_b[:], in_=ident_f), [B_sq], [B_c["identb"]])
        S.op("pool", lambda e: e.memset(consts[:, 0:1], -0.5), [], [B_const])
        S.op("pool", lambda e: e.memset(big, 0.0), [], [R0, R1, R2])
        for i_ in range(4):
            S.op("pool", lambda e, i_=i_: e.memset(stat_t[i_][:], 1.0), [], [B_stat[i_]])
        S.op("dve", lambda e: e.tensor_scalar(out=sub_row[:], in0=sub_row[:], scalar1=1.0 - LAM_INIT, scalar2=None,
                                              op0=ALU.mult), [B_c["sub"]], [B_c["sub"]])
        S.op("dve", lambda e: e.tensor_tensor(out=lam_t[:, 0, :], in0=lam_t[:, 0, :], in1=lam_t[:, 1, :], op=ALU.mult),
             [B_sq], [B_sq])
        S.op("dve", lambda e: e.tensor_tensor(out=lam_t[:, 2, :], in0=lam_t[:, 2, :], in1=lam_t[:, 3, :], op=ALU.mult),
             [B_sq], [B_sq])
        S.op("dve", lambda e: e.reduce_sum(out=consts[:, 2:3], in_=lam_t[:, 0, :], axis=AX.X), [B_sq], [B_const])
        S.op("dve", lambda e: e.reduce_sum(out=consts[:, 3:4], in_=lam_t[:, 2, :], axis=AX.X), [B_sq], [B_const])
        S.op("act", lambda e: e.activation(out=consts[:, 4:6], in_=consts[:, 2:4], func=AF.Exp), [B_const], [B_const])
        S.op("dve", lambda e: e.tensor_tensor(out=consts[:, 6:7], in0=consts[:, 5:6], in1=consts[:, 4:5], op=ALU.subtract),
             [B_const], [B_const])
        S.op("dve", lambda e: e.tensor_scalar(out=consts[:, 1:2], in0=consts[:, 6:7], scalar1=-LAM_INIT, scalar2=None,
                                              op0=ALU.add), [B_const], [B_const])
        mhalf = consts[:, 0:1]
        lamneg = consts[:, 1:2]

        class Stream:
            def __init__(self, bufs, depth, slack=0):
                self.bufs = bufs
                self.depth = depth
                self.slack = slack
                self.items = []
                self.issued = 0

            def add(self, fn):
                self.items.append(fn)
                return len(self.items) - 1

            def acquire(self, i):
                lim = min(len(self.items), i + self.depth - self.slack)
                while self.issued < lim:
                    j = self.issued
                    self.items[j](j % self.depth)
                    self.issued += 1
                return i % self.depth

        w16s = Stream(B_w16, 4, slack=1)
        w4s = Stream(B_w4, 4)
        w16v = [t[:] for t in w16_t]
        w4v = [t[:] for t in w4_t]

        def rstd_of(ss_ap, st, stb, ncol, inv_n, nr=128):
            S.op("dve", lambda e: e.tensor_scalar(out=st[0:nr, 4:4 + ncol], in0=ss_ap[0:nr, :], scalar1=inv_n, scalar2=EPS,
                                                  op0=ALU.mult, op1=ALU.add), [stb], [stb])
            S.op("pool", lambda e: e.tensor_tensor(out=st[0:nr, 8:8 + ncol], in0=st[0:nr, 4:4 + ncol],
                                                   in1=mhalf[0:nr, :].to_broadcast([nr, ncol]) if ncol > 1 else mhalf[0:nr, :],
                                                   op=ALU.pow), [stb, B_const], [stb])
            return st[:, 8:8 + ncol]

        def transpose16(src, srcb, nrows, dst, dstbufs, gain_idx, evac_eng="dve"):
            for half in range(2):
                bi = half
                def pe_fn(e, half=half, bi=bi):
                    ins = None
                    for j in range(8):
                        k = half * 8 + j
                        ins = e.transpose(out=bankb[bi][:, j * 128:j * 128 + nrows],
                                          in_=src[0:nrows, k * 128:(k + 1) * 128],
                                          identity=ident_b[0:nrows, 0:nrows])
                    return ins
                S.op("pe", pe_fn, [srcb, B_c["identb"]], [B_bank[bi]])
                pview = bankb[bi].rearrange("p (k n) -> p k n", n=128)[:, :, 0:nrows]
                dview = dst[:, half * 8:half * 8 + 8, :]
                if gain_idx is None:
                    S.op("dve", lambda e, pview=pview, dview=dview: e.tensor_copy(out=dview, in_=pview),
                         [B_bank[bi]], dstbufs)
                else:
                    g = gT[:, gain_idx * 16 + half * 8:gain_idx * 16 + half * 8 + 8].unsqueeze(2).to_broadcast([128, 8, nrows])
                    S.op("dve", lambda e, pview=pview, dview=dview, g=g: e.tensor_tensor(out=dview, in0=pview, in1=g, op=ALU.mult),
                         [B_bank[bi], B_c["gT"]], dstbufs)

        def norm_A(x_ap, xbuf, nrows, u32_out=None):
            si = rot("stat", 4)
            st, stb = stat_t[si][:], B_stat[si]
            S.op("dve", lambda e: e.scalar_tensor_tensor(out=sq_t[0:nrows, :], in0=x_ap[0:nrows, :], scalar=1.0, in1=x_ap[0:nrows, :],
                                                          op0=ALU.mult, op1=ALU.mult, accum_out=st[0:nrows, 0:1]),
                 [xbuf], [B_sq, stb])
            rstd = rstd_of(st[:, 0:1], st, stb, 1, 1.0 / D, nrows)
            ui = rot("utm", 2)
            utm, utmb = utm_t[ui][:], B_utm[ui]
            S.op("act", lambda e: e.activation(out=utm[0:nrows, :], in_=x_ap[0:nrows, :], func=AF.Copy, scale=rstd[0:nrows, :]),
                 [xbuf, stb], [utmb])
            if u32_out is not None:
                u32_out(rstd)
            return utm, utmb

        def norm_T(x_ap, xbuf, nrows, gain_idx, dst, dstbufs, u32_out=None):
            utm, utmb = norm_A(x_ap, xbuf, nrows, u32_out)
            transpose16(utm, utmb, nrows, dst, dstbufs, gain_idx)

        def norm_multi(tiles, xbuf, gain_idx, dstbufs):
            prev = None
            for (x_ap, nrows, dst, u32) in tiles:
                cur = (norm_A(x_ap, xbuf, nrows, u32), nrows, dst)
                if prev is not None:
                    (utm, utmb), nr_, dst_ = prev
                    transpose16(utm, utmb, nr_, dst_, dstbufs, gain_idx)
                prev = cur
            if prev is not None:
                (utm, utmb), nr_, dst_ = prev
                transpose16(utm, utmb, nr_, dst_, dstbufs, gain_idx)

        def next_bank():
            return 2 + rot("bank", 6)

        def ffn_items(layer):
            gu = []
            for cg in range(NFC // 2):
                pair = []
                for gi, (wsrc, wbuf) in enumerate(((wg_b, B_wg), (wu_b, B_wu))):
                    def ld(slot, cg=cg, wsrc=wsrc, wbuf=wbuf):
                        dstv = w16v[slot].rearrange("p (k n) -> p k n", k=16)
                        src = wsrc[layer][:, cg * 256:(cg + 1) * 256].rearrange("(k p) n -> p k n", p=128)
                        dma("sp", dstv, src, [wbuf[layer][(cg * 256) // 1536]], [B_w16[slot]], B_w16[slot])
                    pair.append(w16s.add(ld))
                gu.append(pair)
            wd = []
            for n in range(4):
                for kg in range(22):
                    def ld(slot, n=n, kg=kg):
                        src = wd_b[layer][kg * 256:(kg + 1) * 256, n * 512:(n + 1) * 512].rearrange("(k p) n -> p k n", p=128)
                        dma("sp", w4v[slot], src, [B_wd[layer]], [B_w4[slot]], B_w4[slot])
                    wd.append(w4s.add(ld))
            return gu, wd

        def ffn(layer, NT, gain_idx, items):
            gu, wd = items
            ncol = NT * 128
            norm_multi([(xb[:, t, :], 128, hT[:, :, t * 128:(t + 1) * 128], None) for t in range(NT)], B_xb, gain_idx, [B_hT])
            for cg in range(NFC // 2):
                slots = [w16s.acquire(gu[cg][0]), w16s.acquire(gu[cg][1])]
                wvs = [w16v[sl].rearrange("p (k n) -> p k n", k=16) for sl in slots]
                for cc in range(2):
                    c = cg * 2 + cc
                    pr = (c % 2) * 2 + 2
                    for gi in range(2):
                        def pe_fn(e, gi=gi, cc=cc, pr=pr, wv=wvs[gi]):
                            ins = None
                            for k in range(16):
                                ins = e.matmul(out=bank[pr + gi][:, 0:ncol], lhsT=wv[:, k, cc * 128:(cc + 1) * 128],
                                               rhs=hT[:, k, 0:ncol], start=(k == 0), stop=(k == 15))
                            return ins
                        S.op("pe", pe_fn, [B_w16[slots[gi]], B_hT], [B_bank[pr + gi]])
                    si = rot("kf", 2)
                    sg, sgb = kf[si], B_kf[si]
                    S.op("act", lambda e, pr=pr, sg=sg: e.activation(out=sg[:, 0:ncol], in_=bank[pr][:, 0:ncol], func=AF.Silu),
                         [B_bank[pr]], [sgb])
                    S.op("dve", lambda e, pr=pr, sg=sg, c=c: e.tensor_tensor(out=actT[:, c, 0:ncol], in0=sg[:, 0:ncol],
                                                                             in1=bank[pr + 1][:, 0:ncol], op=ALU.mult),
                         [sgb, B_bank[pr + 1]], [rbuf(c)])
            for n in range(4):
                for kg in range(22):
                    slot = w4s.acquire(wd[n * 22 + kg])
                    def pe_fn(e, kg=kg, slot=slot):
                        ins = None
                        for kk in range(2):
                            k = kg * 2 + kk
                            for t in range(NT):
                                ins = e.matmul(out=bank[2 + t][:, :], lhsT=actT[:, k, t * 128:(t + 1) * 128],
                                               rhs=w4v[slot][:, kk, :], start=(k == 0), stop=(k == 43))
                        return ins
                    S.op("pe", pe_fn, [B_w4[slot], R0, R1, R2], [B_bank[2 + t] for t in range(NT)])
                for t in range(NT):
                    S.op("dve", lambda e, t=t, n=n: e.tensor_tensor(out=xb[:, t, n * 512:(n + 1) * 512],
                                                                    in0=xb[:, t, n * 512:(n + 1) * 512],
                                                                    in1=bank[2 + t][:, :], op=ALU.add),
                         [B_xb, B_bank[2 + t]], [B_xb])

        def proj_items(wsrc, wbuf, col0, nblocks):
            ids = []
            for n in range(nblocks):
                pair = []
                for hf in range(2):
                    def ld(slot, n=n, hf=hf):
                        dstv = w16v[slot].rearrange("p (k n) -> p k n", n=512)
                        src = wsrc[hf * 1024:(hf + 1) * 1024, col0 + n * 512:col0 + (n + 1) * 512].rearrange("(k p) n -> p k n", p=128)
                        dma("sp", dstv, src, [wbuf], [B_w16[slot]], B_w16[slot])
                    pair.append(w16s.add(ld))
                ids.append(pair)
            return ids

        def proj(srcT, srcbufs, NT, ids, evac, rows_last=128):
            for n, pair in enumerate(ids):
                slots = [w16s.acquire(pair[0]), w16s.acquire(pair[1])]
                wvs = [w16v[sl].rearrange("p (k n) -> p k n", n=512) for sl in slots]
                for t in range(NT):
                    nr = 128 if t < NT - 1 else rows_last
                    bi = next_bank()
                    def pe_fn(e, t=t, bi=bi, wvs=wvs, nr=nr):
                        ins = None
                        for k in range(16):
                            ins = e.matmul(out=bank[bi][0:nr, :], lhsT=srcT[:, k, t * 128:t * 128 + nr], rhs=wvs[k // 8][:, k % 8, :],
                                           start=(k == 0), stop=(k == 15))
                        return ins
                    S.op("pe", pe_fn, [B_w16[slots[0]], B_w16[slots[1]]] + srcbufs, [B_bank[bi]])
                    evac(t, n, bi, nr)

        def qk_norm(tile, tb, nr, grow, growb, tbl=None):
            tbl = tbl if tbl is not None else [tb]
            si = rot("stat", 4)
            st, stb = stat_t[si][:], B_stat[si]
            S.op("dve", lambda e: e.tensor_tensor(out=sq_t[0:nr, 0:512], in0=tile[0:nr, :], in1=tile[0:nr, :], op=ALU.mult),
                 tbl, [B_sq])
            S.op("dve", lambda e: e.reduce_sum(out=st[0:nr, 0:4], in_=sq_t[0:nr, 0:512].rearrange("p (g d) -> p g d", g=4), axis=AX.X),
                 [B_sq], [stb])
            rstd = rstd_of(st[:, 0:4], st, stb, 4, 1.0 / 128, nr)
            S.op("dve", lambda e: e.tensor_tensor(out=tile[0:nr, :].rearrange("p (g d) -> p g d", g=4),
                                                  in0=tile[0:nr, :].rearrange("p (g d) -> p g d", g=4),
                                                  in1=rstd[0:nr, :].unsqueeze(2).to_broadcast([nr, 4, 128]), op=ALU.mult),
                 tbl + [stb], tbl)
            S.op("dve", lambda e: e.tensor_tensor(out=tile[0:nr, :].rearrange("p (g d) -> p g d", g=4),
                                                  in0=tile[0:nr, :].rearrange("p (g d) -> p g d", g=4),
                                                  in1=grow[0:nr, :].unsqueeze(1).to_broadcast([nr, 4, 128]), op=ALU.mult),
                 tbl + [growb], tbl)

        def transpose4(src, srcb, nr, dst, dstbufs):
            bi = rot("tb", 2)
            def pe_fn(e):
                ins = None
                for j in range(4):
                    ins = e.transpose(out=bankb[bi][:, j * 128:j * 128 + nr], in_=src[0:nr, j * 128:(j + 1) * 128],
                                      identity=ident_b[0:nr, 0:nr])
                return ins
            S.op("pe", pe_fn, [srcb, B_c["identb"]], [B_bank[bi]])
            pview = bankb[bi][:, 0:512].rearrange("p (k n) -> p k n", n=128)[:, :, 0:nr]
            S.op("act", lambda e: e.activation(out=dst, in_=pview, func=AF.Copy), [B_bank[bi]], dstbufs)

        def load_tabs_p1():
            load_const(psc_row, psc_d.partition_broadcast(128), B_tabs)
            load_const(pbs_row, pb_d.partition_broadcast(128), B_tabs)
            load_const(g0row, g0row_d.partition_broadcast(128), B_tabs)
            S.op("dve", lambda e: e.tensor_tensor(out=pbs_row, in0=pbs_row, in1=psc_row, op=ALU.mult), [B_tabs], [B_tabs])

        def pool_items():
            ids = []
            for g in range(4):
                def ld(slot, g=g):
                    dstv = w16v[slot][:, 0:2048].rearrange("p (k n) -> p k n", n=512)
                    src = pw_b[g * 512:(g + 1) * 512, :].rearrange("(k p) n -> p k n", p=128)
                    dma("sp", dstv, src, [B_pw], [B_w16[slot]], B_w16[slot])
                ids.append(w16s.add(ld))
            return ids

        def phase1_items():
            return dict(pool=pool_items(), ffn=ffn_items(0), kv=proj_items(wqkv_b, B_wqkv, D, 8))

        def p1_loads(NT, ntok, rows_last, x_src, pcorr_src):
            dma("sp", xb[:, 0:NT, :] if rows_last == 128 else xb[0:rows_last, 0:1, :],
                x_src.rearrange("(t p) d -> p t d", p=min(128, ntok)), [], [B_xb], B_xb)
            pi = rot("vst", 2)
            if pcorr_src is not None:
                dma("sp", pcorr_t[pi][:], pcorr_src.partition_broadcast(128), [], [B_pcorr[pi]], B_pcorr[pi])
            return pi

        def phase1(items, NT, segs, x_src, halo_src, halo_is_x, pcorr_src, own_idx, scr_blk, k_dst, v_dst, x1_idx,
                   ps_dst, preloaded=None, next_loads=None):
            ntok = NT * 128 if NT * 128 <= sum(s[1] for s in segs) else sum(s[1] for s in segs)
            nseg = len(segs)
            seglen = segs[0][1]
            EXT = nseg * (16 + seglen)
            rows_last = ntok - (NT - 1) * 128
            if not preloaded:
                pi = p1_loads(NT, ntok, rows_last, x_src, pcorr_src)
            else:
                pi = preloaded[0]
            for s_, (c0, _) in enumerate(segs):
                ui = rot("utm", 2)
                utm, utmb = utm_t[ui][:], B_utm[ui]
                hx = miscf[0:16, 0:2048]
                hxb = B_kf + B_kout[0:2]
                dma("sp", hx, halo_src[s_], [], hxb, B_kf[0])
                if halo_is_x:
                    si = rot("stat", 4)
                    st, stb = stat_t[si][:], B_stat[si]
                    S.op("dve", lambda e: e.tensor_tensor(out=sq_t[0:16, :], in0=hx, in1=hx, op=ALU.mult), hxb, [B_sq])
                    S.op("dve", lambda e, st=st: e.reduce_sum(out=st[0:16, 0:1], in_=sq_t[0:16, :], axis=AX.X), [B_sq], [stb])
                    rstd = rstd_of(st[:, 0:1], st, stb, 1, 1.0 / D, 16)
                    S.op("act", lambda e, utm=utm, rstd=rstd: e.activation(out=utm[0:16, :], in_=hx, func=AF.Copy,
                                                                           scale=rstd[0:16, :]), hxb + [stb], [utmb])
                    transpose16(utm, utmb, 16, uTe[:, :, c0:c0 + 16], [R0], 0)
                else:
                    S.op("act", lambda e, utm=utm: e.activation(out=utm[0:16, :], in_=hx, func=AF.Copy), hxb, [utmb])
                    transpose16(utm, utmb, 16, uTe[:, :, c0:c0 + 16], [R0], None)
            ckpt("halo")
            tiles_per_seg = max(1, seglen // 128)
            mix0_tiles = []
            for t in range(NT):
                nr = 128 if t < NT - 1 else rows_last
                if seglen >= 128:
                    s_ = t // tiles_per_seg
                    cbase = segs[s_][0] + 16 + (t % tiles_per_seg) * 128
                    dst = uTe[:, :, cbase:cbase + 128]
                else:
                    nsg = nr // seglen
                    s0 = t * (128 // seglen)
                    cb0 = segs[s0][0] + 16
                    dst = uTe[:, :, cb0:cb0 + nsg * (16 + seglen)].rearrange("p k (s n) -> p k s n", n=16 + seglen)[:, :, :, 0:seglen]
                u32 = None
                if ps_dst is not None and ps_dst(t) is not None:
                    def u32(rstd, t=t, nr=nr):
                        S.op("dve", lambda e: e.scalar_tensor_tensor(out=sq_t[0:nr, :], in0=xb[0:nr, t, :], scalar=rstd[0:nr, :],
                                                                      in1=g0row[0:nr, :], op0=ALU.mult, op1=ALU.mult),
                             [B_xb, B_tabs] + B_stat, [B_sq])
                        for (r0, dst_ap) in ps_dst(t):
                            dma("pool", dst_ap, sq_t[r0:r0 + 16, :], [B_sq], [B_out], B_sq)
                if seglen >= 128:
                    mix0_tiles.append((xb[:, t, :], nr, dst, u32))
                else:
                    norm_T_seg(xb[:, t, :], nr, dst, u32, nr // seglen, seglen)
            if mix0_tiles:
                norm_multi(mix0_tiles, B_xb, 0, [R0])
            ckpt("norm0")
            for g in range(4):
                w = 2 << g
                cur = None
                sh = 1
                flip = 0
                for lvl in range(g + 1):
                    outb = sumA if flip == 0 else sumB
                    src = uTe[:, 4 * g:4 * g + 4, :] if cur is None else cur
                    S.op("dve",
                         lambda e, outb=outb, src=src, sh=sh: e.tensor_tensor(out=outb[:, :, sh:EXT], in0=src[:, :, sh:EXT],
                                                                              in1=src[:, :, 0:EXT - sh], op=ALU.add),
                         [R0, R1], [R1])
                    cur = outb
                    sh *= 2
                    flip ^= 1
                if pcorr_src is not None:
                    c0 = segs[0][0] + 16
                    S.op("dve", lambda e, cur=cur, g=g, c0=c0: e.tensor_tensor(
                        out=cur[:, :, c0:c0 + 16], in0=cur[:, :, c0:c0 + 16],
                        in1=pcorr_t[pi][:, g * 16:(g + 1) * 16].unsqueeze(1).to_broadcast([128, 4, 16]), op=ALU.mult),
                         [R1, B_pcorr[pi]], [R1])
                if nseg == 1:
                    c0 = segs[0][0] + 16
                    S.op("dve", lambda e, cur=cur, g=g, c0=c0, w=w: e.scalar_tensor_tensor(
                        out=hT[:, 4 * g:4 * g + 4, 0:ntok], in0=cur[:, :, c0:c0 + ntok], scalar=1.0 / w,
                        in1=uTe[:, 4 * g:4 * g + 4, c0:c0 + ntok], op0=ALU.mult, op1=ALU.subtract), [R0, R1], [B_hT])
                else:
                    for kk in range(4):
                        sv = cur[:, kk, 0:EXT].rearrange("p (s n) -> p s n", n=16 + seglen)[:, :, 16:16 + seglen]
                        uv = uTe[:, 4 * g + kk, 0:EXT].rearrange("p (s n) -> p s n", n=16 + seglen)[:, :, 16:16 + seglen]
                        ov = hT[:, 4 * g + kk, 0:ntok].rearrange("p (s n) -> p s n", n=seglen)
                        S.op("dve", lambda e, sv=sv, uv=uv, ov=ov, w=w: e.scalar_tensor_tensor(
                            out=ov, in0=sv, scalar=1.0 / w, in1=uv, op0=ALU.mult, op1=ALU.subtract), [R0, R1], [B_hT])
            ckpt("sums")
            for t in range(NT):
                nr = 128 if t < NT - 1 else rows_last
                S.op("pool", lambda e, t=t, nr=nr: e.tensor_tensor(out=xb[0:nr, t, :], in0=xb[0:nr, t, :], in1=pbs_row[0:nr, :], op=ALU.add),
                     [B_xb, B_tabs], [B_xb])
            for g in range(4):
                slot = w16s.acquire(items["pool"][g])
                wv = w16v[slot][:, 0:2048].rearrange("p (k n) -> p k n", n=512)
                for t in range(NT):
                    nr = 128 if t < NT - 1 else rows_last
                    bi = next_bank()
                    def pe_fn(e, t=t, bi=bi, wv=wv, nr=nr, g=g):
                        ins = None
                        for kk in range(4):
                            ins = e.matmul(out=bank[bi][0:nr, :], lhsT=hT[:, 4 * g + kk, t * 128:t * 128 + nr], rhs=wv[:, kk, :],
                                           start=(kk == 0), stop=(kk == 3))
                        return ins
                    S.op("pe", pe_fn, [B_w16[slot], B_hT], [B_bank[bi]])
                    si = rot("kf", 2)
                    tmp, tmpb = kf[si], B_kf[si]
                    S.op("dve", lambda e, bi=bi, tmp=tmp, nr=nr, g=g: e.tensor_tensor(out=tmp[0:nr, :], in0=bank[bi][0:nr, :],
                                                                                      in1=psc_row[0:nr, g * 512:(g + 1) * 512], op=ALU.mult),
                         [B_bank[bi], B_tabs], [tmpb])
                    S.op("pool", lambda e, t=t, tmp=tmp, nr=nr, g=g: e.tensor_tensor(out=xb[0:nr, t, g * 512:(g + 1) * 512],
                                                                                     in0=xb[0:nr, t, g * 512:(g + 1) * 512],
                                                                                     in1=tmp[0:nr, :], op=ALU.add),
                         [B_xb, tmpb], [B_xb])
            ckpt("poolmm")
            ffn_rows(0, NT, 1, items["ffn"], rows_last)
            ckpt("ffn0")
            if x1_idx is not None:
                dma("pool", x1S[x1_idx, 0:ntok, :].rearrange("(t p) d -> p t d", p=min(128, ntok)),
                    xb[:, 0:NT, :] if rows_last == 128 else xb[0:rows_last, 0:1, :], [B_xb], [B_x1[x1_idx]], B_xb)
            norm_multi([(xb[:, t, :], (128 if t < NT - 1 else rows_last),
                         hT[:, :, t * 128:t * 128 + (128 if t < NT - 1 else rows_last)], None) for t in range(NT)], B_xb, 2, [B_hT])

            ckpt("norm1")
            if next_loads is not None:
                next_loads()

            pend_tr = []

            def flush_tr(keep):
                while len(pend_tr) > keep:
                    a_ = pend_tr.pop(0)
                    transpose4(*a_)

            def evac(t, n, bi, nr):
                flush_tr(1 if n < 4 else 0)
                if n < 4:
                    ki = rot("kout", 3)
                    ko, kob = kout[ki], B_kout[ki]
                    S.op("act", lambda e: e.activation(out=ko[0:nr, :], in_=bank[bi][0:nr, :], func=AF.Copy), [B_bank[bi]], [kob])
                    qk_norm(ko, kob, nr, kg_row[:], B_c["kg"])
                    bi2 = rot("kb", 2)
                    kb, kbb = kb16[bi2], B_kb16[bi2]
                    S.op("act", lambda e: e.activation(out=kb[0:nr, :], in_=ko[0:nr, :], func=AF.Copy), [kob], [kbb])
                    if k_dst is not None:
                        dma("pool", k_dst[t * 128:t * 128 + nr, n * 512:(n + 1) * 512], ko[0:nr, :], [kob], [B_out], kob)
                    pend_tr.append((kb, kbb, nr, ktT[:, 4 * n:4 * n + 4, t * 128:t * 128 + nr], [R0]))
                else:
                    m = n - 4
                    vi = rot("vf", 2)
                    vv, vvb = vf[vi], B_vf[vi]
                    S.op("act", lambda e: e.activation(out=vv[0:nr, :], in_=bank[bi][0:nr, :], func=AF.Copy), [B_bank[bi]], [vvb])
                    if v_dst is not None:
                        dma("pool", v_dst[t * 128:t * 128 + nr, m * 512:(m + 1) * 512], vv[0:nr, :], [vvb], [B_out], vvb)
                    S.op("dve", lambda e: e.tensor_copy(out=vstage[t][0:nr, m * 512:(m + 1) * 512], in_=vv[0:nr, :]),
                         [vvb], [R1])

            proj(hT, [B_hT], NT, items["kv"], evac, rows_last)
            flush_tr(0)
            dma("pool", ktS[scr_blk][:, :, 0:ntok].rearrange("k p n -> p k n"), ktT[:, :, 0:ntok], [R0], [B_kt[scr_blk]], R0)
            for t in range(NT):
                nr = 128 if t < NT - 1 else rows_last
                dma("pool", vS[scr_blk, t * 128:t * 128 + nr, :], vstage[t][0:nr, :], [R1], [B_vs[scr_blk]], R1)

        def ffn_rows(layer, NT, gain_idx, items, rows_last):
            ffn(layer, NT, gain_idx, items)

        def norm_T_seg(x_ap, nr, dst4, u32, nsg, seglen):
            si = rot("stat", 4)
            st, stb = stat_t[si][:], B_stat[si]
            S.op("dve", lambda e: e.tensor_tensor(out=sq_t[0:nr, :], in0=x_ap[0:nr, :], in1=x_ap[0:nr, :], op=ALU.mult), [B_xb], [B_sq])
            S.op("dve", lambda e: e.reduce_sum(out=st[0:nr, 0:1], in_=sq_t[0:nr, :], axis=AX.X), [B_sq], [stb])
            rstd = rstd_of(st[:, 0:1], st, stb, 1, 1.0 / D, nr)
            ui = rot("utm", 2)
            utm, utmb = utm_t[ui][:], B_utm[ui]
            S.op("act", lambda e: e.activation(out=utm[0:nr, :], in_=x_ap[0:nr, :], func=AF.Copy, scale=rstd[0:nr, :]), [B_xb, stb], [utmb])
            if u32 is not None:
                u32(rstd)
            for half in range(2):
                bi = half
                def pe_fn(e, half=half, bi=bi):
                    ins = None
                    for j in range(8):
                        k = half * 8 + j
                        ins = e.transpose(out=bankb[bi][:, j * 128:j * 128 + nr], in_=utm[0:nr, k * 128:(k + 1) * 128],
                                          identity=ident_b[0:nr, 0:nr])
                    return ins
                S.op("pe", pe_fn, [utmb, B_c["identb"]], [B_bank[bi]])
                for j in range(8):
                    k = half * 8 + j
                    pview = bankb[bi][:, j * 128:j * 128 + nr].rearrange("p (s n) -> p s n", n=seglen)
                    dview = dst4[:, k, :, :]
                    S.op("dve", lambda e, pview=pview, dview=dview, k=k: e.tensor_scalar(out=dview, in0=pview, scalar1=gT[:, k:k + 1],
                                                                                       scalar2=None, op0=ALU.mult),
                         [B_bank[bi], B_c["gT"]], [R0])

        def load_tabs_p3():
            load_const(qrow.rearrange("p h n -> p (h n)"), qrow_d.partition_broadcast(128), B_tabs)
            load_const(dm.rearrange("p k n -> p (k n)"), dm_d, B_tabs)

        kts = Stream(B_ktr, 4)
        vts = Stream(B_vtr, 3)

        def init_vones():
            for i in range(3):
                S.op("pool", lambda e, i=i: e.memset(vtr[i][:, :, 256:257], 1.0), [], [B_vtr[i]])

        def attention(NQ, q_cols, ktiles, o_dst_rows, o_col_t, fast_heads=False):
            NTq = max(1, NQ // 128)
            nqr = min(128, NQ)
            blocks = []
            for kt in ktiles:
                if not blocks or blocks[-1][0] != (kt["blk"], kt.get("koff", 0)):
                    blocks.append(((kt["blk"], kt.get("koff", 0)), []))
                blocks[-1][1].append(kt)
            r0 = o_dst_rows
            ids = {}
            for h in range(NH):
                for c in range(2):
                    chunk = 2 * h + c
                    kid, vid = [], []
                    for (blk, koff), kl in blocks:
                        nkeys = sum(k_["nk"] for k_ in kl)
                        def ldk(slot, blk=blk, koff=koff, nkeys=nkeys, chunk=chunk):
                            dma("sp", ktr[slot][:, 0:nkeys], ktS[blk, chunk][:, koff:koff + nkeys], [B_kt[blk]], [B_ktr[slot]], B_ktr[slot])
                        def ldv(slot, blk=blk, koff=koff, kl=kl, h=h):
                            nkt = len(kl)
                            nk0 = kl[0]["nk"]
                            src = vS[blk, koff:koff + nkt * nk0, h * 256:(h + 1) * 256].rearrange("(k p) n -> p k n", p=nk0)
                            dma("sp", vtr[slot][0:nk0, 0:nkt, 0:256], src, [B_vs[blk]], [B_vtr[slot]], B_vtr[slot])
                        kid.append(kts.add(ldk))
                        vid.append(vts.add(ldv))
                    ids[(h, c)] = (kid, vid)
            LA = 2
            units = []
            for h in range(NH):
                for c in range(2):
                    kid, vid = ids[(h, c)]
                    nblocks = len(blocks)
                    for bi_, ((blk, koff), kl) in enumerate(blocks):
                        for j, kt in enumerate(kl):
                            units.append(dict(h=h, c=c, bi=bi_, j=j, kt=kt, kid=kid[bi_], vid=vid[bi_],
                                              first=(bi_ == 0 and j == 0),
                                              last=(bi_ == nblocks - 1 and j == len(kl) - 1)))
            U = len(units)

            def emit_front(u):
                h, c, j, kt = u["h"], u["c"], u["j"], u["kt"]
                chunk = 2 * h + c
                ks = kts.acquire(u["kid"])
                nk = kt["nk"]
                sbk = rot("sb", 4)
                def pe_s(e):
                    return e.matmul(out=bank[sbk][0:nk, 0:NQ], lhsT=ktr[ks][:, j * nk:(j + 1) * nk],
                                    rhs=qT[:, chunk, q_cols[0]:q_cols[0] + NQ], start=True, stop=True)
                S.op("pe", pe_s, [B_ktr[ks], R1], [B_bank[sbk]])
                ei = rot("ein", 3)
                eb, ebb = ein[ei], B_ein[ei]
                pi_ = rot("pt", 3)
                pb_, pbb = ptr[pi_], B_ptr[pi_]
                u["pb"] = (pb_, pbb)
                fast = fast_heads and h >= 2
                if kt["kind"] == "vis" and fast:
                    kcs = kt["kcols"](h)
                    S.op("act", lambda e: e.activation(out=pb_[0:nk, 0:NQ], in_=bank[sbk][0:nk, 0:NQ], func=AF.Exp, scale=SCALE,
                                                       bias=kcs[0:nk, :]), [B_bank[sbk], B_kcol], [pbb])
                    return
                if kt["kind"] == "vis":
                    kc = kt["kcol"](h)
                    qr = kt["qrow"](h)
                    S.op("dve", lambda e: e.scalar_tensor_tensor(
                        out=eb[0:nk, 0:NQ], in0=bank[sbk][0:nk, 0:NQ], scalar=kc[0:nk, :], in1=qr[0:nk, 0:NQ],
                        op0=ALU.add, op1=ALU.add), [B_bank[sbk], B_kcol, B_tabs, B_c["qrows"], B_c["kcols"]], [ebb])
                else:
                    dmt = kt["dm"]
                    sl = kt["nslope"](h)
                    S.op("dve", lambda e: e.scalar_tensor_tensor(
                        out=eb[0:nk, 0:NQ], in0=dmt[0:nk, 0:NQ], scalar=sl, in1=bank[sbk][0:nk, 0:NQ],
                        op0=ALU.mult, op1=ALU.add), [B_bank[sbk], B_tabs, B_c["dms"]], [ebb])
                    if fast:
                        qr = kt["qrow"](h)
                        S.op("dve", lambda e: e.tensor_tensor(out=eb[0:nk, 0:NQ], in0=eb[0:nk, 0:NQ], in1=qr[0:nk, 0:NQ],
                                                              op=ALU.subtract), [ebb, B_tabs], [ebb])
                S.op("act", lambda e: e.activation(out=pb_[0:nk, 0:NQ], in_=eb[0:nk, 0:NQ], func=AF.Exp, scale=SCALE), [ebb], [pbb])

            def emit_back(u):
                h, c, j, kt = u["h"], u["c"], u["j"], u["kt"]
                nk = kt["nk"]
                pb_, pbb = u["pb"]
                vs_ = vts.acquire(u["vid"])
                first, last = u["first"], u["last"]
                def pe_av(e):
                    ins = None
                    for t in range(NTq):
                        ins = e.matmul(out=bank[4 + t][r0:r0 + nqr, 0:257], lhsT=pb_[0:nk, t * 128:t * 128 + nqr],
                                       rhs=vtr[vs_][0:nk, j, 0:257], start=first, stop=last)
                    return ins
                S.op("pe", pe_av, [pbb, B_vtr[vs_]], [B_bank[4 + t] for t in range(NTq)])
                if not last:
                    return
                for t in range(NTq):
                    S.op("dve", lambda e, t=t: e.reciprocal(out=rs_t[r0:r0 + nqr, t:t + 1], in_=bank[4 + t][r0:r0 + nqr, 256:257]),
                         [B_bank[4 + t]], [B_rs])
                    if c == 0:
                        S.op("dve", lambda e, t=t: e.tensor_scalar(out=tmp0[r0:r0 + nqr, t, :], in0=bank[4 + t][r0:r0 + nqr, 0:256],
                                                                   scalar1=rs_t[r0:r0 + nqr, t:t + 1], scalar2=None, op0=ALU.mult),
                             [B_bank[4 + t], B_rs], [B_tmp0])
                    else:
                        S.op("dve", lambda e, t=t: e.tensor_tensor(out=rs_t[r0:r0 + nqr, 4 + t:5 + t], in0=rs_t[r0:r0 + nqr, t:t + 1],
                                                                   in1=lamneg[r0:r0 + nqr, :], op=ALU.mult), [B_rs, B_const], [B_rs])
                        S.op("dve", lambda e, t=t: e.scalar_tensor_tensor(
                            out=o_tm[r0:r0 + nqr, o_col_t + t, h * 256:(h + 1) * 256], in0=bank[4 + t][r0:r0 + nqr, 0:256],
                            scalar=rs_t[r0:r0 + nqr, 4 + t:5 + t], in1=tmp0[r0:r0 + nqr, t, :], op0=ALU.mult, op1=ALU.add),
                             [B_bank[4 + t], B_rs, B_tmp0], [R0])

            for ui in range(U + LA):
                if ui < U:
                    emit_front(units[ui])
                if ui - LA >= 0:
                    emit_back(units[ui - LA])

        def subln_tile(t, nr):
            si = rot("stat", 4)
            st, stb = stat_t[si][:], B_stat[si]
            ov = o_tm[0:nr, t, :]
            S.op("dve", lambda e: e.tensor_tensor(out=sq_t[0:nr, :], in0=ov, in1=ov, op=ALU.mult), [R0], [B_sq])
            S.op("dve", lambda e: e.reduce_sum(out=st[0:nr, 0:8], in_=sq_t[0:nr, :].rearrange("p (h d) -> p h d", h=8), axis=AX.X),
                 [B_sq], [stb])
            S.op("dve", lambda e: e.tensor_scalar(out=st[0:nr, 8:16], in0=st[0:nr, 0:8], scalar1=1.0 / 256, scalar2=EPS,
                                                  op0=ALU.mult, op1=ALU.add), [stb], [stb])
            S.op("pool", lambda e: e.tensor_tensor(out=st[0:nr, 0:8], in0=st[0:nr, 8:16], in1=mhalf[0:nr, :].to_broadcast([nr, 8]),
                                                   op=ALU.pow), [stb, B_const], [stb])
            S.op("dve", lambda e: e.tensor_tensor(out=sq_t[0:nr, :].rearrange("p (h d) -> p h d", h=8),
                                                  in0=ov.rearrange("p (h d) -> p h d", h=8),
                                                  in1=st[0:nr, 0:8].unsqueeze(2).to_broadcast([nr, 8, 256]), op=ALU.mult),
                 [R0, stb], [B_sq])
            ui = rot("utm", 2)
            utm, utmb = utm_t[ui][:], B_utm[ui]
            S.op("dve", lambda e: e.tensor_tensor(out=utm[0:nr, :].rearrange("p (h d) -> p h d", h=8),
                                                  in0=sq_t[0:nr, :].rearrange("p (h d) -> p h d", h=8),
                                                  in1=sub_row[0:nr, :].unsqueeze(1).to_broadcast([nr, 8, 256]), op=ALU.mult),
                 [B_sq, B_c["sub"]], [utmb])
            return utm, utmb

        def phase3_items():
            return dict(q=proj_items(wqkv_b, B_wqkv, 0, 4), o=proj_items(wo_b, B_wo, 0, 4), ffn=ffn_items(1))

        def phase3(items, NT, ntok, x1_idx, attn_fn, y_dst):
            rows_last = ntok - (NT - 1) * 128
            dma("sp", xb[:, 0:NT, :] if rows_last == 128 else xb[0:rows_last, 0:1, :],
                x1S[x1_idx, 0:ntok, :].rearrange("(t p) d -> p t d", p=min(128, ntok)), [B_x1[x1_idx]], [B_xb], B_xb)
            norm_multi([(xb[:, t, :], (128 if t < NT - 1 else rows_last),
                         hT[:, :, t * 128:t * 128 + (128 if t < NT - 1 else rows_last)], None) for t in range(NT)], B_xb, 2, [B_hT])

            pend_q = []

            def flush_q(keep):
                while len(pend_q) > keep:
                    a_ = pend_q.pop(0)
                    transpose4(*a_)

            def evac_q(t, n, bi, nr):
                flush_q(1)
                qi = rot("qf", 3)
                qf, qfl = qstage[qi]
                qfb = qfl[0]
                S.op("act", lambda e: e.activation(out=qf[0:nr, :], in_=bank[bi][0:nr, :], func=AF.Copy), [B_bank[bi]], qfl)
                qk_norm(qf, qfb, nr, qg_row[:], B_c["qg"], qfl)
                ui = rot("utm", 2)
                qb, qbb = utm_t[ui][:, 0:512], B_utm[ui]
                S.op("act", lambda e: e.activation(out=qb[0:nr, :], in_=qf[0:nr, :], func=AF.Copy), qfl, [qbb])
                pend_q.append((qb, qbb, nr, qT[:, 4 * n:4 * n + 4, t * 128:t * 128 + nr], [R1]))
            proj(hT, [B_hT], NT, items["q"], evac_q, rows_last)
            flush_q(0)
            attn_fn()
            for t in range(NT):
                nr = 128 if t < NT - 1 else rows_last
                utm, utmb = subln_tile(t, nr)
                transpose16(utm, utmb, nr, hT[:, :, t * 128:t * 128 + nr], [B_hT], None)

            def evac_o(t, n, bi, nr):
                S.op("dve", lambda e: e.tensor_tensor(out=xb[0:nr, t, n * 512:(n + 1) * 512], in0=xb[0:nr, t, n * 512:(n + 1) * 512],
                                                      in1=bank[bi][0:nr, :], op=ALU.add), [B_xb, B_bank[bi]], [B_xb])
            proj(hT, [B_hT], NT, items["o"], evac_o, rows_last)
            ffn_rows(1, NT, 3, items["ffn"], rows_last)
            dma("pool", y_dst.rearrange("(t p) d -> p t d", p=min(128, ntok)),
                xb[:, 0:NT, :] if rows_last == 128 else xb[0:rows_last, 0:1, :], [B_xb], [B_out], B_xb)

        class _Stop(Exception):
            pass

        ck_state = {"n": 0}

        def ckpt(tag):
            ck_state["n"] += 1
            if cfg.substop is not None and ck_state["n"] >= cfg.substop:
                print("STOP at checkpoint", ck_state["n"], tag, flush=True)
                raise _Stop()

        def program():
            try:
                program_inner()
            except _Stop:
                pass

        def program_inner():
            if cfg.stop == 0:
                return
            load_tabs_p1()
            reg = {}
            pre_state = {}

            def reg_p1(key):
                reg[key] = phase1_items()

            order = [("p", lb) for lb in range(NBLK)] + [("s", 0)]
            reg_p1(order[0])
            for oi, key in enumerate(order):
                if cfg.stop is not None and oi + 1 >= cfg.stop:
                    return
                if oi + 1 < len(order):
                    reg_p1(order[oi + 1])
                if oi == 1:
                    conv_late()
                if key[0] == "p":
                    lb = key[1]
                    own = (lb % 2 == 0)
                    i = lb // 2
                    ps_fn = None
                    if lb >= NBLK - 2:
                        def ps_fn(t, lb=lb):
                            return [(112, ps_p[lb - (NBLK - 2)])] if t == 3 else None
                    nl = None
                    if lb + 1 < NBLK:
                        def nl(lb=lb):
                            pre_state["pi"] = [p1_loads(4, 512, 128, xloc[lb + 1], pcorr[lb + 1:lb + 2, :])]
                    pre = pre_state.pop("pi", None)
                    phase1(reg[key], 4, [(0, 512)], xloc[lb], [xhalo[lb]], True, pcorr[lb:lb + 1, :], own, lb,
                           k_own[i] if own else None, v_own[i] if own else None, i if own else None, ps_fn,
                           preloaded=pre, next_loads=nl)
                else:
                    for s_ in range(NS):
                        for cb in range(NCB):
                            blk = NBLK + s_ * NCB + cb
                            dma("sp", xb[:, :, :], ck[s_, cb * 512:(cb + 1) * 512, :].rearrange("(t p) d -> p t d", p=128), [], [B_xb], B_xb)
                            for t in range(4):
                                ui = rot("utm", 2)
                                utm, utmb = utm_t[ui][:], B_utm[ui]
                                S.op("act", lambda e, utm=utm, t=t: e.activation(out=utm, in_=xb[:, t, :], func=AF.Copy), [B_xb], [utmb])
                                transpose16(utm, utmb, 128, ktT[:, :, t * 128:(t + 1) * 128], [R0], None)
                            dma("pool", ktS[blk].rearrange("k p n -> p k n"), ktT, [R0], [B_kt[blk]], R0)
                    segs = [(s_ * 80, 64) for s_ in range(NS)]

                    def ps_fn_s(t):
                        out = []
                        for s_ in range(NS):
                            if (s_ * 64) // 128 == t:
                                out.append((((s_ * 64) % 128) + 48, ps_s[s_]))
                        return out
                    phase1(reg[key], NTS, segs, xs, [shalo[s_] for s_ in range(NS)], False, None, True, cfg.nscr - 1,
                           k_s, v_s, NSTEP, ps_fn_s)

            S.op("pool", lambda e: e.memset(rs_t[:, 0:8], 1.0), [], B_kout + B_vf + B_kb16 + B_kf + B_ktr + B_vtr + [B_rs])
            load_tabs_p3()
            init_vones()

            def slope_neg(h):
                return -float(2.0 ** (-(h + 1))) * SQ128

            reg3 = {}
            order3 = [("p", i) for i in range(NSTEP)] + [("s", 0)]
            reg3[order3[0]] = phase3_items()
            for oi, key in enumerate(order3):
                if oi + 1 < len(order3):
                    reg3[order3[oi + 1]] = phase3_items()
                if key[0] == "p":
                    i = key[1]
                    nkt = (2 * i + 2) * 4

                    def attn_fn(i=i, nkt=nkt):
                        S.op("pool", lambda e: e.tensor_scalar(out=kdiff_t[:, 0:nkt], in0=kpos_t[:, 0:nkt], scalar1=qref_t[:, i:i + 1],
                                                               scalar2=None, op0=ALU.subtract), [B_c["kpos"], B_c["qref"]], [B_c["kdiff"]])
                        S.op("pool", lambda e: e.tensor_tensor(out=kcol_t[:, 0:nkt, :],
                                                               in0=kdiff_t[:, 0:nkt].unsqueeze(2).to_broadcast([128, nkt, 8]),
                                                               in1=slopes_t[:].unsqueeze(1).to_broadcast([128, nkt, 8]), op=ALU.mult),
                             [B_c["kdiff"], B_c["slopes"]], [B_kcol])
                        S.op("pool", lambda e: e.tensor_scalar(out=kcol_t[:, nkt - 4:nkt, :], in0=kcol_t[:, nkt - 4:nkt, :],
                                                               scalar1=omask_t[:, i:i + 1], scalar2=None, op0=ALU.add),
                             [B_kcol, B_c["omask"]], [B_kcol])
                        S.op("pool", lambda e: e.tensor_scalar(out=kcolS_t[:, 0:nkt, :], in0=kcol_t[:, 0:nkt, :], scalar1=SCALE,
                                                               scalar2=None, op0=ALU.mult), [B_kcol], [B_kcol])
                        ktiles = []
                        for lb in range(2 * i + 2):
                            for kt in range(4):
                                g = lb * 4 + kt
                                if lb == 2 * i:
                                    ktiles.append(dict(blk=lb, kt=kt, nk=128, kind="diag", dm=dm[:, kt, :], nslope=slope_neg))
                                else:
                                    ktiles.append(dict(blk=lb, kt=kt, nk=128, kind="vis",
                                                       kcol=(lambda h, g=g: kcol_t[:, g, h:h + 1]),
                                                       kcols=(lambda h, g=g: kcolS_t[:, g, h:h + 1]),
                                                       qrow=(lambda h: qrow[:, h, :])))
                        for kt_ in ktiles:
                            kt_.setdefault("qrow", (lambda h: qrow[:, h, :]))
                        attention(512, (0, 512), ktiles, 0, 0, fast_heads=True)
                    phase3(reg3[key], 4, 512, i, attn_fn, y_own[i])
                else:
                    def attn_fn_s():
                        for s_ in range(NS):
                            ktiles = []
                            for cb in range(NCB):
                                for kt in range(4):
                                    g = cb * 4 + kt
                                    ktiles.append(dict(blk=NBLK + s_ * NCB + cb, kt=kt, nk=128, kind="vis",
                                                       kcol=(lambda h, g=g: kcols_t[:, g, h:h + 1]),
                                                       qrow=(lambda h: qrows_t[:, h, :])))
                            ktiles.append(dict(blk=cfg.nscr - 1, koff=s_ * 64, kt=0, nk=64, kind="diag", dm=dms_t[:, :], nslope=slope_neg))
                            attention(64, (s_ * 64, 64), ktiles, (s_ * 64) % 128, (s_ * 64) // 128)
                    phase3(reg3[key], NTS, NSTOK, NSTEP, attn_fn_s, y_s)


        program()

        S.finalize()
        engsem = {}
        for e_ in ("pe", "act", "dve", "pool"):
            engsem[e_] = es.enter_context(nc.semaphore("sem_" + e_))
        for b_ in S.sembufs:
            b_.sem = es.enter_context(nc.semaphore("d_" + b_.name))
        block = es.enter_context(nc.Block())

        def run(name, e):
            for o in S.q[name]:
                for (sm, v) in o.waits:
                    e.wait_ge(engsem[sm] if isinstance(sm, str) else sm.sem, v)
                ins = o.fn(e)
                if o.isdma:
                    ins.then_inc(o.sembuf.sem, 16)
                elif o.inc:
                    ins.then_inc(engsem[name], 1)
            if name == "sp":
                for b_ in S.sembufs:
                    e.wait_ge(b_.sem, 16 * b_.dcnt)

        @block.sync
        def _(e):
            run("sp", e)

        @block.tensor
        def _(e):
            run("pe", e)

        @block.scalar
        def _(e):
            run("act", e)

        @block.vector
        def _(e):
            run("dve", e)

        @block.gpsimd
        def _(e):
            run("pool", e)
    return nc


def _zigzag(nstep, role):
    own, other = [], []
    for i in range(nstep):
        a, b = 2 * i, 2 * i + 1
        first_owner = 0 if i % 2 == 0 else 1
        if role == first_owner:
            own.append(a); other.append(b)
        else:
            own.append(b); other.append(a)
    return own, other


def _const_tables(cfg):
    slopes = np.array([2.0 ** (-(h + 1)) for h in range(8)], np.float64) * SQ128
    p = np.arange(128)
    j = np.arange(512)
    dm = np.zeros((128, 4, 512), np.float32)
    for kt in range(4):
        k = kt * 128 + p[:, None]
        vis = (k // 64) <= (j[None, :] // 64)
        dm[:, kt, :] = np.where(vis, np.abs(j[None, :] - k), BIG)
    qrow = (-slopes[:, None] * (j[None, :] - 256)).astype(np.float32).reshape(1, 8 * 512)
    ref = cfg.past + 32
    kcols = np.zeros((128, cfg.ncb * 4, 8), np.float32)
    for g in range(cfg.ncb * 4):
        kcols[:, g, :] = ((g * 128 + p[:, None]) - ref) * slopes[None, :]
    jq = np.arange(64)
    qrows = (-slopes[:, None] * (cfg.past + jq[None, :] - ref)).astype(np.float32).reshape(1, 8 * 64)
    dms = np.zeros((128, 64), np.float32)
    dms[0:64, :] = np.abs(jq[None, :] - np.arange(64)[:, None])
    return dict(dm=dm.reshape(128, 4 * 512), qrow=qrow, slopes=slopes.astype(np.float32).reshape(1, 8),
                kcols=kcols.reshape(128, -1), qrows=qrows, dms=dms, ident=np.eye(128, dtype=np.float32))


def _prepare(inputs, cfg):
    f = lambda a: np.ascontiguousarray(np.asarray(a, dtype=np.float32))
    xp, xsmp, sp_ = f(inputs["x_prompt"]), f(inputs["x_sample"]), f(inputs["state_pool"])
    ck, cv = f(inputs["cache_k"]), f(inputs["cache_v"])
    NBLK, NSTEP, NS = cfg.nblk, cfg.nstep, cfg.ns
    consts = _const_tables(cfg)
    nm, nf = f(inputs["norm_mix"]), f(inputs["norm_ffn"])
    gains = [nm[0], nf[0], nm[1], nf[1]]
    gT = np.concatenate([g.reshape(16, 128).T for g in gains], axis=1)
    shared = dict(consts)
    shared.update(
        gT=np.ascontiguousarray(gT), g0row=nm[0].reshape(1, D), pool_b=f(inputs["pool_b"]).reshape(1, D),
        pool_scale=f(inputs["pool_scale"]).reshape(1, D), q_norm=f(inputs["q_norm"]).reshape(1, 128),
        k_norm=f(inputs["k_norm"]).reshape(1, 128), subln=f(inputs["subln"]).reshape(1, 256),
        lams=np.stack([f(inputs["lambda_q1"]), f(inputs["lambda_k1"]), f(inputs["lambda_q2"]), f(inputs["lambda_k2"])]),
        pool_w=f(inputs["pool_w"]).reshape(4 * 512, 512), w_qkv=f(inputs["w_qkv"]), w_o=f(inputs["w_o"]),
        w_gate=f(inputs["w_gate"]), w_up=f(inputs["w_up"]), w_down=f(inputs["w_down"]))
    in_maps, meta = [], []
    for c in range(cfg.ncores):
        b, role = c // 2, c % 2
        own, other = _zigzag(NSTEP, role)
        L = []
        for i in range(NSTEP):
            L += [own[i], other[i]]
        xloc = np.stack([xp[b, g * 512:(g + 1) * 512] for g in L])
        xhalo = np.zeros((NBLK, 16, D), np.float32)
        pcorr = np.ones((NBLK, 4, 16), np.float32)
        for li, g in enumerate(L):
            if g > 0:
                xhalo[li] = xp[b, g * 512 - 16:g * 512]
            else:
                t = np.arange(16)
                for gi, w in enumerate((2, 4, 8, 16)):
                    pcorr[li, gi] = w / np.minimum(w, t + 1)
        kpos = np.zeros((128, NBLK * 4), np.float32)
        for li, g in enumerate(L):
            for kt in range(4):
                kpos[:, li * 4 + kt] = g * 512 + kt * 128 + np.arange(128)
        qref = np.tile(np.array([g * 512 + 256 for g in own], np.float32)[None, :], (128, 1))
        omask = np.tile(np.array([0.0 if other[i] < own[i] else -BIG for i in range(NSTEP)], np.float32)[None, :], (128, 1))
        shalo = np.zeros((NS, 16, D), np.float32)
        shalo[:, 1:, :] = sp_[c * NS:(c + 1) * NS]
        m = dict(shared)
        m.update(xloc=xloc, xhalo=xhalo, pcorr=pcorr.reshape(NBLK, 64), xs=xsmp[c * NS:(c + 1) * NS].reshape(NS * 64, D),
                 shalo=shalo, ck=ck[c * NS:(c + 1) * NS].reshape(NS, cfg.past, D), cv=cv[c * NS:(c + 1) * NS].reshape(NS, cfg.past, D),
                 kpos=kpos, qref=np.ascontiguousarray(qref), omask=np.ascontiguousarray(omask))
        in_maps.append(m)
        meta.append((b, role, own, L))
    return in_maps, meta


def _assemble(results, meta, cfg, nb_prompt, nb_sample):
    NS, NSTEP = cfg.ns, cfg.nstep
    T = cfg.nblk * 512
    y_p = np.zeros((nb_prompt, T, D), np.float32)
    k_p = np.zeros((nb_prompt, T, D), np.float32)
    v_p = np.zeros((nb_prompt, T, D), np.float32)
    ps_p = np.zeros((nb_prompt, 15, D), np.float32)
    y_s = np.zeros((nb_sample, 64, D), np.float32)
    k_s = np.zeros((nb_sample, 64, D), np.float32)
    v_s = np.zeros((nb_sample, 64, D), np.float32)
    ps_s = np.zeros((nb_sample, 15, D), np.float32)
    for c, r in enumerate(results):
        b, role, own, L = meta[c]
        for i, g in enumerate(own):
            y_p[b, g * 512:(g + 1) * 512] = r["y_own"][i]
            k_p[b, g * 512:(g + 1) * 512] = r["k_own"][i]
            v_p[b, g * 512:(g + 1) * 512] = r["v_own"][i]
        for j in range(2):
            if L[cfg.nblk - 2 + j] == cfg.nblk - 1 and role == 0:
                ps_p[b] = r["ps_p"][j][1:16]
        y_s[c * NS:(c + 1) * NS] = r["y_s"].reshape(NS, 64, D)
        k_s[c * NS:(c + 1) * NS] = r["k_s"].reshape(NS, 64, D)
        v_s[c * NS:(c + 1) * NS] = r["v_s"].reshape(NS, 64, D)
        ps_s[c * NS:(c + 1) * NS] = r["ps_s"][:, 1:16, :]
    return (y_p, y_s, ps_p, ps_s, k_p.reshape(nb_prompt, T, 8, 256), v_p.reshape(nb_prompt, T, 8, 256),
            k_s.reshape(nb_sample, 64, 8, 256), v_s.reshape(nb_sample, 64, 8, 256))


def run_cfg(inputs, cfg):
    in_maps, meta = _prepare(inputs, cfg)
    nc = build(cfg)
    res = run_bass_kernel_spmd(nc, in_maps, core_ids=list(range(cfg.ncores)))
    return _assemble(res.results, meta, cfg, cfg.ncores // 2, cfg.ncores * cfg.ns)


def kernel(**inputs):
    cfg = Cfg(nblk=16, ns=4, past=1024, ncores=8)
    return run_cfg(inputs, cfg)
```
